# Optimizing a Trainium2 kernel written in Bass

```python
import math
import jax, jax.numpy as jnp
from jax import lax
import numpy as np


D_MODEL = 2048
BATCH = 8
SEQ = 4096
DEPTH = 2
DEC_BATCH = 2
DEC_SEQ = 16384
PAST_LEN = 128

N_EVEN = (DEPTH + 1) // 2
N_ODD = DEPTH // 2
HEAD_DIM = 128
EPS = 1e-6
MIX_A = D_MODEL // 2
POOL_WINDOWS = (2, 4, 8, 16)
POOL_GROUP = MIX_A // len(POOL_WINDOWS)
ATT_Q_HEADS = (D_MODEL // 2) // HEAD_DIM
ATT_KV_HEADS = ATT_Q_HEADS // 4
Q_W = ATT_Q_HEADS * HEAD_DIM
KV_W = ATT_KV_HEADS * HEAD_DIM
WINDOW = 128
BLOCK = 128
ROPE_THETA = 500000.0
ROPE_DIMS = HEAD_DIM // 4
AXIAL_THETA = 10000.0
GRID_W = 64
SSM_INNER = D_MODEL // 2
SSM_HEAD_DIM = 64
SSM_HEADS = SSM_INNER // SSM_HEAD_DIM
SSM_GROUPS = 2
SSM_STATE = 128
SSM_CONV = 5
SSM_CHUNK = 128
SSM_XBC = SSM_INNER + 2 * SSM_GROUPS * SSM_STATE
D_FF = -(-8 * D_MODEL // (3 * 256)) * 256
EVEN_IN = MIX_A + Q_W + 2 * KV_W
EVEN_OUT = MIX_A + Q_W
ODD_IN = Q_W + 2 * KV_W + SSM_INNER + SSM_XBC + 2 * SSM_HEADS
ODD_OUT = Q_W + SSM_INNER

kernel_name = 'hybrid_pool_swa_axial_ssd_encoder'


def _rms_norm(x, g):
    xf = x.astype(jnp.float32)
    y = xf * lax.rsqrt(jnp.mean(xf * xf, axis=-1, keepdims=True) + EPS)
    return (y * g.astype(jnp.float32)).astype(x.dtype)


def _rope(x, pos, theta):
    half = x.shape[-1] // 2
    freqs = theta ** (-jnp.arange(half, dtype=jnp.float32) / half)
    ang = pos.astype(jnp.float32)[:, None] * freqs[None, :]
    cos = jnp.cos(ang)[:, None, :]
    sin = jnp.sin(ang)[:, None, :]
    xf = x.astype(jnp.float32)
    x1, x2 = xf[..., :half], xf[..., half:]
    return jnp.concatenate([x1 * cos - x2 * sin, x2 * cos + x1 * sin], axis=-1).astype(x.dtype)


def _pool_mixer(u, pool_w, pool_scale):
    bsz, s, _ = u.shape
    uf = u.astype(jnp.float32)
    csum = jnp.concatenate([jnp.zeros((bsz, 1, MIX_A), jnp.float32), jnp.cumsum(uf, axis=1)], axis=1)
    t = jnp.arange(s)
    outs = []
    for gi, win in enumerate(POOL_WINDOWS):
        sl = slice(gi * POOL_GROUP, (gi + 1) * POOL_GROUP)
        lo = jnp.clip(t - win // 2, 0, s)
        hi = jnp.clip(t + win // 2, 0, s)
        cg = csum[..., sl]
        mean = (cg[:, hi] - cg[:, lo]) / (hi - lo).astype(jnp.float32)[None, :, None]
        outs.append(jnp.einsum('bsc,cd->bsd', (mean - uf[..., sl]).astype(u.dtype), pool_w[gi]))
    return jnp.concatenate(outs, axis=-1) * pool_scale


def _banded_attention(q, k, v, sink):
    bsz, s, hq, hd = q.shape
    hkv = k.shape[2]
    r = hq // hkv
    nb = s // BLOCK
    qb = q.reshape(bsz, nb, BLOCK, hkv, r, hd)
    pad = ((0, 0), (BLOCK, BLOCK), (0, 0), (0, 0))
    kp = jnp.pad(k, pad).reshape(bsz, nb + 2, BLOCK, hkv, hd)
    vp = jnp.pad(v, pad).reshape(bsz, nb + 2, BLOCK, hkv, hd)
    kb = jnp.concatenate([kp[:, :-2], kp[:, 1:-1], kp[:, 2:]], axis=2)
    vb = jnp.concatenate([vp[:, :-2], vp[:, 1:-1], vp[:, 2:]], axis=2)
    qpos = jnp.arange(nb)[:, None] * BLOCK + jnp.arange(BLOCK)[None, :]
    kpos = jnp.arange(nb)[:, None] * BLOCK - BLOCK + jnp.arange(3 * BLOCK)[None, :]
    valid = ((kpos[:, None, :] >= 0) & (kpos[:, None, :] < s)
             & (jnp.abs(qpos[:, :, None] - kpos[:, None, :]) <= WINDOW))
    scores = jnp.einsum('bnqkrd,bnskd->bnkrqs', qb, kb).astype(jnp.float32) * (hd ** -0.5)
    scores = jnp.where(valid[None, :, None, None], scores, -jnp.inf)
    sink_b = sink.astype(jnp.float32).reshape(hkv, r)[None, None, :, :, None, None]
    m = jnp.maximum(scores.max(axis=-1, keepdims=True), sink_b)
    p = jnp.exp(scores - m)
    p = (p / (p.sum(axis=-1, keepdims=True) + jnp.exp(sink_b - m))).astype(v.dtype)
    out = jnp.einsum('bnkrqs,bnskd->bnqkrd', p, vb)
    return out.reshape(bsz, s, hq * hd)


def _dense_attention(q, k, v):
    bsz, s, hq, hd = q.shape
    hkv = k.shape[2]
    r = hq // hkv
    nb = s // BLOCK
    qb = jnp.moveaxis(q.reshape(bsz, nb, BLOCK, hkv, r, hd), 1, 0)

    def one_block(qblk):
        sc = jnp.einsum('bqkrd,bskd->bkrqs', qblk, k).astype(jnp.float32) * (hd ** -0.5)
        p = jax.nn.softmax(sc, axis=-1).astype(v.dtype)
        return jnp.einsum('bkrqs,bskd->bqkrd', p, v)

    out = lax.map(one_block, qb)
    return jnp.moveaxis(out, 0, 1).reshape(bsz, s, hq * hd)


def _axial_rope(x, row, col):
    half = HEAD_DIM // 2
    return jnp.concatenate([_rope(x[..., :half], row, AXIAL_THETA),
                            _rope(x[..., half:], col, AXIAL_THETA)], axis=-1)


def _ssd_scan(x, dt, a, bm, cm):
    bsz, s, nh, hp = x.shape
    g, n = bm.shape[2], bm.shape[3]
    r = nh // g
    L = SSM_CHUNK
    nc = s // L
    xd = (x * dt[..., None]).reshape(bsz, nc, L, g, r, hp)
    acs = jnp.cumsum((dt * a).reshape(bsz, nc, L, g, r), axis=2)
    bc = bm.reshape(bsz, nc, L, g, n)
    cc = cm.reshape(bsz, nc, L, g, n)
    tri = jnp.tril(jnp.ones((L, L), dtype=bool))[None, None, :, :, None, None]
    seg = acs[:, :, :, None] - acs[:, :, None, :]
    lmat = jnp.exp(jnp.where(tri, seg, -jnp.inf))
    cb = jnp.einsum('bclgn,bcsgn->bclsg', cc, bc)
    y_diag = jnp.einsum('bclsgr,bcsgrp->bclgrp', cb[..., None] * lmat, xd)
    decay_states = jnp.exp(acs[:, :, -1:] - acs)
    states = jnp.einsum('bclgn,bclgr,bclgrp->bcgrpn', bc, decay_states, xd)
    chunk_decay = jnp.exp(acs[:, :, -1])

    def step(carry, inp):
        st, dec = inp
        return carry * dec[..., None, None] + st, carry

    init = jnp.zeros((bsz, g, r, hp, n), x.dtype)
    _, prev = lax.scan(step, init, (jnp.moveaxis(states, 1, 0), jnp.moveaxis(chunk_decay, 1, 0)))
    prev = jnp.moveaxis(prev, 0, 1)
    y_off = jnp.einsum('bclgn,bcgrpn,bclgr->bclgrp', cc, prev, jnp.exp(acs))
    return (y_diag + y_off).reshape(bsz, s, nh, hp)


def _ssd_mixer(z, xbc, dt, conv_w, conv_b, dt_bias, a_log, d_skip, gate_norm):
    bsz, s, _ = z.shape
    xbc = lax.conv_general_dilated(xbc, conv_w[:, None, :].astype(xbc.dtype), window_strides=(1,),
                                   padding=[(SSM_CONV // 2, SSM_CONV // 2)],
                                   dimension_numbers=('NWC', 'WIO', 'NWC'),
                                   feature_group_count=SSM_XBC) + conv_b
    xbc = jax.nn.silu(xbc).astype(jnp.float32)
    gn = SSM_GROUPS * SSM_STATE
    xs = xbc[..., :SSM_INNER].reshape(bsz, s, SSM_HEADS, SSM_HEAD_DIM)
    bm = xbc[..., SSM_INNER:SSM_INNER + gn].reshape(bsz, s, SSM_GROUPS, SSM_STATE)
    cm = xbc[..., SSM_INNER + gn:].reshape(bsz, s, SSM_GROUPS, SSM_STATE)
    dtf = jax.nn.softplus(dt.astype(jnp.float32).reshape(bsz, s, 2, SSM_HEADS) + dt_bias.astype(jnp.float32))
    a = -jnp.exp(a_log.astype(jnp.float32))
    flip = lambda t: jnp.flip(t, axis=1)
    y_f = _ssd_scan(xs, dtf[:, :, 0], a[0], bm, cm)
    y_b = flip(_ssd_scan(flip(xs), flip(dtf[:, :, 1]), a[1], flip(bm), flip(cm)))
    y = y_f + y_b + xs * d_skip.astype(jnp.float32)[:, None]
    y = y.reshape(bsz, s, SSM_INNER) * jax.nn.silu(z.astype(jnp.float32))
    yg = y.reshape(bsz, s, SSM_GROUPS, SSM_INNER // SSM_GROUPS)
    yg = yg * lax.rsqrt(jnp.mean(yg * yg, axis=-1, keepdims=True) + EPS)
    return (yg.reshape(bsz, s, SSM_INNER) * gate_norm.astype(jnp.float32)).astype(z.dtype)


def _even_mixer(h, w_in, w_out, pool_w, pool_scale, q_norm, k_norm, sink):
    bsz, s, _ = h.shape
    proj = h @ w_in
    u = proj[..., :MIX_A]
    q = proj[..., MIX_A:MIX_A + Q_W].reshape(bsz, s, ATT_Q_HEADS, HEAD_DIM)
    k = proj[..., MIX_A + Q_W:MIX_A + Q_W + KV_W].reshape(bsz, s, ATT_KV_HEADS, HEAD_DIM)
    v = proj[..., MIX_A + Q_W + KV_W:].reshape(bsz, s, ATT_KV_HEADS, HEAD_DIM)
    a_out = _pool_mixer(u, pool_w, pool_scale)
    pos = jnp.arange(s)
    q = _rms_norm(q, q_norm)
    k = _rms_norm(k, k_norm)
    q = jnp.concatenate([_rope(q[..., :ROPE_DIMS], pos, ROPE_THETA), q[..., ROPE_DIMS:]], axis=-1)
    k = jnp.concatenate([_rope(k[..., :ROPE_DIMS], pos, ROPE_THETA), k[..., ROPE_DIMS:]], axis=-1)
    b_out = _banded_attention(q, k, v, sink)
    return jnp.concatenate([a_out, b_out], axis=-1) @ w_out


def _odd_mixer(h, w_in, w_out, q_norm, k_norm, conv_w, conv_b, dt_bias, a_log, d_skip, gate_norm):
    bsz, s, _ = h.shape
    proj = h @ w_in
    o1 = Q_W
    o2 = o1 + KV_W
    o3 = o2 + KV_W
    o4 = o3 + SSM_INNER
    o5 = o4 + SSM_XBC
    q = proj[..., :o1].reshape(bsz, s, ATT_Q_HEADS, HEAD_DIM)
    k = proj[..., o1:o2].reshape(bsz, s, ATT_KV_HEADS, HEAD_DIM)
    v = proj[..., o2:o3].reshape(bsz, s, ATT_KV_HEADS, HEAD_DIM)
    z = proj[..., o3:o4]
    xbc = proj[..., o4:o5]
    dt = proj[..., o5:]
    rows = s // GRID_W
    row = jnp.repeat(jnp.arange(rows), GRID_W)
    col = jnp.tile(jnp.arange(GRID_W), rows)
    q = _axial_rope(_rms_norm(q, q_norm), row, col)
    k = _axial_rope(_rms_norm(k, k_norm), row, col)
    c_out = _dense_attention(q, k, v)
    d_out = _ssd_mixer(z, xbc, dt, conv_w, conv_b, dt_bias, a_log, d_skip, gate_norm)
    return jnp.concatenate([c_out, d_out], axis=-1) @ w_out


def _swiglu(h, wg, wu, wd):
    return (jax.nn.silu(h @ wg) * (h @ wu)) @ wd


def _trunk(x, norm_mix, norm_ffn, ffn_w_gate, ffn_w_up, ffn_w_down,
           ev_w_in, ev_w_out, ev_pool_w, ev_pool_scale, ev_q_norm, ev_k_norm, ev_sink,
           od_w_in, od_w_out, od_q_norm, od_k_norm, od_conv_w, od_conv_b,
           od_dt_bias, od_a_log, od_d_skip, od_gate_norm):
    for i in range(DEPTH):
        j = i // 2
        h = _rms_norm(x, norm_mix[i])
        if i % 2 == 0:
            x = x + _even_mixer(h, ev_w_in[j], ev_w_out[j], ev_pool_w[j], ev_pool_scale[j],
                                ev_q_norm[j], ev_k_norm[j], ev_sink[j])
        else:
            x = x + _odd_mixer(h, od_w_in[j], od_w_out[j], od_q_norm[j], od_k_norm[j],
                               od_conv_w[j], od_conv_b[j], od_dt_bias[j], od_a_log[j],
                               od_d_skip[j], od_gate_norm[j])
        h = _rms_norm(x, norm_ffn[i])
        x = x + _swiglu(h, ffn_w_gate[i], ffn_w_up[i], ffn_w_down[i])
    return x


def setup_inputs(seed: int = 0) -> dict:
    key = jax.random.key(seed)
    ks = jax.random.split(key, 24)
    f32 = jnp.float32

    def nrm(k, shape, scale):
        return jax.random.normal(k, shape, f32) * scale

    dt0 = jnp.exp(jax.random.uniform(ks[20], (N_ODD, 2, SSM_HEADS), f32, math.log(1e-3), math.log(1e-1)))
    return {
        'x_prompt': nrm(ks[0], (BATCH, SEQ, D_MODEL), 1.0),
        'x_sample': nrm(ks[1], (DEC_BATCH, DEC_SEQ, D_MODEL), 1.0),
        'norm_mix': 1.0 + nrm(ks[2], (DEPTH, D_MODEL), 0.02),
        'norm_ffn': 1.0 + nrm(ks[3], (DEPTH, D_MODEL), 0.02),
        'ffn_w_gate': nrm(ks[4], (DEPTH, D_MODEL, D_FF), D_MODEL ** -0.5),
        'ffn_w_up': nrm(ks[5], (DEPTH, D_MODEL, D_FF), D_MODEL ** -0.5),
        'ffn_w_down': nrm(ks[6], (DEPTH, D_FF, D_MODEL), D_FF ** -0.5),
        'ev_w_in': nrm(ks[7], (N_EVEN, D_MODEL, EVEN_IN), D_MODEL ** -0.5),
        'ev_w_out': nrm(ks[8], (N_EVEN, EVEN_OUT, D_MODEL), EVEN_OUT ** -0.5),
        'ev_pool_w': nrm(ks[9], (N_EVEN, len(POOL_WINDOWS), POOL_GROUP, POOL_GROUP), POOL_GROUP ** -0.5),
        'ev_pool_scale': 1.0 + nrm(ks[10], (N_EVEN, MIX_A), 0.02),
        'ev_q_norm': 1.0 + nrm(ks[11], (N_EVEN, HEAD_DIM), 0.02),
        'ev_k_norm': 1.0 + nrm(ks[12], (N_EVEN, HEAD_DIM), 0.02),
        'ev_sink': nrm(ks[13], (N_EVEN, ATT_Q_HEADS), 0.5),
        'od_w_in': nrm(ks[14], (N_ODD, D_MODEL, ODD_IN), D_MODEL ** -0.5),
        'od_w_out': nrm(ks[15], (N_ODD, ODD_OUT, D_MODEL), ODD_OUT ** -0.5),
        'od_q_norm': 1.0 + nrm(ks[16], (N_ODD, HEAD_DIM), 0.02),
        'od_k_norm': 1.0 + nrm(ks[17], (N_ODD, HEAD_DIM), 0.02),
        'od_conv_w': nrm(ks[18], (N_ODD, SSM_CONV, SSM_XBC), SSM_CONV ** -0.5),
        'od_conv_b': nrm(ks[19], (N_ODD, SSM_XBC), 0.02),
        'od_dt_bias': dt0 + jnp.log(-jnp.expm1(-dt0)),
        'od_a_log': jnp.log(jax.random.uniform(ks[21], (N_ODD, 2, SSM_HEADS), f32, 1.0, 16.0)),
        'od_d_skip': 1.0 + nrm(ks[22], (N_ODD, SSM_HEADS), 0.02),
        'od_gate_norm': 1.0 + nrm(ks[23], (N_ODD, SSM_INNER), 0.02),
    }


def reference(x_prompt, x_sample, norm_mix, norm_ffn, ffn_w_gate, ffn_w_up, ffn_w_down,
              ev_w_in, ev_w_out, ev_pool_w, ev_pool_scale, ev_q_norm, ev_k_norm, ev_sink,
              od_w_in, od_w_out, od_q_norm, od_k_norm, od_conv_w, od_conv_b,
              od_dt_bias, od_a_log, od_d_skip, od_gate_norm):
    y_prompt = _trunk(x_prompt, norm_mix, norm_ffn, ffn_w_gate, ffn_w_up, ffn_w_down,
                      ev_w_in, ev_w_out, ev_pool_w, ev_pool_scale, ev_q_norm, ev_k_norm, ev_sink,
                      od_w_in, od_w_out, od_q_norm, od_k_norm, od_conv_w, od_conv_b,
                      od_dt_bias, od_a_log, od_d_skip, od_gate_norm)
    y_sample = _trunk(x_sample, norm_mix, norm_ffn, ffn_w_gate, ffn_w_up, ffn_w_down,
                      ev_w_in, ev_w_out, ev_pool_w, ev_pool_scale, ev_q_norm, ev_k_norm, ev_sink,
                      od_w_in, od_w_out, od_q_norm, od_k_norm, od_conv_w, od_conv_b,
                      od_dt_bias, od_a_log, od_d_skip, od_gate_norm)
    return (y_prompt, y_sample)
```

```python
import math
from contextlib import ExitStack
import numpy as np
import concourse.bass as bass
import concourse.mybir as mybir
from concourse.bass_utils import run_bass_kernel_spmd

F32 = mybir.dt.float32
BF16 = mybir.dt.bfloat16
AF = mybir.ActivationFunctionType
ALU = mybir.AluOpType
AX = mybir.AxisListType

D = 2048
KC = 16
DFF = 5632
FC = 44
EPS = 1e-6
NR = 4
HALO = 128


class Res:
    __slots__ = ("name", "w", "r", "sem", "cnt")

    def __init__(self, name):
        self.name = name
        self.w = []
        self.r = []
        self.sem = None
        self.cnt = 0


class KB:
    def __init__(self, nc, es):
        self.nc = nc
        self.es = es
        self.eng = {"pe": nc.tensor, "act": nc.scalar, "dve": nc.vector, "pool": nc.gpsimd, "sp": nc.sync}
        self.esem = {}
        self.ecnt = {}
        for e in ("pe", "act", "dve", "pool"):
            self.esem[e] = es.enter_context(nc.semaphore("es_" + e))
            self.ecnt[e] = 0
        self.known = {e: {} for e in self.eng}
        self.semh = {}
        self.ninstr = 0
        self.store_evs = []
        self.nsem = 0

    def res(self, name):
        return Res(name)

    def sb(self, es, name, shape, dtype):
        self.nt = getattr(self, "nt", 0) + 1
        name = "%s_%d" % (name, self.nt)
        t = es.enter_context(self.nc.sbuf_tensor(name, list(shape), dtype))
        return t, Res(name)

    def newsem(self, name):
        self.nsem += 1
        return self.es.enter_context(self.nc.semaphore(name))

    def _waits(self, en, reads, writes, extra=()):
        need = {}

        def add(ev):
            s, v = ev
            k = id(s)
            self.semh[k] = s
            if need.get(k, 0) < v:
                need[k] = v
        for r in reads:
            for ev in r.w:
                add(ev)
        for w in writes:
            for ev in w.w:
                add(ev)
            for ev in w.r:
                add(ev)
        for ev in extra:
            add(ev)
        kn = self.known[en]
        own = id(self.esem[en]) if en in self.esem else None
        for k, v in need.items():
            if kn.get(k, 0) >= v:
                continue
            if k == own and (en == "pe" or v > self.ecnt[en]):
                continue
            self.eng[en].wait_ge(self.semh[k], v)
            kn[k] = v

    def _commit(self, ev, reads, writes):
        for r in reads:
            r.r.append(ev)
            if len(r.r) > 48:
                d = {}
                for s, v in r.r:
                    if d.get(id(s), (None, 0))[1] < v:
                        d[id(s)] = (s, v)
                r.r = list(d.values())
        for w in writes:
            w.w = [ev]
            w.r = []

    def op(self, en, fn, reads=(), writes=(), inc=True):
        self._waits(en, reads, writes)
        ins = fn(self.eng[en])
        self.ninstr += 1
        if inc:
            self.ecnt[en] += 1
            ins.then_inc(self.esem[en], 1)
            ev = (self.esem[en], self.ecnt[en])
        else:
            ev = (self.esem[en], self.ecnt[en] + 1)
        self._commit(ev, reads, writes)
        return ins

    def dma(self, out, in_, sbuf_res, reads=(), writes=(), q="sp", is_store=False):
        self._waits(q, reads, writes)
        ins = self.eng[q].dma_start(out=out, in_=in_)
        self.ninstr += 1
        if sbuf_res.sem is None:
            pool = self.__dict__.setdefault("sempool", [])
            if pool:
                sbuf_res.sem, sbuf_res.cnt = pool.pop()
            else:
                sbuf_res.sem = self.newsem("dsem%d" % self.nsem)
                sbuf_res.cnt = 0
            self.__dict__.setdefault("phase_res", []).append(sbuf_res)
        sbuf_res.cnt += 16
        ins.then_inc(sbuf_res.sem, 16)
        ev = (sbuf_res.sem, sbuf_res.cnt)
        self._commit(ev, reads, writes)
        if is_store:
            self.store_evs.append(ev)
        return ev

    def load(self, out, in_, res):
        return self.dma(out, in_, res, writes=[res])

    def store(self, out, in_, res):
        return self.dma(out, in_, res, reads=[res], is_store=True)

    def barrier_stores(self, engines=("sp",)):
        for e in engines:
            self._waits(e, (), (), extra=self.store_evs)
        self.store_evs = []

    def end_phase(self, keep=()):
        evs = list(self.store_evs) + [(self.esem[e], self.ecnt[e]) for e in self.esem if self.ecnt[e] > 0]
        for r in self.__dict__.get("phase_res", []):
            if r.sem is not None and r.cnt > 0 and not any(r is x for x in keep):
                evs.append((r.sem, r.cnt))
        for e in self.eng:
            self._waits(e, (), (), extra=evs)
        self.store_evs = []
        pool = self.__dict__.setdefault("sempool", [])
        rest = []
        for r in self.__dict__.get("phase_res", []):
            if any(r is x for x in keep):
                rest.append(r)
                continue
            pool.append((r.sem, r.cnt))
            r.sem = None
        self.phase_res = rest

    def wait_events(self, en, evs):
        self._waits(en, (), (), extra=evs)


class Prog:
    def __init__(self, TP, debug=(), stop_after=None):
        self.TP = TP
        self.TE = TP + 2 * HALO
        self.NT = TP // 512
        self.NCH = TP // 128
        self.debug = debug
        self.stop_after = stop_after
        self.nc = bass.Bass("TRN2", target_bir_lowering=False)
        self.dbg_outs = {}

    def din(self, name, shape, dt=F32):
        return self.nc.dram_tensor(name, list(shape), dt, kind="ExternalInput").ap()

    def dout(self, name, shape, dt=F32):
        return self.nc.dram_tensor(name, list(shape), dt, kind="ExternalOutput").ap()

    def dscr(self, name, shape, dt=F32):
        if name in self.debug:
            ap = self.nc.dram_tensor(name, list(shape), dt, kind="ExternalOutput").ap()
            self.dbg_outs[name] = ap
            return ap
        return self.nc.dram_tensor(name, list(shape), dt, kind="Internal").ap()

    def build(self):
        nc = self.nc
        TP, TE = self.TP, self.TE
        self.xin = self.din("xin", [2, D, TE])
        self.w_in0 = self.din("ev_w_in", [1, D, 2560])
        self.w_out0 = self.din("ev_w_out", [1, D, D])
        self.w_pool = self.din("ev_pool_w", [1, 4, 256, 256])
        self.w_gate = self.din("ffn_w_gate", [2, D, DFF])
        self.w_up = self.din("ffn_w_up", [2, D, DFF])
        self.w_down = self.din("ffn_w_down", [2, DFF, D])
        self.w_in1 = self.din("od_w_in", [1, D, 4128])
        self.w_out1 = self.din("od_w_out", [1, D, D])
        self.ptab_d = self.din("ptab", [128, PT_N])
        self.rope0 = self.din("rope0", [2, 2, 128, TE])
        self.rope1 = self.din("rope1", [2, 2, 128, TP])
        self.invc_d = self.din("invc", [2, self.NT, 2048])
        self.cmat_d = self.din("cmat", [128, CM_N])
        self.yout = self.dout("yout", [2, D, TP])
        self.wt_in0 = self.dscr("wt_in0", [20, 128, 16, 128], BF16)
        self.wt_out0 = self.dscr("wt_out0", [16, 128, 16, 128], BF16)
        self.wt_pool = self.dscr("wt_pool", [4, 2, 128, 2, 128], BF16)
        self.wt_gate = [self.dscr(f"wt_gate{l}", [FC, 128, 16, 128], BF16) for l in range(2)]
        self.wt_up = [self.dscr(f"wt_up{l}", [FC, 128, 16, 128], BF16) for l in range(2)]
        self.wt_down = [self.dscr(f"wt_down{l}", [16, 128, FC, 128], BF16) for l in range(2)]
        self.wt_in1 = self.dscr("wt_in1", [32, 128, 16, 128], BF16)
        self.wt_dt = self.dscr("wt_dt", [128, 16, 32], BF16)
        self.wt_out1 = self.dscr("wt_out1", [16, 128, 16, 128], BF16)
        self.UT = self.dscr("UT", [2, 1024, TE], F32)
        self.Q0 = self.dscr("Q0", [2, 1024, TP], BF16)
        self.K0 = self.dscr("K0", [2, 256, TE], BF16)
        self.V0 = self.dscr("V0", [2, TE, 256], BF16)
        self.X1a = self.dscr("X1a", [2, D, TP], F32)
        self.X1b = self.dscr("X1b", [2, D, TP], F32)
        self.X2a = self.dscr("X2a", [2, D, TP], F32)
        self.gn_d = self.din("gnrow", [128, 1024])
        self.Q1 = self.dscr("Q1", [2, 1024, TP], BF16)
        self.K1p = self.dscr("K1p", [256, TP], BF16)
        self.K1s = [self.dscr(f"K1s{g}", [128, TP], BF16) for g in range(2)]
        self.K1g = [self.dscr(f"K1g{g}", [NR * 128, TP], BF16) for g in range(2)]
        self.V1p = self.dscr("V1p", [TP, 256], BF16)
        self.V1s = [self.dscr(f"V1s{g}", [TP, 128], BF16) for g in range(2)]
        self.V1g = [self.dscr(f"V1g{g}", [NR * TP, 128], BF16) for g in range(2)]
        self.XBC = self.dscr("XBC", [2, 1536, TP + 4], F32)
        self.XBin = self.dscr("XBin", [1536, 4], F32)
        self.XBg = self.dscr("XBg", [NR * 1536, 4], F32)
        self.Z = self.dscr("Z", [2, TP, 1024], F32)
        self.DT = self.dscr("DT", [2, TP, 32], F32)
        self.Y = self.dscr("Y", [2, TP, 1024], F32)
        self.CUM = self.dscr("CUM", [2, 2, TP, 16], F32)
        self.CT = self.dscr("CT", [256, TP], BF16)
        self.SSin = self.dscr("SSin", [512, 512], F32)
        self.SSg = self.dscr("SSg", [NR * 512, 512], F32)
        self.SAin = self.dscr("SAin", [8, 32], F32)
        self.SAg = self.dscr("SAg", [NR * 8, 32], F32)
        self.CO = self.dscr("CO", [2, 1024, TP], BF16)
        self.DBG_INI = self.dscr("DBG_INI", [128, 4 * 512], F32) if "DBG_INI" in self.debug else None
        self.DBG_ST = self.dscr("DBG_ST", [2, 128, 1024], F32) if "DBG_ST" in self.debug else None

        with ExitStack() as es:
            k = KB(nc, es)
            self.k = k
            self.ps = [es.enter_context(nc.psum_tensor(f"psb{i}", [128, 512], F32)) for i in range(8)]
            self.ps_r = [k.res(f"psb{i}") for i in range(8)]
            self.ptab, self.ptab_r = k.sb(es, "ptab_sb", [128, PT_N], F32)
            self.cmat, self.cmat_r = k.sb(es, "cmat_sb", [128, CM_N], F32)
            self.cmb, self.cmb_r = k.sb(es, "cmat_bf", [128, CMB_N], BF16)
            k.load(self.ptab[:], self.ptab_d[:, :], self.ptab_r)
            k.load(self.cmat[:], self.cmat_d[:, :], self.cmat_r)
            k.op("dve", lambda e: e.tensor_copy(out=self.cmb[:], in_=self.cmat[:, 0:CMB_N]),
                 reads=[self.cmat_r], writes=[self.cmb_r])
            k.op("act", lambda e: e.activation(out=self.ptab[:, PT_SINK:PT_SINK + 8], in_=self.ptab[:, PT_SINK:PT_SINK + 8], func=AF.Exp),
                 reads=[self.ptab_r], writes=[self.ptab_r])

            self.keep = [self.ptab_r, self.cmat_r]
            self.phase_weights()
            k.end_phase(self.keep)
            if self.stop_after == "W":
                return self.finish()
            self.phase_l0a()
            k.end_phase(self.keep)
            if self.stop_after == "L0a":
                return self.finish()
            self.phase_l0b()
            k.end_phase(self.keep)
            if self.stop_after == "L0b":
                return self.finish()
            self.phase_ffn(0, self.X1a, self.X1b if not self.stop_after == "F0" else self.yout)
            k.end_phase(self.keep)
            if self.stop_after == "F0":
                return self.finish()
            self.gnrow, self.gn_r = k.sb(es, "gnrow_sb", [128, 1024], F32)
            k.load(self.gnrow[:], self.gn_d[:, :], self.gn_r)
            self.keep.append(self.gn_r)
            k.op("act", lambda e: e.activation(out=self.ptab[:, PT_ALOG:PT_ALOG + 32], in_=self.ptab[:, PT_ALOG:PT_ALOG + 32], func=AF.Exp),
                 reads=[self.ptab_r], writes=[self.ptab_r])
            k.op("dve", lambda e: e.tensor_scalar(out=self.ptab[:, PT_ALOG:PT_ALOG + 32], in0=self.ptab[:, PT_ALOG:PT_ALOG + 32], scalar1=-1.0, scalar2=None, op0=ALU.mult),
                 reads=[self.ptab_r], writes=[self.ptab_r])
            self.phase_l1a()
            k.end_phase(self.keep)
            self.phase_exchange1()
            k.end_phase(self.keep)
            if self.stop_after == "L1a":
                return self.finish()
            self.phase_att()
            k.end_phase(self.keep)
            if self.stop_after == "ATT":
                return self.finish()
            for s_ in range(2):
                for d_ in range(2):
                    self.ssd_sweep(s_, d_)
                    k.end_phase(self.keep)
            self.phase_exchange2()
            k.end_phase(self.keep)
            if self.stop_after == "SSD":
                return self.finish()
            self.phase_l1c()
            k.end_phase(self.keep)
            if self.stop_after == "L1c":
                return self.finish()
            self.phase_ffn(1, self.X2a, self.yout)
            return self.finish()

    def finish(self):
        k = self.k
        k.barrier_stores(engines=("sp",))
        return self.nc

    def phase_weights(self):
        k = self.k
        nc = self.nc
        self.wev = {}

        def conv(dst, src2d, nkc, ncols_mc, key="x"):
            wres = self.wev.setdefault(key, k.res("wconv_" + key))
            step = 11 if ncols_mc > 22 else ncols_mc
            for kc in range(nkc):
                for m0 in range(0, ncols_mc, step):
                    m1 = min(ncols_mc, m0 + step)
                    k.dma(dst[m0:m1, :, kc, :].rearrange("mc p m -> p mc m"),
                          src2d[kc * 128:(kc + 1) * 128, m0 * 128:m1 * 128].rearrange("p (mc m) -> p mc m", m=128),
                          wres, q="pool")
        conv(self.wt_in0, self.w_in0[0], 16, 20, "in0")
        for g in range(4):
            conv(self.wt_pool[g], self.w_pool[0, g], 2, 2, "out0")
        conv(self.wt_out0, self.w_out0[0], 16, 16, "out0")
        conv(self.wt_gate[0], self.w_gate[0], 16, FC, "ffn0")
        conv(self.wt_up[0], self.w_up[0], 16, FC, "ffn0")
        conv(self.wt_down[0], self.w_down[0], FC, 16, "ffn0")
        conv(self.wt_in1, self.w_in1[0][:, 0:4096], 16, 32, "in1")
        wres = self.wev["in1"]
        for kc in range(16):
            k.dma(self.wt_dt[:, kc, :], self.w_in1[0][kc * 128:(kc + 1) * 128, 4096:4128], wres, q="pool")
        conv(self.wt_out1, self.w_out1[0], 16, 16, "out1")
        conv(self.wt_gate[1], self.w_gate[1], 16, FC, "ffn1")
        conv(self.wt_up[1], self.w_up[1], 16, FC, "ffn1")
        conv(self.wt_down[1], self.w_down[1], FC, 16, "ffn1")
        self.keep += list(self.wev.values())

    def need_w(self, key):
        r = self.wev[key]
        self.k.wait_events("sp", [(r.sem, r.cnt)])

    def rmsnorm_fm(self, xT, xT_r, hT, hT_r, n, gcol, sq, sq_r, rstd, rstd_r, bank, part=None):
        k = self.k
        if part == "b":
            for kc in range(KC):
                k.op("dve", lambda e: e.scalar_tensor_tensor(out=hT[:, kc, 0:n], in0=xT[:, kc, 0:n], scalar=self.ptab[:, gcol + kc:gcol + kc + 1],
                                                             in1=rstd[:, 0:n], op0=ALU.mult, op1=ALU.mult),
                     reads=[xT_r, rstd_r, self.ptab_r], writes=[hT_r])
            return
        ps, ps_r = self.ps[bank], self.ps_r[bank]
        ones = self.cmb[:, CM_ONES:CM_ONES + 128]
        for kc in range(KC):
            j = kc % 2
            k.op("act", lambda e: e.activation(out=sq[j][:, 0:n], in_=xT[:, kc, 0:n], func=AF.Square),
                 reads=[xT_r], writes=[sq_r[j]])
            k.op("pe", lambda e: e.matmul(ps[:, 0:n], lhsT=ones, rhs=sq[j][:, 0:n], start=(kc == 0), stop=(kc == KC - 1)),
                 reads=[sq_r[j], self.cmb_r], writes=[ps_r])
        k.op("act", lambda e: e.activation(out=rstd[:, 0:n], in_=ps[:, 0:n], func=AF.Ln, bias=EPS, scale=1.0 / D),
             reads=[ps_r], writes=[rstd_r])
        k.op("act", lambda e: e.activation(out=rstd[:, 0:n], in_=rstd[:, 0:n], func=AF.Exp, scale=-0.5),
             reads=[rstd_r], writes=[rstd_r])
        if part == "a":
            return
        for kc in range(KC):
            k.op("dve", lambda e: e.scalar_tensor_tensor(out=hT[:, kc, 0:n], in0=xT[:, kc, 0:n], scalar=self.ptab[:, gcol + kc:gcol + kc + 1],
                                                         in1=rstd[:, 0:n], op0=ALU.mult, op1=ALU.mult),
                 reads=[xT_r, rstd_r, self.ptab_r], writes=[hT_r])

    def mm_group(self, bank_ap, bank_r, w_tile, w_r, nkc, rhs_fn, rhs_res, n):
        k = self.k
        for kc in range(nkc):
            k.op("pe", lambda e: e.matmul(bank_ap, lhsT=w_tile[:, kc, :], rhs=rhs_fn(kc), start=(kc == 0), stop=(kc == nkc - 1)),
                 reads=[w_r] + list(rhs_res), writes=[bank_r], inc=(kc == nkc - 1))

    class WStream:
        def __init__(self, prog, es, name, nkc, nbuf):
            self.prog = prog
            self.bufs = [prog.k.sb(es, f"{name}{i}", [128, nkc, 128], BF16) for i in range(nbuf)]
            self.i = 0
            self.queue = []

        def prefetch(self, src_ap):
            t, r = self.bufs[self.i % len(self.bufs)]
            self.i += 1
            self.prog.k.load(t[:].rearrange("p a b -> p (a b)"), src_ap.rearrange("p a b -> p (a b)"), r)
            self.queue.append((t, r))

        def pop(self):
            return self.queue.pop(0)

    def phase_ffn(self, layer, Xin, Xout):
        k = self.k
        self.need_w("ffn%d" % layer)
        NT = self.NT
        gcol = PT_GFFN + 16 * layer
        with ExitStack() as es:
            xT = [k.sb(es, f"f_xT{i}", [128, KC, 512], F32) for i in range(2)]
            hT, hT_r = k.sb(es, "f_hT", [128, KC, 512], BF16)
            aT, aT_r = k.sb(es, "f_aT", [128, FC, 512], BF16)
            sq = [k.sb(es, f"f_sq{i}", [128, 512], BF16) for i in range(2)]
            rstd, rstd_r = k.sb(es, "f_rstd", [128, 512], F32)
            sg = [k.sb(es, f"f_sg{i}", [128, 512], F32) for i in range(2)]
            wg = Prog.WStream(self, es, "f_wg", KC, 2)
            wu = Prog.WStream(self, es, "f_wu", KC, 2)
            wd = Prog.WStream(self, es, "f_wd", FC, 2)
            tiles = [(s, t) for s in range(2) for t in range(NT)]

            def load_x(i):
                s, t = tiles[i]
                xt, xr = xT[i % 2]
                k.load(xt[:], Xin[s, :, t * 512:(t + 1) * 512].rearrange("(kc p) t -> p kc t", p=128), xr)
            load_x(0)
            sched = []
            for i in range(len(tiles)):
                for mc in range(FC):
                    sched.append(("gu", mc))
                for mc in range(16):
                    sched.append(("d", mc))
            pos = [0]

            def prefetch_next():
                if pos[0] >= len(sched):
                    return
                kind, mc = sched[pos[0]]
                pos[0] += 1
                if kind == "gu":
                    wg.prefetch(self.wt_gate[layer][mc])
                    wu.prefetch(self.wt_up[layer][mc])
                else:
                    wd.prefetch(self.wt_down[layer][mc])
            prefetch_next()
            for i, (s, t) in enumerate(tiles):
                xt, xr = xT[i % 2]
                if i + 1 < len(tiles):
                    load_x(i + 1)
                sqa, sqr = [a for a, _ in sq], [b for _, b in sq]
                if i == 0:
                    self.rmsnorm_fm(xt, xr, hT, hT_r, 512, gcol, sqa, sqr, rstd, rstd_r, 0)
                for mc in range(FC):
                    prefetch_next()
                    wgt, wgr = wg.pop()
                    wut, wur = wu.pop()
                    bg, bu = 1 + 2 * (mc % 2), 2 + 2 * (mc % 2)
                    self.mm_group(self.ps[bg][:], self.ps_r[bg], wgt, wgr, KC, lambda kc: hT[:, kc, :], [hT_r], 512)
                    self.mm_group(self.ps[bu][:], self.ps_r[bu], wut, wur, KC, lambda kc: hT[:, kc, :], [hT_r], 512)
                    sgt, sgr = sg[mc % 2]
                    k.op("act", lambda e: e.activation(out=sgt[:], in_=self.ps[bg][:], func=AF.Silu), reads=[self.ps_r[bg]], writes=[sgr])
                    k.op("dve", lambda e: e.tensor_tensor(out=aT[:, mc, :], in0=sgt[:], in1=self.ps[bu][:], op=ALU.mult),
                         reads=[sgr, self.ps_r[bu]], writes=[aT_r])
                for mc in range(16):
                    prefetch_next()
                    wdt, wdr = wd.pop()
                    by = 5 + (mc % 2)
                    self.mm_group(self.ps[by][:], self.ps_r[by], wdt, wdr, FC, lambda kc: aT[:, kc, :], [aT_r], 512)
                    k.op("dve", lambda e: e.tensor_tensor(out=xt[:, mc, :], in0=xt[:, mc, :], in1=self.ps[by][:], op=ALU.add),
                         reads=[xr, self.ps_r[by]], writes=[xr])
                    if i + 1 < len(tiles) and mc in (1, 4):
                        xn, xnr = xT[(i + 1) % 2]
                        self.rmsnorm_fm(xn, xnr, hT, hT_r, 512, gcol, sqa, sqr, rstd, rstd_r, 0, part=("a" if mc == 1 else "b"))
                k.store(Xout[s, :, t * 512:(t + 1) * 512].rearrange("(kc p) t -> p kc t", p=128), xt[:], xr)
            k.barrier_stores()

    def head_norm_rope(self, bsrc, n, gcol, rmat_col, cos_ap, sin_ap, tab_r, out_ap, out_r, tmp, bank_ss, bank_rq, store_fn=None):
        k = self.k
        (qraw, qraw_r), (qsq, qsq_r), (rs, rs_r), (qn, qn_r), (t1, t1_r) = tmp
        ps, ps_r = self.ps, self.ps_r
        ones = self.cmb[:, CM_ONES:CM_ONES + 128]

        def S0():
            k.op("act", lambda e: e.copy(out=qraw[:, 0:n], in_=ps[bsrc][:, 0:n]), reads=[ps_r[bsrc]], writes=[qraw_r])
            k.op("act", lambda e: e.activation(out=qsq[:, 0:n], in_=ps[bsrc][:, 0:n], func=AF.Square), reads=[ps_r[bsrc]], writes=[qsq_r])

        def S1():
            k.op("pe", lambda e: e.matmul(ps[bank_ss][:, 0:n], lhsT=ones, rhs=qsq[:, 0:n], start=True, stop=True),
                 reads=[qsq_r, self.cmb_r], writes=[ps_r[bank_ss]])
            k.op("act", lambda e: e.activation(out=rs[:, 0:n], in_=ps[bank_ss][:, 0:n], func=AF.Ln, bias=EPS, scale=1.0 / 128),
                 reads=[ps_r[bank_ss]], writes=[rs_r])
            k.op("act", lambda e: e.activation(out=rs[:, 0:n], in_=rs[:, 0:n], func=AF.Exp, scale=-0.5), reads=[rs_r], writes=[rs_r])
            k.op("dve", lambda e: e.scalar_tensor_tensor(out=qn[:, 0:n], in0=qraw[:, 0:n], scalar=self.ptab[:, gcol:gcol + 1], in1=rs[:, 0:n],
                                                         op0=ALU.mult, op1=ALU.mult), reads=[qraw_r, rs_r, self.ptab_r], writes=[qn_r])

        def S2():
            k.op("pe", lambda e: e.matmul(ps[bank_rq][:, 0:n], lhsT=self.cmat[:, rmat_col:rmat_col + 128], rhs=qn[:, 0:n], start=True, stop=True),
                 reads=[qn_r, self.cmat_r], writes=[ps_r[bank_rq]])
            k.op("dve", lambda e: e.tensor_tensor(out=t1[:, 0:n], in0=qn[:, 0:n], in1=cos_ap, op=ALU.mult), reads=[qn_r, tab_r], writes=[t1_r])
            k.op("dve", lambda e: e.tensor_tensor(out=qn[:, 0:n], in0=ps[bank_rq][:, 0:n], in1=sin_ap, op=ALU.mult),
                 reads=[ps_r[bank_rq], tab_r, qn_r], writes=[qn_r])
            k.op("dve", lambda e: e.tensor_tensor(out=out_ap, in0=t1[:, 0:n], in1=qn[:, 0:n], op=ALU.add), reads=[t1_r, qn_r], writes=[out_r])
            if store_fn is not None:
                store_fn()
        return [S0, S1, S2]

    @staticmethod
    def run_pending(pending, new=None, flush=False):
        if new is not None:
            new[0]()
            entry = [1, new]
        for p in list(pending):
            p[1][p[0]]()
            p[0] += 1
            if p[0] >= len(p[1]):
                pending.remove(p)
        if new is not None:
            pending.append(entry)
        if flush:
            while pending:
                Prog.run_pending(pending)

    def phase_l0a(self):
        k = self.k
        self.need_w("in0")
        TP, TE, NT = self.TP, self.TE, self.NT
        with ExitStack() as es:
            xT = [k.sb(es, f"a_xT{i}", [128, KC, 512], F32) for i in range(2)]
            tab = [k.sb(es, f"a_tab{i}", [128, 2, 512], F32) for i in range(2)]
            hT, hT_r = k.sb(es, "a_hT", [128, KC, 512], BF16)
            sq = [k.sb(es, f"a_sq{i}", [128, 512], BF16) for i in range(2)]
            rstd, rstd_r = k.sb(es, "a_rstd", [128, 512], F32)
            ust = [k.sb(es, f"a_ust{i}", [128, 512], F32) for i in range(2)]
            qkst = [k.sb(es, f"a_qkst{i}", [128, 512], BF16) for i in range(3)]
            vst = [k.sb(es, f"a_vst{i}", [128, 256], BF16) for i in range(2)]
            tmps = [[k.sb(es, f"a_qraw{i}", [128, 512], F32), k.sb(es, f"a_qsq{i}", [128, 512], BF16), k.sb(es, f"a_rs{i}", [128, 512], F32),
                     k.sb(es, f"a_qn{i}", [128, 512], F32), k.sb(es, f"a_t1{i}", [128, 512], F32)] for i in range(3)]
            pending = []
            wv, wv_r = k.sb(es, "a_wv", [128, 2, KC, 128], BF16)
            ws = Prog.WStream(self, es, "a_ws", KC, 3)
            k.load(wv[:].rearrange("p a b c -> p a (b c)"), self.wt_in0[18:20].rearrange("mc p kc m -> p mc (kc m)"), wv_r)
            tiles = []
            for s in range(2):
                tiles.append((s, 0, 128))
                for t in range(NT):
                    tiles.append((s, HALO + t * 512, 512))
                tiles.append((s, HALO + TP, 128))

            def load_x(i):
                s, t0, n = tiles[i]
                xt, xr = xT[i % 2]
                k.load(xt[:, :, 0:n], self.xin[s, :, t0:t0 + n].rearrange("(kc p) t -> p kc t", p=128), xr)
                tb, tr = tab[i % 2]
                k.load(tb[:, :, 0:n], self.rope0[s, :, :, t0:t0 + n].rearrange("c p t -> p c t"), tr)
            load_x(0)
            nchunk = 18 * len(tiles)
            pos = [0]

            def prefetch_next():
                if pos[0] < nchunk:
                    ws.prefetch(self.wt_in0[pos[0] % 18])
                    pos[0] += 1
            prefetch_next()
            prefetch_next()
            cnt_u = 0
            cnt_qk = 0
            cnt_v = 0
            for i, (s, t0, n) in enumerate(tiles):
                xt, xr = xT[i % 2]
                tb, tr = tab[i % 2]
                if i + 1 < len(tiles):
                    load_x(i + 1)
                self.rmsnorm_fm(xt, xr, hT, hT_r, n, PT_GMIX, [a for a, _ in sq], [b for _, b in sq], rstd, rstd_r, 0)
                interior = (t0 >= HALO and t0 < HALO + TP)
                for mc in range(18):
                    prefetch_next()
                    wt, wr = ws.pop()
                    b = 1 + (mc % 2)
                    self.mm_group(self.ps[b][:, 0:n], self.ps_r[b], wt, wr, KC, lambda kc: hT[:, kc, 0:n], [hT_r], n)
                    if mc < 8:
                        st, sr = ust[cnt_u % 2]
                        cnt_u += 1
                        k.op("act", lambda e: e.copy(out=st[:, 0:n], in_=self.ps[b][:, 0:n]), reads=[self.ps_r[b]], writes=[sr])
                        k.store(self.UT[s, mc * 128:(mc + 1) * 128, t0:t0 + n], st[:, 0:n], sr)
                        Prog.run_pending(pending)
                    else:
                        isq = mc < 16
                        if isq and not interior:
                            Prog.run_pending(pending)
                            continue
                        st, sr = qkst[cnt_qk % 3]
                        if isq:
                            dstap = self.Q0[s, (mc - 8) * 128:(mc - 7) * 128, t0 - HALO:t0 - HALO + n]
                        else:
                            dstap = self.K0[s, (mc - 16) * 128:(mc - 15) * 128, t0:t0 + n]
                        stages = self.head_norm_rope(b, n, PT_QN0 if isq else PT_KN0, CM_R0, tb[:, 0, 0:n], tb[:, 1, 0:n], tr,
                                                     st[:, 0:n], sr, tmps[cnt_qk % 3], 3, 4,
                                                     store_fn=(lambda dstap=dstap, st=st, sr=sr, n=n: k.store(dstap, st[:, 0:n], sr)))
                        cnt_qk += 1
                        Prog.run_pending(pending, new=stages)
                Prog.run_pending(pending, flush=True)
                for j in range(n // 128):
                    b = 5 + (cnt_v % 2)
                    for g in range(2):
                        for kc in range(KC):
                            k.op("pe", lambda e: e.matmul(self.ps[b][:, g * 128:(g + 1) * 128], lhsT=hT[:, kc, j * 128:(j + 1) * 128], rhs=wv[:, g, kc, :],
                                                          start=(kc == 0), stop=(kc == KC - 1)),
                                 reads=[hT_r, wv_r], writes=[self.ps_r[b]], inc=(kc == KC - 1))
                    st, sr = vst[cnt_v % 2]
                    cnt_v += 1
                    k.op("act", lambda e: e.copy(out=st[:], in_=self.ps[b][:, 0:256]), reads=[self.ps_r[b]], writes=[sr])
                    k.store(self.V0[s, t0 + j * 128:t0 + (j + 1) * 128, :], st[:], sr)
            k.barrier_stores()

    def phase_l0b(self):
        k = self.k
        self.need_w("out0")
        TP, TE, NT = self.TP, self.TE, self.NT
        ps, ps_r = self.ps, self.ps_r
        scale = 128.0 ** -0.5
        with ExitStack() as es:
            xT = [k.sb(es, f"b_xT{i}", [128, KC, 512], F32) for i in range(1)]
            uT = [k.sb(es, f"b_uT{i}", [128, 8, 528], F32) for i in range(2)]
            ivc = [k.sb(es, f"b_ivc{i}", [128, 4, 512], F32) for i in range(1)]
            sa, sa_r = k.sb(es, "b_sa", [128, 8, 528], F32)
            sb_, sb_r = k.sb(es, "b_sb", [128, 8, 528], F32)
            dif, dif_r = k.sb(es, "b_dif", [128, 8, 512], BF16)
            catT, cat_r = k.sb(es, "b_cat", [128, 16, 512], BF16)
            wp, wp_r = k.sb(es, "b_wp", [128, 4, 2, 2, 128], BF16)
            kT = [k.sb(es, f"b_kT{i}", [128, 768], BF16) for i in range(2)]
            vv = [k.sb(es, f"b_vv{i}", [128, 6, 128], BF16) for i in range(2)]
            qT = [k.sb(es, f"b_qT{i}", [128, 512], BF16) for i in range(2)]
            ee = [k.sb(es, f"b_ee{i}", [128, 384], F32) for i in range(2)]
            pTs = [k.sb(es, f"b_pT{i}", [128, 6, 384], BF16) for i in range(2)]
            mk, mk_r = k.sb(es, "b_mk", [128, 2, 2, 384], BF16)
            den, den_r = k.sb(es, "b_den", [128, 512], F32)
            ws = Prog.WStream(self, es, "b_ws", KC, 3)
            k.load(wp[:].rearrange("p g a b c -> p (g a) (b c)"), self.wt_pool.rearrange("g mo p ki m -> p (g mo) (ki m)"), wp_r)
            band = self.cmb[:, CM_BAND:CM_BAND + 384]
            for s in range(2):
                for lr in range(2):
                    col = PT_VALID + 2 * s + lr
                    k.op("dve", lambda e: e.tensor_scalar(out=mk[:, s, lr, :], in0=band, scalar1=self.ptab[:, col:col + 1], scalar2=None, op0=ALU.mult),
                         reads=[self.cmb_r, self.ptab_r], writes=[mk_r])
            tiles = [(s, t) for s in range(2) for t in range(NT)]

            def load_x(i):
                s, t = tiles[i]
                t0 = HALO + t * 512
                ut, ur = uT[i % 2]
                k.load(ut[:], self.UT[s, :, t0 - 8:t0 + 520].rearrange("(c p) t -> p c t", p=128), ur)

            def load_x2(i):
                s, t = tiles[i]
                t0 = HALO + t * 512
                xt, xr = xT[0]
                k.load(xt[:], self.xin[s, :, t0:t0 + 512].rearrange("(kc p) t -> p kc t", p=128), xr)
                iv, ivr = ivc[0]
                k.load(iv[:].rearrange("p g t -> p (g t)"), self.invc_d[s, t:t + 1, :].partition_broadcast(128), ivr)
            load_x(0)
            nchunk = 16 * len(tiles)
            pos = [0]

            def prefetch_next():
                if pos[0] < nchunk:
                    ws.prefetch(self.wt_out0[pos[0] % 16])
                    pos[0] += 1
            prefetch_next()
            prefetch_next()
            cq = 0
            ckv = 0
            ce = 0
            for i, (s, t) in enumerate(tiles):
                t0 = HALO + t * 512
                xt, xr = xT[0]
                ut, ur = uT[i % 2]
                iv, ivr = ivc[0]
                load_x2(i)
                if i + 1 < len(tiles):
                    load_x(i + 1)
                TT = ALU.add
                k.op("dve", lambda e: e.tensor_tensor(out=sa[:, :, 1:528], in0=ut[:, :, 0:527], in1=ut[:, :, 1:528], op=TT), reads=[ur], writes=[sa_r])
                k.op("dve", lambda e: e.tensor_tensor(out=sb_[:, 2:8, 2:527], in0=sa[:, 2:8, 1:526], in1=sa[:, 2:8, 3:528], op=TT), reads=[sa_r], writes=[sb_r])
                k.op("dve", lambda e: e.tensor_tensor(out=sa[:, 4:8, 4:525], in0=sb_[:, 4:8, 2:523], in1=sb_[:, 4:8, 6:527], op=TT), reads=[sb_r, sa_r], writes=[sa_r])
                k.op("dve", lambda e: e.tensor_tensor(out=sb_[:, 6:8, 8:520], in0=sa[:, 6:8, 4:516], in1=sa[:, 6:8, 12:524], op=TT), reads=[sa_r, sb_r], writes=[sb_r])
                srcs = [sa, sb_, sa, sb_]
                for g in range(4):
                    src = srcs[g]
                    ivb = iv[:, g:g + 1, :].to_broadcast([128, 2, 512])
                    k.op("dve", lambda e: e.tensor_tensor(out=src[:, 2 * g:2 * g + 2, 8:520], in0=src[:, 2 * g:2 * g + 2, 8:520], in1=ivb, op=ALU.mult),
                         reads=[sa_r, sb_r, ivr], writes=[sa_r, sb_r])
                    k.op("dve", lambda e: e.tensor_tensor(out=dif[:, 2 * g:2 * g + 2, :], in0=src[:, 2 * g:2 * g + 2, 8:520], in1=ut[:, 2 * g:2 * g + 2, 8:520], op=ALU.subtract),
                         reads=[sa_r, sb_r, ur], writes=[dif_r])
                for g in range(4):
                    for mo in range(2):
                        b = 1 + (mo % 2)
                        for ki in range(2):
                            k.op("pe", lambda e: e.matmul(ps[b][:], lhsT=wp[:, g, mo, ki, :], rhs=dif[:, 2 * g + ki, :], start=(ki == 0), stop=(ki == 1)),
                                 reads=[wp_r, dif_r], writes=[ps_r[b]], inc=(ki == 1))
                        c = 2 * g + mo
                        k.op("act", lambda e: e.activation(out=catT[:, c, :], in_=ps[b][:], func=AF.Copy, scale=self.ptab[:, PT_PSCALE + c:PT_PSCALE + c + 1]),
                             reads=[ps_r[b], self.ptab_r], writes=[cat_r])
                heads = [(g, j) for g in range(2) for j in range(4)]
                ctxs = {}

                def stageA(hi):
                    nonlocal cq, ckv, ce
                    g, j = heads[hi]
                    h = 4 * g + j
                    if j == 0:
                        kt, ktr = kT[ckv % 2]
                        vt, vtr = vv[ckv % 2]
                        ckv += 1
                        k.load(kt[:], self.K0[s, g * 128:(g + 1) * 128, t0 - 128:t0 + 640], ktr)
                        k.load(vt[:], self.V0[s, t0 - 128:t0 + 640, g * 128:(g + 1) * 128].rearrange("(b p) d -> p b d", p=128), vtr)
                        ctxs[g] = (kt, ktr, vt, vtr)
                    kt, ktr, vt, vtr = ctxs[g]
                    qt, qtr = qT[cq % 2]
                    pT, pT_r = pTs[cq % 2]
                    cq += 1
                    k.load(qt[:], self.Q0[s, h * 128:(h + 1) * 128, t * 512:(t + 1) * 512], qtr)
                    for kb in range(6):
                        qlo, qhi = max(kb - 2, 0), min(kb, 3)
                        nq = (qhi - qlo + 1) * 128
                        m0 = (qlo - (kb - 2)) * 128
                        b = 3 + (ce % 2)
                        et, etr = ee[ce % 2]
                        ce += 1
                        k.op("pe", lambda e: e.matmul(ps[b][:, 0:nq], lhsT=kt[:, kb * 128:(kb + 1) * 128], rhs=qt[:, qlo * 128:(qhi + 1) * 128], start=True, stop=True),
                             reads=[ktr, qtr], writes=[ps_r[b]])
                        k.op("act", lambda e: e.activation(out=et[:, 0:nq], in_=ps[b][:, 0:nq], func=AF.Exp, scale=scale), reads=[ps_r[b]], writes=[etr])
                        if kb == 0 and t == 0:
                            msk, mr = mk[:, s, 0, m0:m0 + nq], mk_r
                        elif kb == 5 and t == NT - 1:
                            msk, mr = mk[:, s, 1, m0:m0 + nq], mk_r
                        else:
                            msk, mr = band[:, m0:m0 + nq], self.cmb_r
                        k.op("dve", lambda e: e.tensor_tensor(out=pT[:, kb, 0:nq], in0=et[:, 0:nq], in1=msk, op=ALU.mult), reads=[etr, mr], writes=[pT_r])
                    return (h, vt, vtr, pT, pT_r)

                def stageB(hi, ctx):
                    h, vt, vtr, pT, pT_r = ctx
                    bo, bs = (5, 6) if hi % 2 == 0 else (7, 0)
                    for ql in range(4):
                        for ii, kb in enumerate((ql, ql + 1, ql + 2)):
                            qlo = max(kb - 2, 0)
                            c0 = (ql - qlo) * 128
                            k.op("pe", lambda e: e.matmul(ps[bo][:, ql * 128:(ql + 1) * 128], lhsT=vt[:, kb, :], rhs=pT[:, kb, c0:c0 + 128], start=(ii == 0), stop=(ii == 2)),
                                 reads=[vtr, pT_r], writes=[ps_r[bo]], inc=False)
                        for ii, kb in enumerate((ql, ql + 1, ql + 2)):
                            qlo = max(kb - 2, 0)
                            c0 = (ql - qlo) * 128
                            k.op("pe", lambda e: e.matmul(ps[bs][:, ql * 128:(ql + 1) * 128], lhsT=self.cmb[:, CM_ONES:CM_ONES + 128], rhs=pT[:, kb, c0:c0 + 128], start=(ii == 0), stop=(ii == 2)),
                                 reads=[self.cmb_r, pT_r], writes=[ps_r[bs]], inc=(ii == 2 and ql == 3))
                    k.op("dve", lambda e: e.tensor_scalar(out=den[:], in0=ps[bs][:], scalar1=self.ptab[:, PT_SINK + h:PT_SINK + h + 1], scalar2=None, op0=ALU.add),
                         reads=[ps_r[bs], self.ptab_r], writes=[den_r])
                    k.op("dve", lambda e: e.reciprocal(out=den[:], in_=den[:]), reads=[den_r], writes=[den_r])
                    k.op("dve", lambda e: e.tensor_tensor(out=catT[:, 8 + h, :], in0=ps[bo][:], in1=den[:], op=ALU.mult), reads=[ps_r[bo], den_r], writes=[cat_r])

                cur = stageA(0)
                for hi in range(8):
                    nxt = stageA(hi + 1) if hi + 1 < 8 else None
                    stageB(hi, cur)
                    cur = nxt
                for mc in range(16):
                    prefetch_next()
                    wt, wr = ws.pop()
                    b = 1 + (mc % 2)
                    self.mm_group(ps[b][:], ps_r[b], wt, wr, KC, lambda kc: catT[:, kc, :], [cat_r], 512)
                    k.op("dve", lambda e: e.tensor_tensor(out=xt[:, mc, :], in0=xt[:, mc, :], in1=ps[b][:], op=ALU.add), reads=[xr, ps_r[b]], writes=[xr])
                k.store(self.X1a[s, :, t * 512:(t + 1) * 512].rearrange("(kc p) t -> p kc t", p=128), xt[:], xr)
            k.barrier_stores()


    def phase_l1a(self):
        k = self.k
        self.need_w("in1")
        TP, NT = self.TP, self.NT
        ps, ps_r = self.ps, self.ps_r
        with ExitStack() as es:
            xT = [k.sb(es, f"c_xT{i}", [128, KC, 512], F32) for i in range(2)]
            tab = [k.sb(es, f"c_tab{i}", [128, 2, 512], F32) for i in range(2)]
            hT, hT_r = k.sb(es, "c_hT", [128, KC, 512], BF16)
            sq = [k.sb(es, f"c_sq{i}", [128, 512], BF16) for i in range(2)]
            rstd, rstd_r = k.sb(es, "c_rstd", [128, 512], F32)
            xst = [k.sb(es, f"c_xst{i}", [128, 512], F32) for i in range(2)]
            qkst = [k.sb(es, f"c_qkst{i}", [128, 512], BF16) for i in range(3)]
            vst = [k.sb(es, f"c_vst{i}", [128, 256], BF16) for i in range(2)]
            zst = [k.sb(es, f"c_zst{i}", [128, 1024], F32) for i in range(2)]
            dst = [k.sb(es, f"c_dst{i}", [128, 32], F32) for i in range(2)]
            tmps = [[k.sb(es, f"c_qraw{i}", [128, 512], F32), k.sb(es, f"c_qsq{i}", [128, 512], BF16), k.sb(es, f"c_rs{i}", [128, 512], F32),
                     k.sb(es, f"c_qn{i}", [128, 512], F32), k.sb(es, f"c_t1{i}", [128, 512], F32)] for i in range(3)]
            pending = []
            wvz, wvz_r = k.sb(es, "c_wvz", [128, 10, KC, 128], BF16)
            wdt, wdt_r = k.sb(es, "c_wdt", [128, KC, 32], BF16)
            ws = Prog.WStream(self, es, "c_ws", KC, 3)
            k.load(wvz[:].rearrange("p a b c -> p a (b c)"), self.wt_in1[10:20].rearrange("mc p kc m -> p mc (kc m)"), wvz_r)
            k.load(wdt[:], self.wt_dt[:, :, :], wdt_r)
            tiles = [(s, t) for s in range(2) for t in range(NT)]
            chunks = list(range(0, 10)) + list(range(20, 32))

            def load_x(i):
                s, t = tiles[i]
                xt, xr = xT[i % 2]
                k.load(xt[:], self.X1b[s, :, t * 512:(t + 1) * 512].rearrange("(kc p) t -> p kc t", p=128), xr)
                tb, tr = tab[i % 2]
                k.load(tb[:], self.rope1[s, :, :, t * 512:(t + 1) * 512].rearrange("c p t -> p c t"), tr)
            load_x(0)
            nchunk = len(chunks) * len(tiles)
            pos = [0]

            def prefetch_next():
                if pos[0] < nchunk:
                    ws.prefetch(self.wt_in1[chunks[pos[0] % len(chunks)]])
                    pos[0] += 1
            prefetch_next()
            prefetch_next()
            cq = cx = cv = 0
            for i, (s, t) in enumerate(tiles):
                xt, xr = xT[i % 2]
                tb, tr = tab[i % 2]
                if i + 1 < len(tiles):
                    load_x(i + 1)
                self.rmsnorm_fm(xt, xr, hT, hT_r, 512, PT_GMIX + 16, [a for a, _ in sq], [b for _, b in sq], rstd, rstd_r, 0)
                for mc in chunks:
                    prefetch_next()
                    wt, wr = ws.pop()
                    b = 1 + (mc % 2)
                    self.mm_group(ps[b][:], ps_r[b], wt, wr, KC, lambda kc: hT[:, kc, :], [hT_r], 512)
                    if mc < 10:
                        isq = mc < 8
                        st, sr = qkst[cq % 3]
                        if isq:
                            dstap = self.Q1[s, mc * 128:(mc + 1) * 128, t * 512:(t + 1) * 512]
                        elif s == 0:
                            dstap = self.K1p[(mc - 8) * 128:(mc - 7) * 128, t * 512:(t + 1) * 512]
                        else:
                            dstap = self.K1s[mc - 8][:, t * 512:(t + 1) * 512]
                        stages = self.head_norm_rope(b, 512, PT_QN1 if isq else PT_KN1, CM_R1, tb[:, 0, :], tb[:, 1, :], tr, st[:], sr, tmps[cq % 3], 3, 4,
                                                     store_fn=(lambda dstap=dstap, st=st, sr=sr: k.store(dstap, st[:], sr)))
                        cq += 1
                        Prog.run_pending(pending, new=stages)
                    else:
                        m2 = mc - 20
                        st, sr = xst[cx % 2]
                        cx += 1
                        k.op("act", lambda e: e.copy(out=st[:], in_=ps[b][:]), reads=[ps_r[b]], writes=[sr])
                        k.store(self.XBC[s, m2 * 128:(m2 + 1) * 128, 2 + t * 512:2 + (t + 1) * 512], st[:], sr)
                        Prog.run_pending(pending)
                        if s == 1 and t == 0:
                            k.store(self.XBin[m2 * 128:(m2 + 1) * 128, 0:2], st[:, 0:2], sr)
                        if s == 1 and t == NT - 1:
                            k.store(self.XBin[m2 * 128:(m2 + 1) * 128, 2:4], st[:, 510:512], sr)
                Prog.run_pending(pending, flush=True)
                for j in range(4):
                    tok0 = t * 512 + j * 128
                    lhs = lambda kc: hT[:, kc, j * 128:(j + 1) * 128]
                    b = 5
                    for g in range(2):
                        for kc in range(KC):
                            k.op("pe", lambda e: e.matmul(ps[b][:, g * 128:(g + 1) * 128], lhsT=lhs(kc), rhs=wvz[:, g, kc, :], start=(kc == 0), stop=(kc == KC - 1)),
                                 reads=[hT_r, wvz_r], writes=[ps_r[b]], inc=(kc == KC - 1))
                    for kc in range(KC):
                        k.op("pe", lambda e: e.matmul(ps[b][:, 256:288], lhsT=lhs(kc), rhs=wdt[:, kc, :], start=(kc == 0), stop=(kc == KC - 1)),
                             reads=[hT_r, wdt_r], writes=[ps_r[b]], inc=(kc == KC - 1))
                    st, sr = vst[cv % 2]
                    dt_, dr = dst[cv % 2]
                    zt, zr = zst[cv % 2]
                    cv += 1
                    k.op("act", lambda e: e.copy(out=st[:], in_=ps[b][:, 0:256]), reads=[ps_r[b]], writes=[sr])
                    k.op("act", lambda e: e.copy(out=dt_[:], in_=ps[b][:, 256:288]), reads=[ps_r[b]], writes=[dr])
                    if s == 0:
                        k.store(self.V1p[tok0:tok0 + 128, :], st[:], sr)
                    else:
                        for g in range(2):
                            k.store(self.V1s[g][tok0:tok0 + 128, :], st[:, g * 128:(g + 1) * 128], sr)
                    k.store(self.DT[s, tok0:tok0 + 128, :], dt_[:], dr)
                    for half in range(2):
                        bz = 6 + half
                        for m in range(4):
                            mz = 2 + half * 4 + m
                            for kc in range(KC):
                                k.op("pe", lambda e: e.matmul(ps[bz][:, m * 128:(m + 1) * 128], lhsT=lhs(kc), rhs=wvz[:, mz, kc, :], start=(kc == 0), stop=(kc == KC - 1)),
                                     reads=[hT_r, wvz_r], writes=[ps_r[bz]], inc=(kc == KC - 1))
                        k.op("act", lambda e: e.activation(out=zt[:, half * 512:(half + 1) * 512], in_=ps[bz][:], func=AF.Silu), reads=[ps_r[bz]], writes=[zr])
                    k.store(self.Z[s, tok0:tok0 + 128, :], zt[:], zr)
            k.barrier_stores(engines=("sp", "pool"))

    def collective(self, src, dst):
        k = self.k
        sem = k.newsem("cc%d" % k.nsem)
        k.eng["pool"].collective_compute("AllGather", ALU.bypass, replica_groups=[[0, 1, 2, 3], [4, 5, 6, 7]],
                                         ins=[src.tensor.ap().opt()], outs=[dst.tensor.ap().opt()]).then_inc(sem)
        k.eng["pool"].wait_ge(sem, 1)
        k.ninstr += 1
        return (sem, 1)

    def phase_exchange1(self):
        k = self.k
        TP = self.TP
        evs = [self.collective(self.K1s[g], self.K1g[g]) for g in range(2)] + [self.collective(self.V1s[g], self.V1g[g]) for g in range(2)]
        evs.append(self.collective(self.XBin, self.XBg))
        k.wait_events("sp", evs)
        with ExitStack() as es:
            zt, zr = k.sb(es, "x_z", [128, 12, 2], F32)
            xg, xgr = k.sb(es, "x_g", [128, NR, 12, 4], F32)
            hl, hlr = k.sb(es, "x_hl", [128, 12, 2], F32)
            hr, hrr = k.sb(es, "x_hr", [128, 12, 2], F32)
            k.op("dve", lambda e: e.memset(zt[:], 0.0), writes=[zr])
            with self.nc.allow_non_contiguous_dma(reason="tiny conv halo"):
                k.store(self.XBC[0, :, 0:2].rearrange("(mc p) c -> p mc c", p=128), zt[:], zr)
                k.store(self.XBC[0, :, TP + 2:TP + 4].rearrange("(mc p) c -> p mc c", p=128), zt[:], zr)
                k.load(xg[:], self.XBg.rearrange("(r mc p) c -> p r mc c", p=128, mc=12), xgr)
                for (dstt, dr, sel, c0) in ((hl, hlr, PT_SELP, 2), (hr, hrr, PT_SELN, 0)):
                    k.op("dve", lambda e: e.tensor_scalar(out=dstt[:], in0=xg[:, 0, :, c0:c0 + 2], scalar1=self.ptab[:, sel:sel + 1], scalar2=None, op0=ALU.mult),
                         reads=[xgr, self.ptab_r], writes=[dr])
                    for r in range(1, NR):
                        k.op("dve", lambda e: e.scalar_tensor_tensor(out=dstt[:], in0=xg[:, r, :, c0:c0 + 2], scalar=self.ptab[:, sel + r:sel + r + 1], in1=dstt[:],
                                                                     op0=ALU.mult, op1=ALU.add), reads=[xgr, self.ptab_r, dr], writes=[dr])
                k.store(self.XBC[1, :, 0:2].rearrange("(mc p) c -> p mc c", p=128), hl[:], hlr)
                k.store(self.XBC[1, :, TP + 2:TP + 4].rearrange("(mc p) c -> p mc c", p=128), hr[:], hrr)
            k.barrier_stores()

    def phase_att(self):
        k = self.k
        TP, NT, NCH = self.TP, self.NT, self.NCH
        ps, ps_r = self.ps, self.ps_r
        scale = 128.0 ** -0.5
        ones = self.cmb[:, CM_ONES:CM_ONES + 128]
        with ExitStack() as es:
            NKmax = NR * NCH
            kT, kT_r = k.sb(es, "d_kT", [128, NKmax * 128], BF16)
            vv, vv_r = k.sb(es, "d_vv", [128, NKmax, 128], BF16)
            qT = [k.sb(es, f"d_qT{i}", [128, 512], BF16) for i in range(2)]
            pT = [k.sb(es, f"d_pT{i}", [128, 512], BF16) for i in range(3)]
            den, den_r = k.sb(es, "d_den", [128, 512], F32)
            ost = [k.sb(es, f"d_ost{i}", [128, 512], BF16) for i in range(2)]
            sacc = [k.sb(es, f"d_sacc{i}", [128, 512], F32) for i in range(2)]
            cq = ce = 0
            for s in range(2):
                NK = NCH if s == 0 else NR * NCH
                for g in range(2):
                    if s == 0:
                        k.load(kT[:, 0:TP], self.K1p[g * 128:(g + 1) * 128, :], kT_r)
                        k.load(vv[:, 0:NCH, :], self.V1p[:, g * 128:(g + 1) * 128].rearrange("(b p) d -> p b d", p=128), vv_r)
                    else:
                        for r in range(NR):
                            k.load(kT[:, r * TP:(r + 1) * TP], self.K1g[g][r * 128:(r + 1) * 128, :], kT_r)
                        k.load(vv[:, :, :], self.V1g[g].rearrange("(b p) d -> p b d", p=128), vv_r)
                    for t in range(NT):
                        for j in range(4):
                            h = 4 * g + j
                            qt, qtr = qT[cq % 2]
                            k.load(qt[:], self.Q1[s, h * 128:(h + 1) * 128, t * 512:(t + 1) * 512], qtr)
                            bo, bs = (5, 6) if cq % 2 == 0 else (7, 0)

                            def S(kb):
                                b = 2 + (kb % 3)
                                k.op("pe", lambda e: e.matmul(ps[b][:], lhsT=kT[:, kb * 128:(kb + 1) * 128], rhs=qt[:], start=True, stop=True),
                                     reads=[kT_r, qtr], writes=[ps_r[b]])
                            S(0)
                            if NK > 1:
                                S(1)
                            for kb in range(NK):
                                b = 2 + (kb % 3)
                                pt_, ptr = pT[kb % 3]
                                k.op("act", lambda e: e.activation(out=pt_[:], in_=ps[b][:], func=AF.Exp, scale=scale), reads=[ps_r[b]], writes=[ptr])
                                if kb + 2 < NK:
                                    S(kb + 2)
                                k.op("pe", lambda e: e.matmul(ps[bo][:], lhsT=vv[:, kb, :], rhs=pt_[:], start=(kb == 0), stop=(kb == NK - 1)),
                                     reads=[vv_r, ptr], writes=[ps_r[bo]])
                                if kb % 2 == 0:
                                    last_even = NK - 1 if (NK - 1) % 2 == 0 else NK - 2
                                    k.op("pe", lambda e: e.matmul(ps[bs][:], lhsT=ones, rhs=pt_[:], start=(kb == 0), stop=(kb == last_even)),
                                         reads=[self.cmb_r, ptr], writes=[ps_r[bs]], inc=(kb == last_even))
                                else:
                                    sa_, sar = sacc[0]
                                    if kb == 1:
                                        k.op("dve", lambda e: e.tensor_copy(out=sa_[:], in_=pt_[:]), reads=[ptr], writes=[sar])
                                    else:
                                        k.op("dve", lambda e: e.tensor_tensor(out=sa_[:], in0=sa_[:], in1=pt_[:], op=ALU.add), reads=[ptr, sar], writes=[sar])
                            ot, otr = ost[cq % 2]
                            cq += 1
                            k.op("act", lambda e: e.copy(out=den[:], in_=ps[bs][:]), reads=[ps_r[bs]], writes=[den_r])
                            if NK > 1:
                                k.op("pe", lambda e: e.matmul(ps[1][:], lhsT=self.cmat[:, CM_ONES:CM_ONES + 128], rhs=sacc[0][0][:], start=True, stop=True),
                                     reads=[self.cmat_r, sacc[0][1]], writes=[ps_r[1]])
                                k.op("dve", lambda e: e.tensor_tensor(out=den[:], in0=den[:], in1=ps[1][:], op=ALU.add), reads=[den_r, ps_r[1]], writes=[den_r])
                            k.op("dve", lambda e: e.reciprocal(out=den[:], in_=den[:]), reads=[den_r], writes=[den_r])
                            k.op("dve", lambda e: e.tensor_tensor(out=ot[:], in0=ps[bo][:], in1=den[:], op=ALU.mult), reads=[ps_r[bo], den_r], writes=[otr])
                            k.store(self.CO[s, h * 128:(h + 1) * 128, t * 512:(t + 1) * 512], ot[:], otr)
            k.barrier_stores()

    def ssd_sweep(self, s, d):
        k = self.k
        TP, NCH = self.TP, self.NCH
        ps, ps_r = self.ps, self.ps_r
        pt = self.ptab
        tri = CM_U if d == 0 else CM_L
        ident = self.cmat[:, CM_I:CM_I + 128]
        with ExitStack() as es:
            def mk(par):
                o = {}
                for nm, shp, dt_ in (("xc", [128, 12, 132], F32), ("dtr", [128, 32], F32), ("yp", [128, 1024], F32), ("acc", [128, 12, 128], F32),
                                     ("xsf", [128, 12, 128], F32), ("bcT", [128, 4, 128], BF16), ("xs", [128, 16, 64], F32), ("btok", [128, 256], BF16),
                                     ("sm", [128, 16, 12], F32), ("xd", [128, 16, 64], BF16), ("xdd", [128, 16, 64], BF16), ("cbm", [128, 2, 128], F32),
                                     ("cst", [128, 16], F32)):
                    if nm == "yp" and d == 0:
                        continue
                    o[nm] = k.sb(es, f"s_{nm}{par}", shp, dt_)
                o["accs"] = [k.res("accm") for _ in range(12)]
                return o
            P = [mk(0), mk(1)]
            sg = [k.sb(es, f"s_sg{i}", [128, 128], F32) for i in range(4)]
            mm = [k.sb(es, f"s_mm{i}", [128, 128], BF16) for i in range(4)]
            yac = [k.sb(es, f"s_yac{i}", [128, 16, 64], F32) for i in range(2)]
            ty, ty_r = k.sb(es, "s_ty", [128, 16, 64], F32)
            state, state_r = k.sb(es, "s_state", [128, 2, 8, 64], F32)
            stb, stb_r = k.sb(es, "s_stb", [128, 2, 512], BF16)
            run, run_r = k.sb(es, "s_run", [128, 16], F32)
            k.op("dve", lambda e: e.memset(state[:], 0.0), writes=[state_r])
            k.op("dve", lambda e: e.memset(stb[:], 0.0), writes=[stb_r])
            k.op("dve", lambda e: e.memset(run[:], 0.0), writes=[run_r])
            order = list(range(NCH)) if d == 0 else list(range(NCH - 1, -1, -1))
            hsl = slice(d * 16, d * 16 + 16)

            def load(i):
                c = order[i]
                B_ = P[i % 2]
                t_, r_ = B_["xc"]
                k.load(t_[:], self.XBC[s, :, c * 128:c * 128 + 132].rearrange("(mc p) t -> p mc t", p=128), r_)
                t2, r2 = B_["dtr"]
                k.load(t2[:], self.DT[s, c * 128:(c + 1) * 128, :], r2)
                if d == 1:
                    t3, r3 = B_["yp"]
                    k.load(t3[:], self.Y[s, c * 128:(c + 1) * 128, :], r3)

            def prep_ops(i):
                c = order[i]
                B_ = P[i % 2]
                xct, xcr = B_["xc"]
                dtt, dtr_r = B_["dtr"]
                acc, acc_r = B_["acc"]
                xsf, xsf_r = B_["xsf"]
                bcT, bcT_r = B_["bcT"]
                xs, xs_r = B_["xs"]
                btok, btok_r = B_["btok"]
                sm, sm_r = B_["sm"]
                xd, xd_r = B_["xd"]
                xdd, xdd_r = B_["xdd"]
                cbm, cbm_r = B_["cbm"]
                cst, cst_r = B_["cst"]
                SM = lambda j: sm[:, :, j]
                T = []
                A = T.append
                accs = B_["accs"]
                wc = lambda m, kk: pt[:, PT_CONVW + m * 5 + kk:PT_CONVW + m * 5 + kk + 1]
                for m in range(12):
                    A(lambda m=m: k.op("dve", lambda e: e.tensor_scalar(out=acc[:, m, :], in0=xct[:, m, 0:128], scalar1=wc(m, 0), scalar2=pt[:, PT_CONVB + m:PT_CONVB + m + 1],
                                                                         op0=ALU.mult, op1=ALU.add), reads=[xcr, self.ptab_r], writes=[accs[m]]))
                for kk in range(1, 5):
                    for m in range(12):
                        A(lambda m=m, kk=kk: k.op("dve", lambda e: e.scalar_tensor_tensor(out=acc[:, m, :], in0=xct[:, m, kk:kk + 128], scalar=wc(m, kk), in1=acc[:, m, :],
                                                                                           op0=ALU.mult, op1=ALU.add), reads=[xcr, self.ptab_r, accs[m]], writes=[accs[m]]))
                A(lambda: k.op("act", lambda e: e.activation(out=xsf[:], in_=acc[:], func=AF.Silu), reads=accs, writes=[xsf_r] + [acc_r]))
                A(lambda: k.op("dve", lambda e: e.tensor_copy(out=bcT[:], in_=xsf[:, 8:12, :]), reads=[xsf_r], writes=[bcT_r]))
                if s == 1 and d == 0:
                    A(lambda: k.store(self.CT[:, c * 128:(c + 1) * 128].rearrange("(g p) t -> p g t", p=128), bcT[:, 2:4, :], bcT_r))
                for m in range(8):
                    A(lambda m=m: k.op("pe", lambda e: e.transpose(ps[m // 4][:, (m % 4) * 128:(m % 4 + 1) * 128], xsf[:, m, :], ident), reads=[xsf_r, self.cmat_r], writes=[ps_r[m // 4]]))
                for b in range(2):
                    A(lambda b=b: k.op("act", lambda e: e.copy(out=xs[:, b * 8:(b + 1) * 8, :].rearrange("p a b -> p (a b)"), in_=ps[b][:]), reads=[ps_r[b]], writes=[xs_r]))
                for m in range(2):
                    A(lambda m=m: k.op("pe", lambda e: e.transpose(ps[2][:, m * 128:(m + 1) * 128], xsf[:, 8 + m, :], ident), reads=[xsf_r, self.cmat_r], writes=[ps_r[2]]))
                A(lambda: k.op("act", lambda e: e.copy(out=btok[:], in_=ps[2][:, 0:256]), reads=[ps_r[2]], writes=[btok_r]))
                A(lambda: k.op("dve", lambda e: e.tensor_tensor(out=SM(0), in0=dtt[:, hsl], in1=pt[:, PT_DTB + d * 16:PT_DTB + d * 16 + 16], op=ALU.add), reads=[dtr_r, self.ptab_r], writes=[sm_r]))
                A(lambda: k.op("dve", lambda e: e.scalar_tensor_tensor(out=SM(1), in0=SM(0), scalar=-1.0, in1=SM(0), op0=ALU.mult, op1=ALU.max), reads=[sm_r], writes=[sm_r]))
                A(lambda: k.op("act", lambda e: e.activation(out=SM(1), in_=SM(1), func=AF.Exp, scale=-1.0), reads=[sm_r], writes=[sm_r]))
                A(lambda: k.op("act", lambda e: e.activation(out=SM(1), in_=SM(1), func=AF.Ln, bias=1.0, scale=1.0), reads=[sm_r], writes=[sm_r]))
                A(lambda: k.op("dve", lambda e: e.scalar_tensor_tensor(out=SM(2), in0=SM(0), scalar=0.0, in1=SM(1), op0=ALU.max, op1=ALU.add), reads=[sm_r], writes=[sm_r]))
                A(lambda: k.op("dve", lambda e: e.tensor_tensor(out=SM(3), in0=SM(2), in1=pt[:, PT_ALOG + d * 16:PT_ALOG + d * 16 + 16], op=ALU.mult), reads=[sm_r, self.ptab_r], writes=[sm_r]))
                A(lambda: k.op("pe", lambda e: e.matmul(ps[2][:, 256:272], lhsT=self.cmat[:, tri:tri + 128], rhs=SM(3), start=True, stop=True), reads=[sm_r, self.cmat_r], writes=[ps_r[2]]))
                A(lambda: k.op("pe", lambda e: e.matmul(ps[2][:, 272:288], lhsT=self.cmat[:, CM_ONES:CM_ONES + 128], rhs=SM(3), start=True, stop=True), reads=[sm_r, self.cmat_r], writes=[ps_r[2]]))
                A(lambda: k.op("dve", lambda e: e.tensor_copy(out=SM(4), in_=ps[2][:, 256:272]), reads=[ps_r[2]], writes=[sm_r]))
                A(lambda: k.op("dve", lambda e: e.tensor_copy(out=SM(5), in_=ps[2][:, 272:288]), reads=[ps_r[2]], writes=[sm_r]))
                A(lambda: k.op("dve", lambda e: e.tensor_tensor(out=SM(6), in0=SM(5), in1=SM(4), op=ALU.subtract), reads=[sm_r], writes=[sm_r]))
                A(lambda: k.op("dve", lambda e: e.tensor_scalar(out=SM(11), in0=SM(4), scalar1=-1.0, scalar2=None, op0=ALU.mult), reads=[sm_r], writes=[sm_r]))
                A(lambda: k.op("act", lambda e: e.activation(out=SM(6), in_=SM(6), func=AF.Exp), reads=[sm_r], writes=[sm_r]))
                A(lambda: k.op("act", lambda e: e.activation(out=SM(7), in_=SM(4), func=AF.Exp), reads=[sm_r], writes=[sm_r]))
                A(lambda: k.op("act", lambda e: e.activation(out=SM(10), in_=SM(5), func=AF.Exp), reads=[sm_r], writes=[sm_r]))
                A(lambda: k.op("dve", lambda e: e.tensor_tensor(out=SM(8), in0=SM(2), in1=SM(6), op=ALU.mult), reads=[sm_r], writes=[sm_r]))
                A(lambda: k.op("dve", lambda e: e.tensor_tensor(out=SM(9), in0=SM(4), in1=run[:], op=ALU.add), reads=[sm_r, run_r], writes=[sm_r]))
                A(lambda: k.op("dve", lambda e: e.tensor_tensor(out=run[:], in0=run[:], in1=SM(5), op=ALU.add), reads=[sm_r, run_r], writes=[run_r]))
                if s == 1:
                    A(lambda: k.op("dve", lambda e: e.tensor_copy(out=cst[:], in_=SM(9)), reads=[sm_r], writes=[cst_r]))
                    A(lambda: k.store(self.CUM[s, d, c * 128:(c + 1) * 128, :], cst[:], cst_r))
                bc3 = lambda j: sm[:, :, j:j + 1].to_broadcast([128, 16, 64])
                A(lambda: k.op("dve", lambda e: e.tensor_tensor(out=xd[:], in0=xs[:], in1=bc3(2), op=ALU.mult), reads=[xs_r, sm_r], writes=[xd_r]))
                A(lambda: k.op("pool", lambda e: e.tensor_tensor(out=xdd[:], in0=xs[:], in1=bc3(8), op=ALU.mult), reads=[xs_r, sm_r], writes=[xdd_r]))
                for g in range(2):
                    A(lambda g=g: k.op("pe", lambda e: e.matmul(ps[3][:, g * 128:(g + 1) * 128], lhsT=bcT[:, g, :], rhs=bcT[:, 2 + g, :], start=True, stop=True), reads=[bcT_r], writes=[ps_r[3]]))
                msk = self.cmat[:, tri:tri + 128].unsqueeze(1).to_broadcast([128, 2, 128])
                A(lambda: k.op("dve", lambda e: e.tensor_tensor(out=cbm[:], in0=ps[3][:, 0:256].rearrange("p (g l) -> p g l", g=2), in1=msk, op=ALU.mult), reads=[ps_r[3], self.cmat_r], writes=[cbm_r]))
                return T

            def main_stage(i, filler):
                c = order[i]
                B_ = P[i % 2]
                bcT, bcT_r = B_["bcT"]
                xs, xs_r = B_["xs"]
                btok, btok_r = B_["btok"]
                sm, sm_r = B_["sm"]
                xd, xd_r = B_["xd"]
                xdd, xdd_r = B_["xdd"]
                cbm, cbm_r = B_["cbm"]
                ya, yar = yac[i % 2]
                nfill = (len(filler) + 15) // 16
                q_r = self.__dict__.setdefault("q_r", [k.res("q%d" % j) for j in range(4)])
                sg4 = sg

                def fill(n):
                    for _ in range(n):
                        if filler:
                            filler.pop(0)()
                for g in range(2):
                    k.op("pe", lambda e: e.matmul(ps[4 + g][:], lhsT=bcT[:, 2 + g, :], rhs=stb[:, g, :], start=True, stop=True), reads=[bcT_r, stb_r], writes=[q_r[2 * g], q_r[2 * g + 1]])
                for g in range(2):
                    k.op("dve", lambda e: e.tensor_tensor(out=ya[:, g * 8:(g + 1) * 8, :], in0=ps[4 + g][:].rearrange("p (a b) -> p a b", b=64),
                                                          in1=sm[:, g * 8:(g + 1) * 8, 7:8].to_broadcast([128, 8, 64]), op=ALU.mult), reads=[q_r[2 * g], q_r[2 * g + 1], sm_r], writes=[yar])
                for h in range(16):
                    g = h // 8
                    j4 = h % 4
                    b = 4 + j4 // 2
                    c4 = (j4 % 2) * 128
                    sgt, sgr = sg[h % 4]
                    mt, mr = mm[h % 4]
                    ntri = CM_NU if d == 0 else CM_NL
                    k.op("pe", lambda e: e.matmul(ps[b][:, c4:c4 + 128], lhsT=sm[:, h, 3:4].to_broadcast([128, 128]), rhs=self.cmat[:, tri:tri + 128], start=True, stop=False),
                         reads=[sm_r, self.cmat_r], writes=[q_r[j4]], inc=False)
                    k.op("pe", lambda e: e.matmul(ps[b][:, c4:c4 + 128], lhsT=ident, rhs=self.cmat[:, ntri:ntri + 128], start=False, stop=True),
                         reads=[self.cmat_r], writes=[q_r[j4]])
                    k.op("act", lambda e: e.activation(out=sgt[:], in_=ps[b][:, c4:c4 + 128], func=AF.Exp, bias=sm[:, h, 11:12], scale=1.0), reads=[q_r[j4], sm_r], writes=[sgr])
                    k.op("dve", lambda e: e.tensor_tensor(out=mt[:], in0=sgt[:], in1=cbm[:, g, :], op=ALU.mult), reads=[sgr, cbm_r], writes=[mr])
                    by = 6 + g
                    k.op("pe", lambda e: e.matmul(ps[by][:, (h % 8) * 64:(h % 8 + 1) * 64], lhsT=mt[:], rhs=xd[:, h, :], start=True, stop=True), reads=[mr, xd_r], writes=[ps_r[by]])
                    fill(nfill)
                for g in range(2):
                    k.op("dve", lambda e: e.tensor_tensor(out=ya[:, g * 8:(g + 1) * 8, :], in0=ya[:, g * 8:(g + 1) * 8, :], in1=ps[6 + g][:].rearrange("p (a b) -> p a b", b=64), op=ALU.add),
                         reads=[yar, ps_r[6 + g]], writes=[yar])
                if d == 0:
                    dsk = pt[:, PT_DSKIP:PT_DSKIP + 16].unsqueeze(2).to_broadcast([128, 16, 64])
                    k.op("pool", lambda e: e.tensor_tensor(out=ty[:], in0=xs[:], in1=dsk, op=ALU.mult), reads=[xs_r, self.ptab_r], writes=[ty_r])
                    k.op("dve", lambda e: e.tensor_tensor(out=ya[:], in0=ya[:], in1=ty[:], op=ALU.add), reads=[yar, ty_r], writes=[yar])
                else:
                    ypt, ypr = B_["yp"]
                    k.op("dve", lambda e: e.tensor_tensor(out=ya[:], in0=ya[:], in1=ypt[:].rearrange("p (a b) -> p a b", b=64), op=ALU.add), reads=[yar, ypr], writes=[yar])
                k.store(self.Y[s, c * 128:(c + 1) * 128, :], ya[:].rearrange("p a b -> p (a b)"), yar)
                for g in range(2):
                    k.op("pe", lambda e: e.matmul(ps[4 + g][:], lhsT=btok[:, g * 128:(g + 1) * 128], rhs=xdd[:, g * 8:(g + 1) * 8, :].rearrange("p a b -> p (a b)"), start=True, stop=True),
                         reads=[btok_r, xdd_r], writes=[q_r[2 * g], q_r[2 * g + 1]])
                for g in range(2):
                    k.op("dve", lambda e: e.tensor_tensor(out=state[:, g], in0=state[:, g], in1=sm[:, g * 8:(g + 1) * 8, 10:11].to_broadcast([128, 8, 64]), op=ALU.mult),
                         reads=[state_r, sm_r], writes=[state_r])
                    k.op("dve", lambda e: e.tensor_tensor(out=state[:, g], in0=state[:, g], in1=ps[4 + g][:].rearrange("p (a b) -> p a b", b=64), op=ALU.add),
                         reads=[state_r, q_r[2 * g], q_r[2 * g + 1]], writes=[state_r])
                k.op("act", lambda e: e.copy(out=stb[:], in_=state[:].rearrange("p g a b -> p g (a b)")), reads=[state_r], writes=[stb_r])
                fill(len(filler))

            load(0)
            if NCH > 1:
                load(1)
            for t_ in prep_ops(0):
                t_()
            for i in range(NCH):
                nxt = prep_ops(i + 1) if i + 1 < NCH else []
                main_stage(i, nxt)
                if i + 2 < NCH:
                    load(i + 2)
            if s == 1:
                for g in range(2):
                    k.store(self.SSin[d * 256 + g * 128:d * 256 + (g + 1) * 128, :], state[:, g].rearrange("p a b -> p (a b)"), state_r)
                k.store(self.SAin[0:1, d * 16:(d + 1) * 16], run[0:1, :], run_r)
                if self.DBG_ST is not None:
                    k.store(self.DBG_ST[d], state[:].rearrange("p g a b -> p (g a b)"), state_r)
            k.barrier_stores(engines=("sp", "pool"))


    def phase_exchange2(self):
        k = self.k
        evs = [self.collective(self.SSin, self.SSg), self.collective(self.SAin, self.SAg)]
        k.wait_events("sp", evs)

    def phase_l1c(self):
        k = self.k
        self.need_w("out1")
        TP, NT = self.TP, self.NT
        ps, ps_r = self.ps, self.ps_r
        pt = self.ptab
        with ExitStack() as es:
            xTs = [k.sb(es, f"e_xT{i}", [128, KC, 512], F32) for i in range(2)]
            cats = [k.sb(es, f"e_cat{i}", [128, 16, 512], BF16) for i in range(2)]
            yt = [k.sb(es, f"e_y{i}", [128, 16, 64], F32) for i in range(2)]
            zt = [k.sb(es, f"e_z{i}", [128, 1024], F32) for i in range(2)]
            ctc = [k.sb(es, f"e_ct{i}", [128, 2, 128], BF16) for i in range(2)]
            cum = [k.sb(es, f"e_cum{i}", [128, 2, 16], F32) for i in range(2)]
            junk, junk_r = k.sb(es, "e_junk", [128, 512], F32)
            ssq, ssq_r = k.sb(es, "e_ssq", [128, 2], F32)
            dn, dn_r = k.sb(es, "e_dn", [128, 1024], F32)
            dn2, dn2_r = k.sb(es, "e_dn2", [128, 1024], F32)
            abc, abc_r = k.sb(es, "e_abc", [128, NR, 32], F32)
            ee, ee_r = k.sb(es, "e_ee", [128, 16], F32)
            sl = [k.sb(es, f"e_sl{i}", [128, 8, 64], F32) for i in range(2)]
            ini, ini_r = k.sb(es, "e_ini", [128, 4, 8, 64], F32)
            inib, inib_r = k.sb(es, "e_inib", [128, 4, 512], BF16)
            ws = Prog.WStream(self, es, "e_ws", KC, 3)
            for r in range(NR):
                k.load(abc[:, r, :], self.SAg[r * 8:r * 8 + 1, :].partition_broadcast(128), abc_r)
            k.op("dve", lambda e: e.memset(ini[:], 0.0), writes=[ini_r])
            cs_ = 0
            for d in range(2):
                MC = PT_MF if d == 0 else PT_MB
                FC_ = PT_FF if d == 0 else PT_FB
                for r1 in range(NR):
                    k.op("dve", lambda e: e.tensor_scalar(out=ee[:], in0=abc[:, 0, d * 16:(d + 1) * 16], scalar1=pt[:, MC + 4 * r1:MC + 4 * r1 + 1], scalar2=None, op0=ALU.mult),
                         reads=[abc_r, self.ptab_r], writes=[ee_r])
                    for r2 in range(1, NR):
                        k.op("dve", lambda e: e.scalar_tensor_tensor(out=ee[:], in0=abc[:, r2, d * 16:(d + 1) * 16], scalar=pt[:, MC + 4 * r1 + r2:MC + 4 * r1 + r2 + 1], in1=ee[:],
                                                                     op0=ALU.mult, op1=ALU.add), reads=[abc_r, self.ptab_r, ee_r], writes=[ee_r])
                    k.op("act", lambda e: e.activation(out=ee[:], in_=ee[:], func=AF.Exp), reads=[ee_r], writes=[ee_r])
                    k.op("dve", lambda e: e.tensor_scalar(out=ee[:], in0=ee[:], scalar1=pt[:, FC_ + r1:FC_ + r1 + 1], scalar2=None, op0=ALU.mult), reads=[ee_r, self.ptab_r], writes=[ee_r])
                    for g in range(2):
                        st, sr = sl[cs_ % 2]
                        cs_ += 1
                        k.load(st[:].rearrange("p a b -> p (a b)"), self.SSg[r1 * 512 + d * 256 + g * 128:r1 * 512 + d * 256 + (g + 1) * 128, :], sr)
                        k.op("dve", lambda e: e.tensor_tensor(out=st[:], in0=st[:], in1=ee[:, g * 8:(g + 1) * 8].unsqueeze(2).to_broadcast([128, 8, 64]), op=ALU.mult),
                             reads=[sr, ee_r], writes=[sr])
                        k.op("dve", lambda e: e.tensor_tensor(out=ini[:, d * 2 + g], in0=ini[:, d * 2 + g], in1=st[:], op=ALU.add), reads=[sr, ini_r], writes=[ini_r])
            k.op("act", lambda e: e.copy(out=inib[:], in_=ini[:].rearrange("p q a b -> p q (a b)")), reads=[ini_r], writes=[inib_r])
            if self.DBG_INI is not None:
                k.store(self.DBG_INI[:, :], ini[:].rearrange("p q a b -> p (q a b)"), ini_r)
            tiles = [(s, t) for s in range(2) for t in range(NT)]
            nchunk = 16 * len(tiles)
            pos = [0]

            def prefetch_next():
                if pos[0] < nchunk:
                    ws.prefetch(self.wt_out1[pos[0] % 16])
                    pos[0] += 1
            prefetch_next()
            prefetch_next()
            ident = self.cmat[:, CM_I:CM_I + 128]
            cnt = [0]
            dns = [(dn, dn_r), (dn2, dn2_r)]
            chunkbuf = {}

            def gate_pre(i, j):
                s, t = tiles[i]
                tok0 = t * 512 + j * 128
                cc = cnt[0]
                cnt[0] += 1
                y_, yr = yt[cc % 2]
                z_, zr = zt[cc % 2]
                ct_, ctr = ctc[cc % 2]
                cu_, cur = cum[cc % 2]
                dnt, dnr = dns[cc % 2]
                chunkbuf[(i, j)] = (dnt, dnr)
                k.load(y_[:].rearrange("p a b -> p (a b)"), self.Y[s, tok0:tok0 + 128, :], yr)
                k.load(z_[:], self.Z[s, tok0:tok0 + 128, :], zr)
                if s == 1:
                    k.load(ct_[:], self.CT[:, tok0:tok0 + 128].rearrange("(g p) t -> p g t", p=128), ctr)
                    for d in range(2):
                        k.load(cu_[:, d, :], self.CUM[1, d, tok0:tok0 + 128, :], cur)
                    k.op("act", lambda e: e.activation(out=cu_[:], in_=cu_[:], func=AF.Exp), reads=[cur], writes=[cur])
                    for d in range(2):
                        for g in range(2):
                            b = 3 + ((2 * d + g) % 2)
                            k.op("pe", lambda e: e.matmul(ps[b][:], lhsT=ct_[:, g, :], rhs=inib[:, 2 * d + g, :], start=True, stop=True), reads=[ctr, inib_r], writes=[ps_r[b]])
                            k.op("dve", lambda e: e.tensor_tensor(out=junk[:].rearrange("p (a b) -> p a b", b=64), in0=ps[b][:].rearrange("p (a b) -> p a b", b=64),
                                                                  in1=cu_[:, d, g * 8:(g + 1) * 8].unsqueeze(2).to_broadcast([128, 8, 64]), op=ALU.mult),
                                 reads=[ps_r[b], cur], writes=[junk_r])
                            k.op("dve", lambda e: e.tensor_tensor(out=y_[:, g * 8:(g + 1) * 8, :], in0=y_[:, g * 8:(g + 1) * 8, :], in1=junk[:].rearrange("p (a b) -> p a b", b=64), op=ALU.add),
                                 reads=[yr, junk_r], writes=[yr])
                yf = y_[:].rearrange("p a b -> p (a b)")
                k.op("dve", lambda e: e.tensor_tensor(out=yf, in0=yf, in1=z_[:], op=ALU.mult), reads=[yr, zr], writes=[yr])
                for g in range(2):
                    k.op("act", lambda e: e.activation(out=junk[:], in_=yf[:, g * 512:(g + 1) * 512], func=AF.Square, accum_out=ssq[:, g:g + 1]), reads=[yr], writes=[junk_r, ssq_r])
                k.op("act", lambda e: e.activation(out=ssq[:], in_=ssq[:], func=AF.Ln, bias=EPS, scale=1.0 / 512), reads=[ssq_r], writes=[ssq_r])
                k.op("act", lambda e: e.activation(out=ssq[:], in_=ssq[:], func=AF.Exp, scale=-0.5), reads=[ssq_r], writes=[ssq_r])
                for g in range(2):
                    k.op("dve", lambda e: e.scalar_tensor_tensor(out=dnt[:, g * 512:(g + 1) * 512], in0=yf[:, g * 512:(g + 1) * 512], scalar=ssq[:, g:g + 1],
                                                                 in1=self.gnrow[:, g * 512:(g + 1) * 512], op0=ALU.mult, op1=ALU.mult), reads=[yr, ssq_r, self.gn_r], writes=[dnr])

            def gate_pe(i, j):
                dnt, dnr = chunkbuf.pop((i, j))
                ct_, ctr_ = cats[i % 2]
                for m in range(8):
                    b = 5 + (m // 4)
                    k.op("pe", lambda e: e.transpose(ps[b][:, (m % 4) * 128:(m % 4 + 1) * 128], dnt[:, m * 128:(m + 1) * 128], ident), reads=[dnr, self.cmat_r], writes=[ps_r[b]])
                for hb in range(2):
                    k.op("act", lambda e: e.copy(out=ct_[:, 8 + hb * 4:8 + hb * 4 + 4, j * 128:(j + 1) * 128], in_=ps[5 + hb][:].rearrange("p (m t) -> p m t", t=128)),
                         reads=[ps_r[5 + hb]], writes=[ctr_])

            def load_co(i):
                s, t = tiles[i]
                ct_, ctr_ = cats[i % 2]
                k.load(ct_[:, 0:8, :], self.CO[s, :, t * 512:(t + 1) * 512].rearrange("(h p) t -> p h t", p=128), ctr_)

            load_co(0)
            for j in range(4):
                gate_pre(0, j)
                gate_pe(0, j)
            for i, (s, t) in enumerate(tiles):
                xt_, xtr = xTs[i % 2]
                ct_, ctr_ = cats[i % 2]
                k.load(xt_[:], self.X1b[s, :, t * 512:(t + 1) * 512].rearrange("(kc p) t -> p kc t", p=128), xtr)
                more = i + 1 < len(tiles)
                if more:
                    load_co(i + 1)
                for j in range(4):
                    if more:
                        gate_pre(i + 1, j)
                    for mc in range(4 * j, 4 * j + 4):
                        prefetch_next()
                        wt, wr = ws.pop()
                        b = 1 + (mc % 2)
                        self.mm_group(ps[b][:], ps_r[b], wt, wr, KC, lambda kc: ct_[:, kc, :], [ctr_], 512)
                        k.op("dve", lambda e: e.tensor_tensor(out=xt_[:, mc, :], in0=xt_[:, mc, :], in1=ps[b][:], op=ALU.add), reads=[xtr, ps_r[b]], writes=[xtr])
                    if more:
                        gate_pe(i + 1, j)
                k.store(self.X2a[s, :, t * 512:(t + 1) * 512].rearrange("(kc p) t -> p kc t", p=128), xt_[:], xtr)
            k.barrier_stores()


PT_GMIX = 0
PT_GFFN = 32
PT_QN0 = 64
PT_KN0 = 65
PT_QN1 = 66
PT_KN1 = 67
PT_PSCALE = 68
PT_SINK = 76
PT_VALID = 84
PT_SELP = 88
PT_SELN = 92
PT_CONVW = 96
PT_CONVB = 156
PT_DTB = 168
PT_ALOG = 200
PT_DSKIP = 232
PT_MF = 248
PT_MB = 264
PT_FF = 280
PT_FB = 284
PT_N = 288

CM_ONES = 0
CM_BAND = 128
CMB_N = 512
CM_R0 = 512
CM_R1 = 640
CM_U = 768
CM_L = 896
CM_I = 1024
CM_NU = 1152
CM_NL = 1280
CM_N = 1408


def _rot_lhsT(blocks):
    R = np.zeros((128, 128), np.float32)
    for a, h in blocks:
        for i in range(a, a + h):
            R[i, i + h] = -1.0
        for i in range(a + h, a + 2 * h):
            R[i, i - h] = 1.0
    return np.ascontiguousarray(R.T)


def make_cmat():
    cm = np.zeros((128, CM_N), np.float32)
    cm[:, CM_ONES:CM_ONES + 128] = 1.0
    b = np.arange(128)[:, None]
    a = np.arange(128)[None, :]
    cm[:, CM_BAND:CM_BAND + 128] = (b <= a)
    cm[:, CM_BAND + 128:CM_BAND + 256] = 1.0
    cm[:, CM_BAND + 256:CM_BAND + 384] = (a <= b)
    cm[:, CM_R0:CM_R0 + 128] = _rot_lhsT([(0, 16)])
    cm[:, CM_R1:CM_R1 + 128] = _rot_lhsT([(0, 32), (64, 32)])
    cm[:, CM_U:CM_U + 128] = (b <= a)
    cm[:, CM_L:CM_L + 128] = (b >= a)
    cm[:, CM_I:CM_I + 128] = np.eye(128, dtype=np.float32)
    cm[:, CM_NU:CM_NU + 128] = -30000.0 * (1.0 - (b <= a))
    cm[:, CM_NL:CM_NL + 128] = -30000.0 * (1.0 - (b >= a))
    return cm


def rope_tables0(pos):
    half = 16
    freqs = (500000.0 ** (-np.arange(half, dtype=np.float32) / half)).astype(np.float32)
    ang = pos.astype(np.float32)[None, :] * freqs[:, None]
    c = np.ones((128, len(pos)), np.float32)
    s = np.zeros((128, len(pos)), np.float32)
    c[0:16] = np.cos(ang)
    c[16:32] = np.cos(ang)
    s[0:16] = np.sin(ang)
    s[16:32] = np.sin(ang)
    return np.stack([c, s])


def rope_tables1(pos):
    half = 32
    freqs = (10000.0 ** (-np.arange(half, dtype=np.float32) / half)).astype(np.float32)
    row = (pos // 64).astype(np.float32)
    col = (pos % 64).astype(np.float32)
    ar = row[None, :] * freqs[:, None]
    ac = col[None, :] * freqs[:, None]
    c = np.concatenate([np.cos(ar), np.cos(ar), np.cos(ac), np.cos(ac)], 0).astype(np.float32)
    s = np.concatenate([np.sin(ar), np.sin(ar), np.sin(ac), np.sin(ac)], 0).astype(np.float32)
    return np.stack([c, s])


def inv_counts(pos, S):
    out = np.zeros((4, len(pos)), np.float32)
    for gi, win in enumerate((2, 4, 8, 16)):
        lo = np.clip(pos - win // 2, 0, S)
        hi = np.clip(pos + win // 2, 0, S)
        out[gi] = 1.0 / (hi - lo).astype(np.float32)
    return out


def host_prepare(inputs, TP, n_cores=8):
    TE = TP + 2 * HALO
    xp = inputs["x_prompt"]
    xs = inputs["x_sample"]
    SP = xp.shape[1]
    SS = xs.shape[1]
    assert SP == TP and SS == NR * TP and xp.shape[0] == n_cores and xs.shape[0] * NR == n_cores
    cmat = make_cmat()
    in_maps = []
    f = np.float32
    for c in range(n_cores):
        r = c % NR
        sq = c // NR
        xin = np.zeros((2, D, TE), f)
        xin[0, :, HALO:HALO + TP] = xp[c].T
        lo = r * TP - HALO
        hi = (r + 1) * TP + HALO
        clo, chi = max(lo, 0), min(hi, SS)
        xin[1, :, clo - lo:chi - lo] = xs[sq, clo:chi].T
        pt = np.zeros((128, PT_N), f)
        pt[:, PT_GMIX:PT_GMIX + 16] = inputs["norm_mix"][0].reshape(16, 128).T
        pt[:, PT_GMIX + 16:PT_GMIX + 32] = inputs["norm_mix"][1].reshape(16, 128).T
        pt[:, PT_GFFN:PT_GFFN + 16] = inputs["norm_ffn"][0].reshape(16, 128).T
        pt[:, PT_GFFN + 16:PT_GFFN + 32] = inputs["norm_ffn"][1].reshape(16, 128).T
        pt[:, PT_QN0] = inputs["ev_q_norm"][0]
        pt[:, PT_KN0] = inputs["ev_k_norm"][0]
        pt[:, PT_QN1] = inputs["od_q_norm"][0]
        pt[:, PT_KN1] = inputs["od_k_norm"][0]
        pt[:, PT_PSCALE:PT_PSCALE + 8] = inputs["ev_pool_scale"][0].reshape(8, 128).T
        pt[:, PT_SINK:PT_SINK + 8] = inputs["ev_sink"][0][None, :]
        pt[:, PT_VALID + 0] = 0.0
        pt[:, PT_VALID + 1] = 0.0
        pt[:, PT_VALID + 2] = 1.0 if r > 0 else 0.0
        pt[:, PT_VALID + 3] = 1.0 if r < NR - 1 else 0.0
        if r > 0:
            pt[:, PT_SELP + r - 1] = 1.0
        if r < NR - 1:
            pt[:, PT_SELN + r + 1] = 1.0
        pt[:, PT_CONVW:PT_CONVW + 60] = inputs["od_conv_w"][0].reshape(5, 12, 128).transpose(2, 1, 0).reshape(128, 60)
        pt[:, PT_CONVB:PT_CONVB + 12] = inputs["od_conv_b"][0].reshape(12, 128).T
        pt[:, PT_DTB:PT_DTB + 32] = inputs["od_dt_bias"][0].reshape(1, 32)
        pt[:, PT_ALOG:PT_ALOG + 32] = inputs["od_a_log"][0].reshape(1, 32)
        pt[:, PT_DSKIP:PT_DSKIP + 16] = inputs["od_d_skip"][0].reshape(1, 16)
        for r1 in range(NR):
            for r2 in range(NR):
                pt[:, PT_MF + 4 * r1 + r2] = 1.0 if (r1 < r2 < r) else 0.0
                pt[:, PT_MB + 4 * r1 + r2] = 1.0 if (r < r2 < r1) else 0.0
            pt[:, PT_FF + r1] = 1.0 if r1 < r else 0.0
            pt[:, PT_FB + r1] = 1.0 if r1 > r else 0.0
        pos0 = np.arange(-HALO, TP + HALO)
        rope0 = np.stack([rope_tables0(pos0), rope_tables0(pos0 + r * TP)]).astype(f)
        pos1 = np.arange(TP)
        rope1 = np.stack([rope_tables1(pos1), rope_tables1(pos1 + r * TP)]).astype(f)
        invc = np.stack([inv_counts(pos1, SP), inv_counts(pos1 + r * TP, SS)]).astype(f)
        invc = np.ascontiguousarray(invc.reshape(2, 4, TP // 512, 512).transpose(0, 2, 1, 3).reshape(2, TP // 512, 2048))
        gn = np.ascontiguousarray(np.broadcast_to(inputs["od_gate_norm"][0][None, :], (128, 1024)), dtype=f)
        m = {"xin": xin, "ptab": pt, "rope0": rope0, "rope1": rope1, "invc": invc, "cmat": cmat, "gnrow": gn}
        for nm in ("ev_w_in", "ev_w_out", "ev_pool_w", "ffn_w_gate", "ffn_w_up", "ffn_w_down", "od_w_in", "od_w_out"):
            m[nm] = np.ascontiguousarray(inputs[nm], dtype=f)
        in_maps.append(m)
    return in_maps


def host_gather(results, TP, n_cores=8):
    yp = np.stack([results[c]["yout"][0].T for c in range(n_cores)])
    ys = np.stack([np.concatenate([results[sq * NR + r]["yout"][1].T for r in range(NR)], 0) for sq in range(n_cores // NR)])
    return np.ascontiguousarray(yp), np.ascontiguousarray(ys)


def kernel(**inputs):
    TP = inputs["x_prompt"].shape[1]
    prog = Prog(TP)
    nc = prog.build()
    in_maps = host_prepare(inputs, TP)
    res = run_bass_kernel_spmd(nc, in_maps, core_ids=list(range(8)))
    return host_gather(res.results, TP)
```

```python
import math
from contextlib import ExitStack
import numpy as np
import concourse.bass as bass
import concourse.mybir as mybir
from concourse.bass_utils import run_bass_kernel_spmd

F32 = mybir.dt.float32
BF16 = mybir.dt.bfloat16
AF = mybir.ActivationFunctionType
ALU = mybir.AluOpType
AX = mybir.AxisListType

D = 2048
KC = 16
DFF = 5632
FC = 44
EPS = 1e-6
NR = 4
HALO = 128


class Res:
    __slots__ = ("name", "w", "r", "sem", "cnt")

    def __init__(self, name):
        self.name = name
        self.w = []
        self.r = []
        self.sem = None
        self.cnt = 0


class KB:
    def __init__(self, nc, es):
        self.nc = nc
        self.es = es
        self.eng = {"pe": nc.tensor, "act": nc.scalar, "dve": nc.vector, "pool": nc.gpsimd, "sp": nc.sync}
        self.esem = {}
        self.ecnt = {}
        for e in ("pe", "act", "dve", "pool"):
            self.esem[e] = es.enter_context(nc.semaphore("es_" + e))
            self.ecnt[e] = 0
        self.known = {e: {} for e in self.eng}
        self.semh = {}
        self.ninstr = 0
        self.store_evs = []
        self.nsem = 0

    def res(self, name):
        return Res(name)

    def sb(self, es, name, shape, dtype):
        self.nt = getattr(self, "nt", 0) + 1
        name = "%s_%d" % (name, self.nt)
        t = es.enter_context(self.nc.sbuf_tensor(name, list(shape), dtype))
        return t, Res(name)

    def newsem(self, name):
        self.nsem += 1
        return self.es.enter_context(self.nc.semaphore(name))

    def _waits(self, en, reads, writes, extra=()):
        need = {}

        def add(ev):
            s, v = ev
            k = id(s)
            self.semh[k] = s
            if need.get(k, 0) < v:
                need[k] = v
        for r in reads:
            for ev in r.w:
                add(ev)
        for w in writes:
            for ev in w.w:
                add(ev)
            for ev in w.r:
                add(ev)
        for ev in extra:
            add(ev)
        kn = self.known[en]
        own = id(self.esem[en]) if en in self.esem else None
        for k, v in need.items():
            if kn.get(k, 0) >= v:
                continue
            if k == own and (en == "pe" or v > self.ecnt[en]):
                continue
            self.eng[en].wait_ge(self.semh[k], v)
            kn[k] = v

    def _commit(self, ev, reads, writes):
        for r in reads:
            r.r.append(ev)
            if len(r.r) > 48:
                d = {}
                for s, v in r.r:
                    if d.get(id(s), (None, 0))[1] < v:
                        d[id(s)] = (s, v)
                r.r = list(d.values())
        for w in writes:
            w.w = [ev]
            w.r = []

    def op(self, en, fn, reads=(), writes=(), inc=True):
        self._waits(en, reads, writes)
        ins = fn(self.eng[en])
        self.ninstr += 1
        if inc:
            self.ecnt[en] += 1
            ins.then_inc(self.esem[en], 1)
            ev = (self.esem[en], self.ecnt[en])
        else:
            ev = (self.esem[en], self.ecnt[en] + 1)
        self._commit(ev, reads, writes)
        return ins

    def dma(self, out, in_, sbuf_res, reads=(), writes=(), q="sp", is_store=False):
        self._waits(q, reads, writes)
        ins = self.eng[q].dma_start(out=out, in_=in_)
        self.ninstr += 1
        if sbuf_res.sem is None:
            pool = self.__dict__.setdefault("sempool", [])
            if pool:
                sbuf_res.sem, sbuf_res.cnt = pool.pop()
            else:
                sbuf_res.sem = self.newsem("dsem%d" % self.nsem)
                sbuf_res.cnt = 0
            self.__dict__.setdefault("phase_res", []).append(sbuf_res)
        sbuf_res.cnt += 16
        ins.then_inc(sbuf_res.sem, 16)
        ev = (sbuf_res.sem, sbuf_res.cnt)
        self._commit(ev, reads, writes)
        if is_store:
            self.store_evs.append(ev)
        return ev

    def load(self, out, in_, res):
        return self.dma(out, in_, res, writes=[res])

    def store(self, out, in_, res):
        return self.dma(out, in_, res, reads=[res], is_store=True)

    def barrier_stores(self, engines=("sp",)):
        for e in engines:
            self._waits(e, (), (), extra=self.store_evs)
        self.store_evs = []

    def end_phase(self, keep=()):
        evs = list(self.store_evs) + [(self.esem[e], self.ecnt[e]) for e in self.esem if self.ecnt[e] > 0]
        for r in self.__dict__.get("phase_res", []):
            if r.sem is not None and r.cnt > 0 and not any(r is x for x in keep):
                evs.append((r.sem, r.cnt))
        for e in self.eng:
            self._waits(e, (), (), extra=evs)
        self.store_evs = []
        pool = self.__dict__.setdefault("sempool", [])
        rest = []
        for r in self.__dict__.get("phase_res", []):
            if any(r is x for x in keep):
                rest.append(r)
                continue
            pool.append((r.sem, r.cnt))
            r.sem = None
        self.phase_res = rest

    def wait_events(self, en, evs):
        self._waits(en, (), (), extra=evs)


class Prog:
    def __init__(self, TP, debug=(), stop_after=None):
        self.TP = TP
        self.TE = TP + 2 * HALO
        self.NT = TP // 512
        self.NCH = TP // 128
        self.debug = debug
        self.stop_after = stop_after
        self.nc = bass.Bass("TRN2", target_bir_lowering=False)
        self.dbg_outs = {}

    def din(self, name, shape, dt=F32):
        return self.nc.dram_tensor(name, list(shape), dt, kind="ExternalInput").ap()

    def dout(self, name, shape, dt=F32):
        return self.nc.dram_tensor(name, list(shape), dt, kind="ExternalOutput").ap()

    def dscr(self, name, shape, dt=F32):
        if name in self.debug:
            ap = self.nc.dram_tensor(name, list(shape), dt, kind="ExternalOutput").ap()
            self.dbg_outs[name] = ap
            return ap
        return self.nc.dram_tensor(name, list(shape), dt, kind="Internal").ap()

    def build(self):
        nc = self.nc
        TP, TE = self.TP, self.TE
        self.xin = self.din("xin", [2, D, TE])
        self.w_in0 = self.din("ev_w_in", [1, D, 2560])
        self.w_out0 = self.din("ev_w_out", [1, D, D])
        self.w_pool = self.din("ev_pool_w", [1, 4, 256, 256])
        self.w_gate = self.din("ffn_w_gate", [2, D, DFF])
        self.w_up = self.din("ffn_w_up", [2, D, DFF])
        self.w_down = self.din("ffn_w_down", [2, DFF, D])
        self.w_in1 = self.din("od_w_in", [1, D, 4128])
        self.w_out1 = self.din("od_w_out", [1, D, D])
        self.ptab_d = self.din("ptab", [128, PT_N])
        self.rope0 = self.din("rope0", [2, 2, 128, TE])
        self.rope1 = self.din("rope1", [2, 2, 128, TP])
        self.invc_d = self.din("invc", [2, self.NT, 2048])
        self.cmat_d = self.din("cmat", [128, CM_N])
        self.yout = self.dout("yout", [2, D, TP])
        self.wt_in0 = self.dscr("wt_in0", [20, 128, 16, 128], BF16)
        self.wt_out0 = self.dscr("wt_out0", [16, 128, 16, 128], BF16)
        self.wt_pool = self.dscr("wt_pool", [4, 2, 128, 2, 128], BF16)
        self.wt_gate = [self.dscr(f"wt_gate{l}", [FC, 128, 16, 128], BF16) for l in range(2)]
        self.wt_up = [self.dscr(f"wt_up{l}", [FC, 128, 16, 128], BF16) for l in range(2)]
        self.wt_down = [self.dscr(f"wt_down{l}", [16, 128, FC, 128], BF16) for l in range(2)]
        self.wt_in1 = self.dscr("wt_in1", [32, 128, 16, 128], BF16)
        self.wt_dt = self.dscr("wt_dt", [128, 16, 32], BF16)
        self.wt_out1 = self.dscr("wt_out1", [16, 128, 16, 128], BF16)
        self.UT = self.dscr("UT", [2, 1024, TE], F32)
        self.Q0 = self.dscr("Q0", [2, 1024, TP], BF16)
        self.K0 = self.dscr("K0", [2, 256, TE], BF16)
        self.V0 = self.dscr("V0", [2, TE, 256], BF16)
        self.X1a = self.dscr("X1a", [2, D, TP], F32)
        self.X1b = self.dscr("X1b", [2, D, TP], F32)
        self.X2a = self.dscr("X2a", [2, D, TP], F32)
        self.gn_d = self.din("gnrow", [128, 1024])
        self.Q1 = self.dscr("Q1", [2, 1024, TP], BF16)
        self.K1p = self.dscr("K1p", [256, TP], BF16)
        self.K1s = [self.dscr(f"K1s{g}", [128, TP], BF16) for g in range(2)]
        self.K1g = [self.dscr(f"K1g{g}", [NR * 128, TP], BF16) for g in range(2)]
        self.V1p = self.dscr("V1p", [TP, 256], BF16)
        self.V1s = [self.dscr(f"V1s{g}", [TP, 128], BF16) for g in range(2)]
        self.V1g = [self.dscr(f"V1g{g}", [NR * TP, 128], BF16) for g in range(2)]
        self.XBC = self.dscr("XBC", [2, 1536, TP + 4], F32)
        self.XBin = self.dscr("XBin", [1536, 4], F32)
        self.XBg = self.dscr("XBg", [NR * 1536, 4], F32)
        self.Z = self.dscr("Z", [2, TP, 1024], F32)
        self.DT = self.dscr("DT", [2, TP, 32], F32)
        self.Y = self.dscr("Y", [2, TP, 1024], F32)
        self.CUM = self.dscr("CUM", [2, 2, TP, 16], F32)
        self.CT = self.dscr("CT", [256, TP], BF16)
        self.SSin = self.dscr("SSin", [512, 512], F32)
        self.SSg = self.dscr("SSg", [NR * 512, 512], F32)
        self.SAin = self.dscr("SAin", [8, 32], F32)
        self.SAg = self.dscr("SAg", [NR * 8, 32], F32)
        self.CO = self.dscr("CO", [2, 1024, TP], BF16)
        self.DBG_INI = self.dscr("DBG_INI", [128, 4 * 512], F32) if "DBG_INI" in self.debug else None
        self.DBG_ST = self.dscr("DBG_ST", [2, 128, 1024], F32) if "DBG_ST" in self.debug else None

        with ExitStack() as es:
            k = KB(nc, es)
            self.k = k
            self.ps = [es.enter_context(nc.psum_tensor(f"psb{i}", [128, 512], F32)) for i in range(8)]
            self.ps_r = [k.res(f"psb{i}") for i in range(8)]
            self.ptab, self.ptab_r = k.sb(es, "ptab_sb", [128, PT_N], F32)
            self.cmat, self.cmat_r = k.sb(es, "cmat_sb", [128, CM_N], F32)
            self.cmb, self.cmb_r = k.sb(es, "cmat_bf", [128, CMB_N], BF16)
            k.load(self.ptab[:], self.ptab_d[:, :], self.ptab_r)
            k.load(self.cmat[:], self.cmat_d[:, :], self.cmat_r)
            k.op("dve", lambda e: e.tensor_copy(out=self.cmb[:], in_=self.cmat[:, 0:CMB_N]),
                 reads=[self.cmat_r], writes=[self.cmb_r])
            k.op("act", lambda e: e.activation(out=self.ptab[:, PT_SINK:PT_SINK + 8], in_=self.ptab[:, PT_SINK:PT_SINK + 8], func=AF.Exp),
                 reads=[self.ptab_r], writes=[self.ptab_r])

            self.keep = [self.ptab_r, self.cmat_r]
            self.phase_weights()
            k.end_phase(self.keep)
            if self.stop_after == "W":
                return self.finish()
            self.phase_l0a()
            k.end_phase(self.keep)
            if self.stop_after == "L0a":
                return self.finish()
            self.phase_l0b()
            k.end_phase(self.keep)
            if self.stop_after == "L0b":
                return self.finish()
            self.phase_ffn(0, self.X1a, self.X1b if not self.stop_after == "F0" else self.yout)
            k.end_phase(self.keep)
            if self.stop_after == "F0":
                return self.finish()
            self.gnrow, self.gn_r = k.sb(es, "gnrow_sb", [128, 1024], F32)
            k.load(self.gnrow[:], self.gn_d[:, :], self.gn_r)
            self.keep.append(self.gn_r)
            k.op("act", lambda e: e.activation(out=self.ptab[:, PT_ALOG:PT_ALOG + 32], in_=self.ptab[:, PT_ALOG:PT_ALOG + 32], func=AF.Exp),
                 reads=[self.ptab_r], writes=[self.ptab_r])
            k.op("dve", lambda e: e.tensor_scalar(out=self.ptab[:, PT_ALOG:PT_ALOG + 32], in0=self.ptab[:, PT_ALOG:PT_ALOG + 32], scalar1=-1.0, scalar2=None, op0=ALU.mult),
                 reads=[self.ptab_r], writes=[self.ptab_r])
            self.phase_l1a()
            k.end_phase(self.keep)
            self.phase_exchange1()
            k.end_phase(self.keep)
            if self.stop_after == "L1a":
                return self.finish()
            self.phase_att()
            k.end_phase(self.keep)
            if self.stop_after == "ATT":
                return self.finish()
            for s_ in range(2):
                for d_ in range(2):
                    self.ssd_sweep(s_, d_)
                    k.end_phase(self.keep)
            self.phase_exchange2()
            k.end_phase(self.keep)
            if self.stop_after == "SSD":
                return self.finish()
            self.phase_l1c()
            k.end_phase(self.keep)
            if self.stop_after == "L1c":
                return self.finish()
            self.phase_ffn(1, self.X2a, self.yout)
            return self.finish()

    def finish(self):
        k = self.k
        k.barrier_stores(engines=("sp",))
        return self.nc

    def phase_weights(self):
        k = self.k
        nc = self.nc
        self.wev = {}

        def conv(dst, src2d, nkc, ncols_mc, key="x"):
            wres = self.wev.setdefault(key, k.res("wconv_" + key))
            step = 11 if ncols_mc > 22 else ncols_mc
            for kc in range(nkc):
                for m0 in range(0, ncols_mc, step):
                    m1 = min(ncols_mc, m0 + step)
                    k.dma(dst[m0:m1, :, kc, :].rearrange("mc p m -> p mc m"),
                          src2d[kc * 128:(kc + 1) * 128, m0 * 128:m1 * 128].rearrange("p (mc m) -> p mc m", m=128),
                          wres, q="pool")
        conv(self.wt_in0, self.w_in0[0], 16, 20, "in0")
        for g in range(4):
            conv(self.wt_pool[g], self.w_pool[0, g], 2, 2, "out0")
        conv(self.wt_out0, self.w_out0[0], 16, 16, "out0")
        conv(self.wt_gate[0], self.w_gate[0], 16, FC, "ffn0")
        conv(self.wt_up[0], self.w_up[0], 16, FC, "ffn0")
        conv(self.wt_down[0], self.w_down[0], FC, 16, "ffn0")
        conv(self.wt_in1, self.w_in1[0][:, 0:4096], 16, 32, "in1")
        wres = self.wev["in1"]
        for kc in range(16):
            k.dma(self.wt_dt[:, kc, :], self.w_in1[0][kc * 128:(kc + 1) * 128, 4096:4128], wres, q="pool")
        conv(self.wt_out1, self.w_out1[0], 16, 16, "out1")
        conv(self.wt_gate[1], self.w_gate[1], 16, FC, "ffn1")
        conv(self.wt_up[1], self.w_up[1], 16, FC, "ffn1")
        conv(self.wt_down[1], self.w_down[1], FC, 16, "ffn1")
        self.keep += list(self.wev.values())

    def need_w(self, key):
        r = self.wev[key]
        self.k.wait_events("sp", [(r.sem, r.cnt)])

    def rmsnorm_fm(self, xT, xT_r, hT, hT_r, n, gcol, sq, sq_r, rstd, rstd_r, bank, part=None):
        k = self.k
        if part == "b":
            for kc in range(KC):
                k.op("dve", lambda e: e.scalar_tensor_tensor(out=hT[:, kc, 0:n], in0=xT[:, kc, 0:n], scalar=self.ptab[:, gcol + kc:gcol + kc + 1],
                                                             in1=rstd[:, 0:n], op0=ALU.mult, op1=ALU.mult),
                     reads=[xT_r, rstd_r, self.ptab_r], writes=[hT_r])
            return
        ps, ps_r = self.ps[bank], self.ps_r[bank]
        ones = self.cmb[:, CM_ONES:CM_ONES + 128]
        for kc in range(KC):
            j = kc % 2
            k.op("act", lambda e: e.activation(out=sq[j][:, 0:n], in_=xT[:, kc, 0:n], func=AF.Square),
                 reads=[xT_r], writes=[sq_r[j]])
            k.op("pe", lambda e: e.matmul(ps[:, 0:n], lhsT=ones, rhs=sq[j][:, 0:n], start=(kc == 0), stop=(kc == KC - 1)),
                 reads=[sq_r[j], self.cmb_r], writes=[ps_r])
        k.op("act", lambda e: e.activation(out=rstd[:, 0:n], in_=ps[:, 0:n], func=AF.Ln, bias=EPS, scale=1.0 / D),
             reads=[ps_r], writes=[rstd_r])
        k.op("act", lambda e: e.activation(out=rstd[:, 0:n], in_=rstd[:, 0:n], func=AF.Exp, scale=-0.5),
             reads=[rstd_r], writes=[rstd_r])
        if part == "a":
            return
        for kc in range(KC):
            k.op("dve", lambda e: e.scalar_tensor_tensor(out=hT[:, kc, 0:n], in0=xT[:, kc, 0:n], scalar=self.ptab[:, gcol + kc:gcol + kc + 1],
                                                         in1=rstd[:, 0:n], op0=ALU.mult, op1=ALU.mult),
                 reads=[xT_r, rstd_r, self.ptab_r], writes=[hT_r])

    def mm_group(self, bank_ap, bank_r, w_tile, w_r, nkc, rhs_fn, rhs_res, n):
        k = self.k
        for kc in range(nkc):
            k.op("pe", lambda e: e.matmul(bank_ap, lhsT=w_tile[:, kc, :], rhs=rhs_fn(kc), start=(kc == 0), stop=(kc == nkc - 1)),
                 reads=[w_r] + list(rhs_res), writes=[bank_r], inc=(kc == nkc - 1))

    class WStream:
        def __init__(self, prog, es, name, nkc, nbuf):
            self.prog = prog
            self.bufs = [prog.k.sb(es, f"{name}{i}", [128, nkc, 128], BF16) for i in range(nbuf)]
            self.i = 0
            self.queue = []

        def prefetch(self, src_ap):
            t, r = self.bufs[self.i % len(self.bufs)]
            self.i += 1
            self.prog.k.load(t[:].rearrange("p a b -> p (a b)"), src_ap.rearrange("p a b -> p (a b)"), r)
            self.queue.append((t, r))

        def pop(self):
            return self.queue.pop(0)

    def phase_ffn(self, layer, Xin, Xout):
        k = self.k
        self.need_w("ffn%d" % layer)
        NT = self.NT
        gcol = PT_GFFN + 16 * layer
        with ExitStack() as es:
            xT = [k.sb(es, f"f_xT{i}", [128, KC, 512], F32) for i in range(2)]
            hT, hT_r = k.sb(es, "f_hT", [128, KC, 512], BF16)
            aT, aT_r = k.sb(es, "f_aT", [128, FC, 512], BF16)
            sq = [k.sb(es, f"f_sq{i}", [128, 512], BF16) for i in range(2)]
            rstd, rstd_r = k.sb(es, "f_rstd", [128, 512], F32)
            sg = [k.sb(es, f"f_sg{i}", [128, 512], F32) for i in range(2)]
            wg = Prog.WStream(self, es, "f_wg", KC, 2)
            wu = Prog.WStream(self, es, "f_wu", KC, 2)
            wd = Prog.WStream(self, es, "f_wd", FC, 2)
            tiles = [(s, t) for s in range(2) for t in range(NT)]

            def load_x(i):
                s, t = tiles[i]
                xt, xr = xT[i % 2]
                k.load(xt[:], Xin[s, :, t * 512:(t + 1) * 512].rearrange("(kc p) t -> p kc t", p=128), xr)
            load_x(0)
            sched = []
            for i in range(len(tiles)):
                for mc in range(FC):
                    sched.append(("gu", mc))
                for mc in range(16):
                    sched.append(("d", mc))
            pos = [0]

            def prefetch_next():
                if pos[0] >= len(sched):
                    return
                kind, mc = sched[pos[0]]
                pos[0] += 1
                if kind == "gu":
                    wg.prefetch(self.wt_gate[layer][mc])
                    wu.prefetch(self.wt_up[layer][mc])
                else:
                    wd.prefetch(self.wt_down[layer][mc])
            prefetch_next()
            for i, (s, t) in enumerate(tiles):
                xt, xr = xT[i % 2]
                if i + 1 < len(tiles):
                    load_x(i + 1)
                sqa, sqr = [a for a, _ in sq], [b for _, b in sq]
                if i == 0:
                    self.rmsnorm_fm(xt, xr, hT, hT_r, 512, gcol, sqa, sqr, rstd, rstd_r, 0)
                for mc in range(FC):
                    prefetch_next()
                    wgt, wgr = wg.pop()
                    wut, wur = wu.pop()
                    bg, bu = 1 + 2 * (mc % 2), 2 + 2 * (mc % 2)
                    self.mm_group(self.ps[bg][:], self.ps_r[bg], wgt, wgr, KC, lambda kc: hT[:, kc, :], [hT_r], 512)
                    self.mm_group(self.ps[bu][:], self.ps_r[bu], wut, wur, KC, lambda kc: hT[:, kc, :], [hT_r], 512)
                    sgt, sgr = sg[mc % 2]
                    k.op("act", lambda e: e.activation(out=sgt[:], in_=self.ps[bg][:], func=AF.Silu), reads=[self.ps_r[bg]], writes=[sgr])
                    k.op("dve", lambda e: e.tensor_tensor(out=aT[:, mc, :], in0=sgt[:], in1=self.ps[bu][:], op=ALU.mult),
                         reads=[sgr, self.ps_r[bu]], writes=[aT_r])
                for mc in range(16):
                    prefetch_next()
                    wdt, wdr = wd.pop()
                    by = 5 + (mc % 2)
                    self.mm_group(self.ps[by][:], self.ps_r[by], wdt, wdr, FC, lambda kc: aT[:, kc, :], [aT_r], 512)
                    k.op("dve", lambda e: e.tensor_tensor(out=xt[:, mc, :], in0=xt[:, mc, :], in1=self.ps[by][:], op=ALU.add),
                         reads=[xr, self.ps_r[by]], writes=[xr])
                    if i + 1 < len(tiles) and mc in (1, 4):
                        xn, xnr = xT[(i + 1) % 2]
                        self.rmsnorm_fm(xn, xnr, hT, hT_r, 512, gcol, sqa, sqr, rstd, rstd_r, 0, part=("a" if mc == 1 else "b"))
                k.store(Xout[s, :, t * 512:(t + 1) * 512].rearrange("(kc p) t -> p kc t", p=128), xt[:], xr)
            k.barrier_stores()

    def head_norm_rope(self, bsrc, n, gcol, rmat_col, cos_ap, sin_ap, tab_r, out_ap, out_r, tmp, bank_ss, bank_rq, store_fn=None):
        k = self.k
        (qraw, qraw_r), (qsq, qsq_r), (rs, rs_r), (qn, qn_r), (t1, t1_r) = tmp
        ps, ps_r = self.ps, self.ps_r
        ones = self.cmb[:, CM_ONES:CM_ONES + 128]

        def S0():
            k.op("act", lambda e: e.copy(out=qraw[:, 0:n], in_=ps[bsrc][:, 0:n]), reads=[ps_r[bsrc]], writes=[qraw_r])
            k.op("act", lambda e: e.activation(out=qsq[:, 0:n], in_=ps[bsrc][:, 0:n], func=AF.Square), reads=[ps_r[bsrc]], writes=[qsq_r])

        def S1():
            k.op("pe", lambda e: e.matmul(ps[bank_ss][:, 0:n], lhsT=ones, rhs=qsq[:, 0:n], start=True, stop=True),
                 reads=[qsq_r, self.cmb_r], writes=[ps_r[bank_ss]])
            k.op("act", lambda e: e.activation(out=rs[:, 0:n], in_=ps[bank_ss][:, 0:n], func=AF.Ln, bias=EPS, scale=1.0 / 128),
                 reads=[ps_r[bank_ss]], writes=[rs_r])
            k.op("act", lambda e: e.activation(out=rs[:, 0:n], in_=rs[:, 0:n], func=AF.Exp, scale=-0.5), reads=[rs_r], writes=[rs_r])
            k.op("dve", lambda e: e.scalar_tensor_tensor(out=qn[:, 0:n], in0=qraw[:, 0:n], scalar=self.ptab[:, gcol:gcol + 1], in1=rs[:, 0:n],
                                                         op0=ALU.mult, op1=ALU.mult), reads=[qraw_r, rs_r, self.ptab_r], writes=[qn_r])

        def S2():
            k.op("pe", lambda e: e.matmul(ps[bank_rq][:, 0:n], lhsT=self.cmat[:, rmat_col:rmat_col + 128], rhs=qn[:, 0:n], start=True, stop=True),
                 reads=[qn_r, self.cmat_r], writes=[ps_r[bank_rq]])
            k.op("dve", lambda e: e.tensor_tensor(out=t1[:, 0:n], in0=qn[:, 0:n], in1=cos_ap, op=ALU.mult), reads=[qn_r, tab_r], writes=[t1_r])
            k.op("dve", lambda e: e.tensor_tensor(out=qn[:, 0:n], in0=ps[bank_rq][:, 0:n], in1=sin_ap, op=ALU.mult),
                 reads=[ps_r[bank_rq], tab_r, qn_r], writes=[qn_r])
            k.op("dve", lambda e: e.tensor_tensor(out=out_ap, in0=t1[:, 0:n], in1=qn[:, 0:n], op=ALU.add), reads=[t1_r, qn_r], writes=[out_r])
            if store_fn is not None:
                store_fn()
        return [S0, S1, S2]

    @staticmethod
    def run_pending(pending, new=None, flush=False):
        if new is not None:
            new[0]()
            entry = [1, new]
        for p in list(pending):
            p[1][p[0]]()
            p[0] += 1
            if p[0] >= len(p[1]):
                pending.remove(p)
        if new is not None:
            pending.append(entry)
        if flush:
            while pending:
                Prog.run_pending(pending)

    def phase_l0a(self):
        k = self.k
        self.need_w("in0")
        TP, TE, NT = self.TP, self.TE, self.NT
        with ExitStack() as es:
            xT = [k.sb(es, f"a_xT{i}", [128, KC, 512], F32) for i in range(2)]
            tab = [k.sb(es, f"a_tab{i}", [128, 2, 512], F32) for i in range(2)]
            hT, hT_r = k.sb(es, "a_hT", [128, KC, 512], BF16)
            sq = [k.sb(es, f"a_sq{i}", [128, 512], BF16) for i in range(2)]
            rstd, rstd_r = k.sb(es, "a_rstd", [128, 512], F32)
            ust = [k.sb(es, f"a_ust{i}", [128, 512], F32) for i in range(2)]
            qkst = [k.sb(es, f"a_qkst{i}", [128, 512], BF16) for i in range(3)]
            vst = [k.sb(es, f"a_vst{i}", [128, 256], BF16) for i in range(2)]
            tmps = [[k.sb(es, f"a_qraw{i}", [128, 512], F32), k.sb(es, f"a_qsq{i}", [128, 512], BF16), k.sb(es, f"a_rs{i}", [128, 512], F32),
                     k.sb(es, f"a_qn{i}", [128, 512], F32), k.sb(es, f"a_t1{i}", [128, 512], F32)] for i in range(3)]
            pending = []
            wv, wv_r = k.sb(es, "a_wv", [128, 2, KC, 128], BF16)
            ws = Prog.WStream(self, es, "a_ws", KC, 3)
            k.load(wv[:].rearrange("p a b c -> p a (b c)"), self.wt_in0[18:20].rearrange("mc p kc m -> p mc (kc m)"), wv_r)
            tiles = []
            for s in range(2):
                tiles.append((s, 0, 128))
                for t in range(NT):
                    tiles.append((s, HALO + t * 512, 512))
                tiles.append((s, HALO + TP, 128))

            def load_x(i):
                s, t0, n = tiles[i]
                xt, xr = xT[i % 2]
                k.load(xt[:, :, 0:n], self.xin[s, :, t0:t0 + n].rearrange("(kc p) t -> p kc t", p=128), xr)
                tb, tr = tab[i % 2]
                k.load(tb[:, :, 0:n], self.rope0[s, :, :, t0:t0 + n].rearrange("c p t -> p c t"), tr)
            load_x(0)
            nchunk = 18 * len(tiles)
            pos = [0]

            def prefetch_next():
                if pos[0] < nchunk:
                    ws.prefetch(self.wt_in0[pos[0] % 18])
                    pos[0] += 1
            prefetch_next()
            prefetch_next()
            cnt_u = 0
            cnt_qk = 0
            cnt_v = 0
            for i, (s, t0, n) in enumerate(tiles):
                xt, xr = xT[i % 2]
                tb, tr = tab[i % 2]
                if i + 1 < len(tiles):
                    load_x(i + 1)
                self.rmsnorm_fm(xt, xr, hT, hT_r, n, PT_GMIX, [a for a, _ in sq], [b for _, b in sq], rstd, rstd_r, 0)
                interior = (t0 >= HALO and t0 < HALO + TP)
                for mc in range(18):
                    prefetch_next()
                    wt, wr = ws.pop()
                    b = 1 + (mc % 2)
                    self.mm_group(self.ps[b][:, 0:n], self.ps_r[b], wt, wr, KC, lambda kc: hT[:, kc, 0:n], [hT_r], n)
                    if mc < 8:
                        st, sr = ust[cnt_u % 2]
                        cnt_u += 1
                        k.op("act", lambda e: e.copy(out=st[:, 0:n], in_=self.ps[b][:, 0:n]), reads=[self.ps_r[b]], writes=[sr])
                        k.store(self.UT[s, mc * 128:(mc + 1) * 128, t0:t0 + n], st[:, 0:n], sr)
                        Prog.run_pending(pending)
                    else:
                        isq = mc < 16
                        if isq and not interior:
                            Prog.run_pending(pending)
                            continue
                        st, sr = qkst[cnt_qk % 3]
                        if isq:
                            dstap = self.Q0[s, (mc - 8) * 128:(mc - 7) * 128, t0 - HALO:t0 - HALO + n]
                        else:
                            dstap = self.K0[s, (mc - 16) * 128:(mc - 15) * 128, t0:t0 + n]
                        stages = self.head_norm_rope(b, n, PT_QN0 if isq else PT_KN0, CM_R0, tb[:, 0, 0:n], tb[:, 1, 0:n], tr,
                                                     st[:, 0:n], sr, tmps[cnt_qk % 3], 3, 4,
                                                     store_fn=(lambda dstap=dstap, st=st, sr=sr, n=n: k.store(dstap, st[:, 0:n], sr)))
                        cnt_qk += 1
                        Prog.run_pending(pending, new=stages)
                Prog.run_pending(pending, flush=True)
                for j in range(n // 128):
                    b = 5 + (cnt_v % 2)
                    for g in range(2):
                        for kc in range(KC):
                            k.op("pe", lambda e: e.matmul(self.ps[b][:, g * 128:(g + 1) * 128], lhsT=hT[:, kc, j * 128:(j + 1) * 128], rhs=wv[:, g, kc, :],
                                                          start=(kc == 0), stop=(kc == KC - 1)),
                                 reads=[hT_r, wv_r], writes=[self.ps_r[b]], inc=(kc == KC - 1))
                    st, sr = vst[cnt_v % 2]
                    cnt_v += 1
                    k.op("act", lambda e: e.copy(out=st[:], in_=self.ps[b][:, 0:256]), reads=[self.ps_r[b]], writes=[sr])
                    k.store(self.V0[s, t0 + j * 128:t0 + (j + 1) * 128, :], st[:], sr)
            k.barrier_stores()

    def phase_l0b(self):
        k = self.k
        self.need_w("out0")
        TP, TE, NT = self.TP, self.TE, self.NT
        ps, ps_r = self.ps, self.ps_r
        scale = 128.0 ** -0.5
        with ExitStack() as es:
            xT = [k.sb(es, f"b_xT{i}", [128, KC, 512], F32) for i in range(1)]
            uT = [k.sb(es, f"b_uT{i}", [128, 8, 528], F32) for i in range(2)]
            ivc = [k.sb(es, f"b_ivc{i}", [128, 4, 512], F32) for i in range(1)]
            sa, sa_r = k.sb(es, "b_sa", [128, 8, 528], F32)
            sb_, sb_r = k.sb(es, "b_sb", [128, 8, 528], F32)
            dif, dif_r = k.sb(es, "b_dif", [128, 8, 512], BF16)
            catT, cat_r = k.sb(es, "b_cat", [128, 16, 512], BF16)
            wp, wp_r = k.sb(es, "b_wp", [128, 4, 2, 2, 128], BF16)
            kT = [k.sb(es, f"b_kT{i}", [128, 768], BF16) for i in range(2)]
            vv = [k.sb(es, f"b_vv{i}", [128, 6, 128], BF16) for i in range(2)]
            qT = [k.sb(es, f"b_qT{i}", [128, 512], BF16) for i in range(2)]
            ee = [k.sb(es, f"b_ee{i}", [128, 384], F32) for i in range(2)]
            pTs = [k.sb(es, f"b_pT{i}", [128, 6, 384], BF16) for i in range(2)]
            mk, mk_r = k.sb(es, "b_mk", [128, 2, 2, 384], BF16)
            den, den_r = k.sb(es, "b_den", [128, 512], F32)
            ws = Prog.WStream(self, es, "b_ws", KC, 3)
            k.load(wp[:].rearrange("p g a b c -> p (g a) (b c)"), self.wt_pool.rearrange("g mo p ki m -> p (g mo) (ki m)"), wp_r)
            band = self.cmb[:, CM_BAND:CM_BAND + 384]
            for s in range(2):
                for lr in range(2):
                    col = PT_VALID + 2 * s + lr
                    k.op("dve", lambda e: e.tensor_scalar(out=mk[:, s, lr, :], in0=band, scalar1=self.ptab[:, col:col + 1], scalar2=None, op0=ALU.mult),
                         reads=[self.cmb_r, self.ptab_r], writes=[mk_r])
            tiles = [(s, t) for s in range(2) for t in range(NT)]

            def load_x(i):
                s, t = tiles[i]
                t0 = HALO + t * 512
                ut, ur = uT[i % 2]
                k.load(ut[:], self.UT[s, :, t0 - 8:t0 + 520].rearrange("(c p) t -> p c t", p=128), ur)

            def load_x2(i):
                s, t = tiles[i]
                t0 = HALO + t * 512
                xt, xr = xT[0]
                k.load(xt[:], self.xin[s, :, t0:t0 + 512].rearrange("(kc p) t -> p kc t", p=128), xr)
                iv, ivr = ivc[0]
                k.load(iv[:].rearrange("p g t -> p (g t)"), self.invc_d[s, t:t + 1, :].partition_broadcast(128), ivr)
            load_x(0)
            nchunk = 16 * len(tiles)
            pos = [0]

            def prefetch_next():
                if pos[0] < nchunk:
                    ws.prefetch(self.wt_out0[pos[0] % 16])
                    pos[0] += 1
            prefetch_next()
            prefetch_next()
            cq = 0
            ckv = 0
            ce = 0
            for i, (s, t) in enumerate(tiles):
                t0 = HALO + t * 512
                xt, xr = xT[0]
                ut, ur = uT[i % 2]
                iv, ivr = ivc[0]
                load_x2(i)
                if i + 1 < len(tiles):
                    load_x(i + 1)
                TT = ALU.add
                k.op("dve", lambda e: e.tensor_tensor(out=sa[:, :, 1:528], in0=ut[:, :, 0:527], in1=ut[:, :, 1:528], op=TT), reads=[ur], writes=[sa_r])
                k.op("dve", lambda e: e.tensor_tensor(out=sb_[:, 2:8, 2:527], in0=sa[:, 2:8, 1:526], in1=sa[:, 2:8, 3:528], op=TT), reads=[sa_r], writes=[sb_r])
                k.op("dve", lambda e: e.tensor_tensor(out=sa[:, 4:8, 4:525], in0=sb_[:, 4:8, 2:523], in1=sb_[:, 4:8, 6:527], op=TT), reads=[sb_r, sa_r], writes=[sa_r])
                k.op("dve", lambda e: e.tensor_tensor(out=sb_[:, 6:8, 8:520], in0=sa[:, 6:8, 4:516], in1=sa[:, 6:8, 12:524], op=TT), reads=[sa_r, sb_r], writes=[sb_r])
                srcs = [sa, sb_, sa, sb_]
                for g in range(4):
                    src = srcs[g]
                    ivb = iv[:, g:g + 1, :].to_broadcast([128, 2, 512])
                    k.op("dve", lambda e: e.tensor_tensor(out=src[:, 2 * g:2 * g + 2, 8:520], in0=src[:, 2 * g:2 * g + 2, 8:520], in1=ivb, op=ALU.mult),
                         reads=[sa_r, sb_r, ivr], writes=[sa_r, sb_r])
                    k.op("dve", lambda e: e.tensor_tensor(out=dif[:, 2 * g:2 * g + 2, :], in0=src[:, 2 * g:2 * g + 2, 8:520], in1=ut[:, 2 * g:2 * g + 2, 8:520], op=ALU.subtract),
                         reads=[sa_r, sb_r, ur], writes=[dif_r])
                for g in range(4):
                    for mo in range(2):
                        b = 1 + (mo % 2)
                        for ki in range(2):
                            k.op("pe", lambda e: e.matmul(ps[b][:], lhsT=wp[:, g, mo, ki, :], rhs=dif[:, 2 * g + ki, :], start=(ki == 0), stop=(ki == 1)),
                                 reads=[wp_r, dif_r], writes=[ps_r[b]], inc=(ki == 1))
                        c = 2 * g + mo
                        k.op("act", lambda e: e.activation(out=catT[:, c, :], in_=ps[b][:], func=AF.Copy, scale=self.ptab[:, PT_PSCALE + c:PT_PSCALE + c + 1]),
                             reads=[ps_r[b], self.ptab_r], writes=[cat_r])
                heads = [(g, j) for g in range(2) for j in range(4)]
                ctxs = {}

                def stageA(hi):
                    nonlocal cq, ckv, ce
                    g, j = heads[hi]
                    h = 4 * g + j
                    if j == 0:
                        kt, ktr = kT[ckv % 2]
                        vt, vtr = vv[ckv % 2]
                        ckv += 1
                        k.load(kt[:], self.K0[s, g * 128:(g + 1) * 128, t0 - 128:t0 + 640], ktr)
                        k.load(vt[:], self.V0[s, t0 - 128:t0 + 640, g * 128:(g + 1) * 128].rearrange("(b p) d -> p b d", p=128), vtr)
                        ctxs[g] = (kt, ktr, vt, vtr)
                    kt, ktr, vt, vtr = ctxs[g]
                    qt, qtr = qT[cq % 2]
                    pT, pT_r = pTs[cq % 2]
                    cq += 1
                    k.load(qt[:], self.Q0[s, h * 128:(h + 1) * 128, t * 512:(t + 1) * 512], qtr)
                    for kb in range(6):
                        qlo, qhi = max(kb - 2, 0), min(kb, 3)
                        nq = (qhi - qlo + 1) * 128
                        m0 = (qlo - (kb - 2)) * 128
                        b = 3 + (ce % 2)
                        et, etr = ee[ce % 2]
                        ce += 1
                        k.op("pe", lambda e: e.matmul(ps[b][:, 0:nq], lhsT=kt[:, kb * 128:(kb + 1) * 128], rhs=qt[:, qlo * 128:(qhi + 1) * 128], start=True, stop=True),
                             reads=[ktr, qtr], writes=[ps_r[b]])
                        k.op("act", lambda e: e.activation(out=et[:, 0:nq], in_=ps[b][:, 0:nq], func=AF.Exp, scale=scale), reads=[ps_r[b]], writes=[etr])
                        if kb == 0 and t == 0:
                            msk, mr = mk[:, s, 0, m0:m0 + nq], mk_r
                        elif kb == 5 and t == NT - 1:
                            msk, mr = mk[:, s, 1, m0:m0 + nq], mk_r
                        else:
                            msk, mr = band[:, m0:m0 + nq], self.cmb_r
                        k.op("dve", lambda e: e.tensor_tensor(out=pT[:, kb, 0:nq], in0=et[:, 0:nq], in1=msk, op=ALU.mult), reads=[etr, mr], writes=[pT_r])
                    return (h, vt, vtr, pT, pT_r)

                def stageB(hi, ctx):
                    h, vt, vtr, pT, pT_r = ctx
                    bo, bs = (5, 6) if hi % 2 == 0 else (7, 0)
                    for ql in range(4):
                        for ii, kb in enumerate((ql, ql + 1, ql + 2)):
                            qlo = max(kb - 2, 0)
                            c0 = (ql - qlo) * 128
                            k.op("pe", lambda e: e.matmul(ps[bo][:, ql * 128:(ql + 1) * 128], lhsT=vt[:, kb, :], rhs=pT[:, kb, c0:c0 + 128], start=(ii == 0), stop=(ii == 2)),
                                 reads=[vtr, pT_r], writes=[ps_r[bo]], inc=False)
                        for ii, kb in enumerate((ql, ql + 1, ql + 2)):
                            qlo = max(kb - 2, 0)
                            c0 = (ql - qlo) * 128
                            k.op("pe", lambda e: e.matmul(ps[bs][:, ql * 128:(ql + 1) * 128], lhsT=self.cmb[:, CM_ONES:CM_ONES + 128], rhs=pT[:, kb, c0:c0 + 128], start=(ii == 0), stop=(ii == 2)),
                                 reads=[self.cmb_r, pT_r], writes=[ps_r[bs]], inc=(ii == 2 and ql == 3))
                    k.op("dve", lambda e: e.tensor_scalar(out=den[:], in0=ps[bs][:], scalar1=self.ptab[:, PT_SINK + h:PT_SINK + h + 1], scalar2=None, op0=ALU.add),
                         reads=[ps_r[bs], self.ptab_r], writes=[den_r])
                    k.op("dve", lambda e: e.reciprocal(out=den[:], in_=den[:]), reads=[den_r], writes=[den_r])
                    k.op("dve", lambda e: e.tensor_tensor(out=catT[:, 8 + h, :], in0=ps[bo][:], in1=den[:], op=ALU.mult), reads=[ps_r[bo], den_r], writes=[cat_r])

                cur = stageA(0)
                for hi in range(8):
                    nxt = stageA(hi + 1) if hi + 1 < 8 else None
                    stageB(hi, cur)
                    cur = nxt
                for mc in range(16):
                    prefetch_next()
                    wt, wr = ws.pop()
                    b = 1 + (mc % 2)
                    self.mm_group(ps[b][:], ps_r[b], wt, wr, KC, lambda kc: catT[:, kc, :], [cat_r], 512)
                    k.op("dve", lambda e: e.tensor_tensor(out=xt[:, mc, :], in0=xt[:, mc, :], in1=ps[b][:], op=ALU.add), reads=[xr, ps_r[b]], writes=[xr])
                k.store(self.X1a[s, :, t * 512:(t + 1) * 512].rearrange("(kc p) t -> p kc t", p=128), xt[:], xr)
            k.barrier_stores()


    def phase_l1a(self):
        k = self.k
        self.need_w("in1")
        TP, NT = self.TP, self.NT
        ps, ps_r = self.ps, self.ps_r
        with ExitStack() as es:
            xT = [k.sb(es, f"c_xT{i}", [128, KC, 512], F32) for i in range(2)]
            tab = [k.sb(es, f"c_tab{i}", [128, 2, 512], F32) for i in range(2)]
            hT, hT_r = k.sb(es, "c_hT", [128, KC, 512], BF16)
            sq = [k.sb(es, f"c_sq{i}", [128, 512], BF16) for i in range(2)]
            rstd, rstd_r = k.sb(es, "c_rstd", [128, 512], F32)
            xst = [k.sb(es, f"c_xst{i}", [128, 512], F32) for i in range(2)]
            qkst = [k.sb(es, f"c_qkst{i}", [128, 512], BF16) for i in range(3)]
            vst = [k.sb(es, f"c_vst{i}", [128, 256], BF16) for i in range(2)]
            zst = [k.sb(es, f"c_zst{i}", [128, 1024], F32) for i in range(2)]
            dst = [k.sb(es, f"c_dst{i}", [128, 32], F32) for i in range(2)]
            tmps = [[k.sb(es, f"c_qraw{i}", [128, 512], F32), k.sb(es, f"c_qsq{i}", [128, 512], BF16), k.sb(es, f"c_rs{i}", [128, 512], F32),
                     k.sb(es, f"c_qn{i}", [128, 512], F32), k.sb(es, f"c_t1{i}", [128, 512], F32)] for i in range(3)]
            pending = []
            wvz, wvz_r = k.sb(es, "c_wvz", [128, 10, KC, 128], BF16)
            wdt, wdt_r = k.sb(es, "c_wdt", [128, KC, 32], BF16)
            ws = Prog.WStream(self, es, "c_ws", KC, 3)
            k.load(wvz[:].rearrange("p a b c -> p a (b c)"), self.wt_in1[10:20].rearrange("mc p kc m -> p mc (kc m)"), wvz_r)
            k.load(wdt[:], self.wt_dt[:, :, :], wdt_r)
            tiles = [(s, t) for s in range(2) for t in range(NT)]
            chunks = list(range(0, 10)) + list(range(20, 32))

            def load_x(i):
                s, t = tiles[i]
                xt, xr = xT[i % 2]
                k.load(xt[:], self.X1b[s, :, t * 512:(t + 1) * 512].rearrange("(kc p) t -> p kc t", p=128), xr)
                tb, tr = tab[i % 2]
                k.load(tb[:], self.rope1[s, :, :, t * 512:(t + 1) * 512].rearrange("c p t -> p c t"), tr)
            load_x(0)
            nchunk = len(chunks) * len(tiles)
            pos = [0]

            def prefetch_next():
                if pos[0] < nchunk:
                    ws.prefetch(self.wt_in1[chunks[pos[0] % len(chunks)]])
                    pos[0] += 1
            prefetch_next()
            prefetch_next()
            cq = cx = cv = 0
            for i, (s, t) in enumerate(tiles):
                xt, xr = xT[i % 2]
                tb, tr = tab[i % 2]
                if i + 1 < len(tiles):
                    load_x(i + 1)
                self.rmsnorm_fm(xt, xr, hT, hT_r, 512, PT_GMIX + 16, [a for a, _ in sq], [b for _, b in sq], rstd, rstd_r, 0)
                for mc in chunks:
                    prefetch_next()
                    wt, wr = ws.pop()
                    b = 1 + (mc % 2)
                    self.mm_group(ps[b][:], ps_r[b], wt, wr, KC, lambda kc: hT[:, kc, :], [hT_r], 512)
                    if mc < 10:
                        isq = mc < 8
                        st, sr = qkst[cq % 3]
                        if isq:
                            dstap = self.Q1[s, mc * 128:(mc + 1) * 128, t * 512:(t + 1) * 512]
                        elif s == 0:
                            dstap = self.K1p[(mc - 8) * 128:(mc - 7) * 128, t * 512:(t + 1) * 512]
                        else:
                            dstap = self.K1s[mc - 8][:, t * 512:(t + 1) * 512]
                        stages = self.head_norm_rope(b, 512, PT_QN1 if isq else PT_KN1, CM_R1, tb[:, 0, :], tb[:, 1, :], tr, st[:], sr, tmps[cq % 3], 3, 4,
                                                     store_fn=(lambda dstap=dstap, st=st, sr=sr: k.store(dstap, st[:], sr)))
                        cq += 1
                        Prog.run_pending(pending, new=stages)
                    else:
                        m2 = mc - 20
                        st, sr = xst[cx % 2]
                        cx += 1
                        k.op("act", lambda e: e.copy(out=st[:], in_=ps[b][:]), reads=[ps_r[b]], writes=[sr])
                        k.store(self.XBC[s, m2 * 128:(m2 + 1) * 128, 2 + t * 512:2 + (t + 1) * 512], st[:], sr)
                        Prog.run_pending(pending)
                        if s == 1 and t == 0:
                            k.store(self.XBin[m2 * 128:(m2 + 1) * 128, 0:2], st[:, 0:2], sr)
                        if s == 1 and t == NT - 1:
                            k.store(self.XBin[m2 * 128:(m2 + 1) * 128, 2:4], st[:, 510:512], sr)
                Prog.run_pending(pending, flush=True)
                for j in range(4):
                    tok0 = t * 512 + j * 128
                    lhs = lambda kc: hT[:, kc, j * 128:(j + 1) * 128]
                    b = 5
                    for g in range(2):
                        for kc in range(KC):
                            k.op("pe", lambda e: e.matmul(ps[b][:, g * 128:(g + 1) * 128], lhsT=lhs(kc), rhs=wvz[:, g, kc, :], start=(kc == 0), stop=(kc == KC - 1)),
                                 reads=[hT_r, wvz_r], writes=[ps_r[b]], inc=(kc == KC - 1))
                    for kc in range(KC):
                        k.op("pe", lambda e: e.matmul(ps[b][:, 256:288], lhsT=lhs(kc), rhs=wdt[:, kc, :], start=(kc == 0), stop=(kc == KC - 1)),
                             reads=[hT_r, wdt_r], writes=[ps_r[b]], inc=(kc == KC - 1))
                    st, sr = vst[cv % 2]
                    dt_, dr = dst[cv % 2]
                    zt, zr = zst[cv % 2]
                    cv += 1
                    k.op("act", lambda e: e.copy(out=st[:], in_=ps[b][:, 0:256]), reads=[ps_r[b]], writes=[sr])
                    k.op("act", lambda e: e.copy(out=dt_[:], in_=ps[b][:, 256:288]), reads=[ps_r[b]], writes=[dr])
                    if s == 0:
                        k.store(self.V1p[tok0:tok0 + 128, :], st[:], sr)
                    else:
                        for g in range(2):
                            k.store(self.V1s[g][tok0:tok0 + 128, :], st[:, g * 128:(g + 1) * 128], sr)
                    k.store(self.DT[s, tok0:tok0 + 128, :], dt_[:], dr)
                    for half in range(2):
                        bz = 6 + half
                        for m in range(4):
                            mz = 2 + half * 4 + m
                            for kc in range(KC):
                                k.op("pe", lambda e: e.matmul(ps[bz][:, m * 128:(m + 1) * 128], lhsT=lhs(kc), rhs=wvz[:, mz, kc, :], start=(kc == 0), stop=(kc == KC - 1)),
                                     reads=[hT_r, wvz_r], writes=[ps_r[bz]], inc=(kc == KC - 1))
                        k.op("act", lambda e: e.activation(out=zt[:, half * 512:(half + 1) * 512], in_=ps[bz][:], func=AF.Silu), reads=[ps_r[bz]], writes=[zr])
                    k.store(self.Z[s, tok0:tok0 + 128, :], zt[:], zr)
            k.barrier_stores(engines=("sp", "pool"))

    def collective(self, src, dst):
        k = self.k
        sem = k.newsem("cc%d" % k.nsem)
        k.eng["pool"].collective_compute("AllGather", ALU.bypass, replica_groups=[[0, 1, 2, 3], [4, 5, 6, 7]],
                                         ins=[src.tensor.ap().opt()], outs=[dst.tensor.ap().opt()]).then_inc(sem)
        k.eng["pool"].wait_ge(sem, 1)
        k.ninstr += 1
        return (sem, 1)

    def phase_exchange1(self):
        k = self.k
        TP = self.TP
        evs = [self.collective(self.K1s[g], self.K1g[g]) for g in range(2)] + [self.collective(self.V1s[g], self.V1g[g]) for g in range(2)]
        evs.append(self.collective(self.XBin, self.XBg))
        k.wait_events("sp", evs)
        with ExitStack() as es:
            zt, zr = k.sb(es, "x_z", [128, 12, 2], F32)
            xg, xgr = k.sb(es, "x_g", [128, NR, 12, 4], F32)
            hl, hlr = k.sb(es, "x_hl", [128, 12, 2], F32)
            hr, hrr = k.sb(es, "x_hr", [128, 12, 2], F32)
            k.op("dve", lambda e: e.memset(zt[:], 0.0), writes=[zr])
            with self.nc.allow_non_contiguous_dma(reason="tiny conv halo"):
                k.store(self.XBC[0, :, 0:2].rearrange("(mc p) c -> p mc c", p=128), zt[:], zr)
                k.store(self.XBC[0, :, TP + 2:TP + 4].rearrange("(mc p) c -> p mc c", p=128), zt[:], zr)
                k.load(xg[:], self.XBg.rearrange("(r mc p) c -> p r mc c", p=128, mc=12), xgr)
                for (dstt, dr, sel, c0) in ((hl, hlr, PT_SELP, 2), (hr, hrr, PT_SELN, 0)):
                    k.op("dve", lambda e: e.tensor_scalar(out=dstt[:], in0=xg[:, 0, :, c0:c0 + 2], scalar1=self.ptab[:, sel:sel + 1], scalar2=None, op0=ALU.mult),
                         reads=[xgr, self.ptab_r], writes=[dr])
                    for r in range(1, NR):
                        k.op("dve", lambda e: e.scalar_tensor_tensor(out=dstt[:], in0=xg[:, r, :, c0:c0 + 2], scalar=self.ptab[:, sel + r:sel + r + 1], in1=dstt[:],
                                                                     op0=ALU.mult, op1=ALU.add), reads=[xgr, self.ptab_r, dr], writes=[dr])
                k.store(self.XBC[1, :, 0:2].rearrange("(mc p) c -> p mc c", p=128), hl[:], hlr)
                k.store(self.XBC[1, :, TP + 2:TP + 4].rearrange("(mc p) c -> p mc c", p=128), hr[:], hrr)
            k.barrier_stores()

    def phase_att(self):
        k = self.k
        TP, NT, NCH = self.TP, self.NT, self.NCH
        ps, ps_r = self.ps, self.ps_r
        scale = 128.0 ** -0.5
        ones = self.cmb[:, CM_ONES:CM_ONES + 128]
        with ExitStack() as es:
            NKmax = NR * NCH
            kT, kT_r = k.sb(es, "d_kT", [128, NKmax * 128], BF16)
            vv, vv_r = k.sb(es, "d_vv", [128, NKmax, 128], BF16)
            qT = [k.sb(es, f"d_qT{i}", [128, 512], BF16) for i in range(2)]
            pT = [k.sb(es, f"d_pT{i}", [128, 512], BF16) for i in range(3)]
            den, den_r = k.sb(es, "d_den", [128, 512], F32)
            ost = [k.sb(es, f"d_ost{i}", [128, 512], BF16) for i in range(2)]
            sacc = [k.sb(es, f"d_sacc{i}", [128, 512], F32) for i in range(2)]
            cq = ce = 0
            for s in range(2):
                NK = NCH if s == 0 else NR * NCH
                for g in range(2):
                    if s == 0:
                        k.load(kT[:, 0:TP], self.K1p[g * 128:(g + 1) * 128, :], kT_r)
                        k.load(vv[:, 0:NCH, :], self.V1p[:, g * 128:(g + 1) * 128].rearrange("(b p) d -> p b d", p=128), vv_r)
                    else:
                        for r in range(NR):
                            k.load(kT[:, r * TP:(r + 1) * TP], self.K1g[g][r * 128:(r + 1) * 128, :], kT_r)
                        k.load(vv[:, :, :], self.V1g[g].rearrange("(b p) d -> p b d", p=128), vv_r)
                    heads = [(t, j) for t in range(NT) for j in range(4)]
                    steps = [(hi, kb) for hi in range(len(heads)) for kb in range(NK)]
                    qbuf = {}
                    sbank = {}
                    last_even = NK - 1 if (NK - 1) % 2 == 0 else NK - 2

                    def S(n):
                        nonlocal cq, ce
                        hi, kb = steps[n]
                        if kb == 0:
                            t, j = heads[hi]
                            h = 4 * g + j
                            qt, qtr = qT[cq % 2]
                            k.load(qt[:], self.Q1[s, h * 128:(h + 1) * 128, t * 512:(t + 1) * 512], qtr)
                            qbuf[hi] = (qt, qtr, cq)
                            cq += 1
                        qt, qtr, _ = qbuf[hi]
                        b = 2 + (ce % 3)
                        sbank[n] = (b, ce % 3)
                        ce += 1
                        k.op("pe", lambda e: e.matmul(ps[b][:], lhsT=kT[:, kb * 128:(kb + 1) * 128], rhs=qt[:], start=True, stop=True),
                             reads=[kT_r, qtr], writes=[ps_r[b]])
                    S(0)
                    if len(steps) > 1:
                        S(1)
                    for n, (hi, kb) in enumerate(steps):
                        t, j = heads[hi]
                        h = 4 * g + j
                        hq = qbuf[hi][2]
                        bo, bs = (5, 6) if hq % 2 == 0 else (7, 0)
                        b, pi = sbank.pop(n)
                        pt_, ptr = pT[pi]
                        k.op("act", lambda e: e.activation(out=pt_[:], in_=ps[b][:], func=AF.Exp, scale=scale), reads=[ps_r[b]], writes=[ptr])
                        if n + 2 < len(steps):
                            S(n + 2)
                        k.op("pe", lambda e: e.matmul(ps[bo][:], lhsT=vv[:, kb, :], rhs=pt_[:], start=(kb == 0), stop=(kb == NK - 1)),
                             reads=[vv_r, ptr], writes=[ps_r[bo]])
                        if kb % 2 == 0:
                            k.op("pe", lambda e: e.matmul(ps[bs][:], lhsT=ones, rhs=pt_[:], start=(kb == 0), stop=(kb == last_even)),
                                 reads=[self.cmb_r, ptr], writes=[ps_r[bs]], inc=(kb == last_even))
                        else:
                            sa_, sar = sacc[hq % 2]
                            if kb == 1:
                                k.op("dve", lambda e: e.tensor_copy(out=sa_[:], in_=pt_[:]), reads=[ptr], writes=[sar])
                            else:
                                k.op("dve", lambda e: e.tensor_tensor(out=sa_[:], in0=sa_[:], in1=pt_[:], op=ALU.add), reads=[ptr, sar], writes=[sar])
                        if kb == NK - 1:
                            sa_, sar = sacc[hq % 2]
                            ot, otr = ost[hq % 2]
                            k.op("act", lambda e: e.copy(out=den[:], in_=ps[bs][:]), reads=[ps_r[bs]], writes=[den_r])
                            if NK > 1:
                                k.op("pe", lambda e: e.matmul(ps[1][:], lhsT=self.cmat[:, CM_ONES:CM_ONES + 128], rhs=sa_[:], start=True, stop=True),
                                     reads=[self.cmat_r, sar], writes=[ps_r[1]])
                                k.op("dve", lambda e: e.tensor_tensor(out=den[:], in0=den[:], in1=ps[1][:], op=ALU.add), reads=[den_r, ps_r[1]], writes=[den_r])
                            k.op("dve", lambda e: e.reciprocal(out=den[:], in_=den[:]), reads=[den_r], writes=[den_r])
                            k.op("dve", lambda e: e.tensor_tensor(out=ot[:], in0=ps[bo][:], in1=den[:], op=ALU.mult), reads=[ps_r[bo], den_r], writes=[otr])
                            k.store(self.CO[s, h * 128:(h + 1) * 128, t * 512:(t + 1) * 512], ot[:], otr)
                            del qbuf[hi]
            k.barrier_stores()

    def ssd_sweep(self, s, d):
        k = self.k
        TP, NCH = self.TP, self.NCH
        ps, ps_r = self.ps, self.ps_r
        pt = self.ptab
        tri = CM_U if d == 0 else CM_L
        ident = self.cmat[:, CM_I:CM_I + 128]
        with ExitStack() as es:
            def mk(par):
                o = {}
                for nm, shp, dt_ in (("xc", [128, 12, 132], F32), ("dtr", [128, 32], F32), ("yp", [128, 1024], F32), ("acc", [128, 12, 128], F32),
                                     ("xsf", [128, 12, 128], F32), ("bcT", [128, 4, 128], BF16), ("xs", [128, 16, 64], F32), ("btok", [128, 256], BF16),
                                     ("sm", [128, 16, 12], F32), ("xd", [128, 16, 64], BF16), ("xdd", [128, 16, 64], BF16), ("cbm", [128, 2, 128], F32),
                                     ("cst", [128, 16], F32)):
                    if nm == "yp" and d == 0:
                        continue
                    o[nm] = k.sb(es, f"s_{nm}{par}", shp, dt_)
                o["accs"] = [k.res("accm") for _ in range(12)]
                return o
            P = [mk(0), mk(1)]
            sg = [k.sb(es, f"s_sg{i}", [128, 128], F32) for i in range(4)]
            mm = [k.sb(es, f"s_mm{i}", [128, 128], BF16) for i in range(4)]
            yac = [k.sb(es, f"s_yac{i}", [128, 16, 64], F32) for i in range(2)]
            ty, ty_r = k.sb(es, "s_ty", [128, 16, 64], F32)
            state, state_r = k.sb(es, "s_state", [128, 2, 8, 64], F32)
            stb, stb_r = k.sb(es, "s_stb", [128, 2, 512], BF16)
            run, run_r = k.sb(es, "s_run", [128, 16], F32)
            k.op("dve", lambda e: e.memset(state[:], 0.0), writes=[state_r])
            k.op("dve", lambda e: e.memset(stb[:], 0.0), writes=[stb_r])
            k.op("dve", lambda e: e.memset(run[:], 0.0), writes=[run_r])
            order = list(range(NCH)) if d == 0 else list(range(NCH - 1, -1, -1))
            hsl = slice(d * 16, d * 16 + 16)

            def load(i):
                c = order[i]
                B_ = P[i % 2]
                t_, r_ = B_["xc"]
                k.load(t_[:], self.XBC[s, :, c * 128:c * 128 + 132].rearrange("(mc p) t -> p mc t", p=128), r_)
                t2, r2 = B_["dtr"]
                k.load(t2[:], self.DT[s, c * 128:(c + 1) * 128, :], r2)
                if d == 1:
                    t3, r3 = B_["yp"]
                    k.load(t3[:], self.Y[s, c * 128:(c + 1) * 128, :], r3)

            def prep_ops(i):
                c = order[i]
                B_ = P[i % 2]
                xct, xcr = B_["xc"]
                dtt, dtr_r = B_["dtr"]
                acc, acc_r = B_["acc"]
                xsf, xsf_r = B_["xsf"]
                bcT, bcT_r = B_["bcT"]
                xs, xs_r = B_["xs"]
                btok, btok_r = B_["btok"]
                sm, sm_r = B_["sm"]
                xd, xd_r = B_["xd"]
                xdd, xdd_r = B_["xdd"]
                cbm, cbm_r = B_["cbm"]
                cst, cst_r = B_["cst"]
                SM = lambda j: sm[:, :, j]
                T = []
                A = T.append
                accs = B_["accs"]
                wc = lambda m, kk: pt[:, PT_CONVW + m * 5 + kk:PT_CONVW + m * 5 + kk + 1]
                for m in range(12):
                    A(lambda m=m: k.op("dve", lambda e: e.tensor_scalar(out=acc[:, m, :], in0=xct[:, m, 0:128], scalar1=wc(m, 0), scalar2=pt[:, PT_CONVB + m:PT_CONVB + m + 1],
                                                                         op0=ALU.mult, op1=ALU.add), reads=[xcr, self.ptab_r], writes=[accs[m]]))
                for kk in range(1, 5):
                    for m in range(12):
                        A(lambda m=m, kk=kk: k.op("dve", lambda e: e.scalar_tensor_tensor(out=acc[:, m, :], in0=xct[:, m, kk:kk + 128], scalar=wc(m, kk), in1=acc[:, m, :],
                                                                                           op0=ALU.mult, op1=ALU.add), reads=[xcr, self.ptab_r, accs[m]], writes=[accs[m]]))
                A(lambda: k.op("act", lambda e: e.activation(out=xsf[:], in_=acc[:], func=AF.Silu), reads=accs, writes=[xsf_r] + [acc_r]))
                A(lambda: k.op("dve", lambda e: e.tensor_copy(out=bcT[:], in_=xsf[:, 8:12, :]), reads=[xsf_r], writes=[bcT_r]))
                if s == 1 and d == 0:
                    A(lambda: k.store(self.CT[:, c * 128:(c + 1) * 128].rearrange("(g p) t -> p g t", p=128), bcT[:, 2:4, :], bcT_r))
                for m in range(8):
                    A(lambda m=m: k.op("pe", lambda e: e.transpose(ps[m // 4][:, (m % 4) * 128:(m % 4 + 1) * 128], xsf[:, m, :], ident), reads=[xsf_r, self.cmat_r], writes=[ps_r[m // 4]]))
                for b in range(2):
                    A(lambda b=b: k.op("act", lambda e: e.copy(out=xs[:, b * 8:(b + 1) * 8, :].rearrange("p a b -> p (a b)"), in_=ps[b][:]), reads=[ps_r[b]], writes=[xs_r]))
                for m in range(2):
                    A(lambda m=m: k.op("pe", lambda e: e.transpose(ps[2][:, m * 128:(m + 1) * 128], xsf[:, 8 + m, :], ident), reads=[xsf_r, self.cmat_r], writes=[ps_r[2]]))
                A(lambda: k.op("act", lambda e: e.copy(out=btok[:], in_=ps[2][:, 0:256]), reads=[ps_r[2]], writes=[btok_r]))
                A(lambda: k.op("dve", lambda e: e.tensor_tensor(out=SM(0), in0=dtt[:, hsl], in1=pt[:, PT_DTB + d * 16:PT_DTB + d * 16 + 16], op=ALU.add), reads=[dtr_r, self.ptab_r], writes=[sm_r]))
                A(lambda: k.op("dve", lambda e: e.scalar_tensor_tensor(out=SM(1), in0=SM(0), scalar=-1.0, in1=SM(0), op0=ALU.mult, op1=ALU.max), reads=[sm_r], writes=[sm_r]))
                A(lambda: k.op("act", lambda e: e.activation(out=SM(1), in_=SM(1), func=AF.Exp, scale=-1.0), reads=[sm_r], writes=[sm_r]))
                A(lambda: k.op("act", lambda e: e.activation(out=SM(1), in_=SM(1), func=AF.Ln, bias=1.0, scale=1.0), reads=[sm_r], writes=[sm_r]))
                A(lambda: k.op("dve", lambda e: e.scalar_tensor_tensor(out=SM(2), in0=SM(0), scalar=0.0, in1=SM(1), op0=ALU.max, op1=ALU.add), reads=[sm_r], writes=[sm_r]))
                A(lambda: k.op("dve", lambda e: e.tensor_tensor(out=SM(3), in0=SM(2), in1=pt[:, PT_ALOG + d * 16:PT_ALOG + d * 16 + 16], op=ALU.mult), reads=[sm_r, self.ptab_r], writes=[sm_r]))
                A(lambda: k.op("pe", lambda e: e.matmul(ps[2][:, 256:272], lhsT=self.cmat[:, tri:tri + 128], rhs=SM(3), start=True, stop=True), reads=[sm_r, self.cmat_r], writes=[ps_r[2]]))
                A(lambda: k.op("pe", lambda e: e.matmul(ps[2][:, 272:288], lhsT=self.cmat[:, CM_ONES:CM_ONES + 128], rhs=SM(3), start=True, stop=True), reads=[sm_r, self.cmat_r], writes=[ps_r[2]]))
                A(lambda: k.op("dve", lambda e: e.tensor_copy(out=SM(4), in_=ps[2][:, 256:272]), reads=[ps_r[2]], writes=[sm_r]))
                A(lambda: k.op("dve", lambda e: e.tensor_copy(out=SM(5), in_=ps[2][:, 272:288]), reads=[ps_r[2]], writes=[sm_r]))
                A(lambda: k.op("dve", lambda e: e.tensor_tensor(out=SM(6), in0=SM(5), in1=SM(4), op=ALU.subtract), reads=[sm_r], writes=[sm_r]))
                A(lambda: k.op("dve", lambda e: e.tensor_scalar(out=SM(11), in0=SM(4), scalar1=-1.0, scalar2=None, op0=ALU.mult), reads=[sm_r], writes=[sm_r]))
                A(lambda: k.op("act", lambda e: e.activation(out=SM(6), in_=SM(6), func=AF.Exp), reads=[sm_r], writes=[sm_r]))
                A(lambda: k.op("act", lambda e: e.activation(out=SM(7), in_=SM(4), func=AF.Exp), reads=[sm_r], writes=[sm_r]))
                A(lambda: k.op("act", lambda e: e.activation(out=SM(10), in_=SM(5), func=AF.Exp), reads=[sm_r], writes=[sm_r]))
                A(lambda: k.op("dve", lambda e: e.tensor_tensor(out=SM(8), in0=SM(2), in1=SM(6), op=ALU.mult), reads=[sm_r], writes=[sm_r]))
                A(lambda: k.op("dve", lambda e: e.tensor_tensor(out=SM(9), in0=SM(4), in1=run[:], op=ALU.add), reads=[sm_r, run_r], writes=[sm_r]))
                A(lambda: k.op("dve", lambda e: e.tensor_tensor(out=run[:], in0=run[:], in1=SM(5), op=ALU.add), reads=[sm_r, run_r], writes=[run_r]))
                if s == 1:
                    A(lambda: k.op("dve", lambda e: e.tensor_copy(out=cst[:], in_=SM(9)), reads=[sm_r], writes=[cst_r]))
                    A(lambda: k.store(self.CUM[s, d, c * 128:(c + 1) * 128, :], cst[:], cst_r))
                bc3 = lambda j: sm[:, :, j:j + 1].to_broadcast([128, 16, 64])
                A(lambda: k.op("dve", lambda e: e.tensor_tensor(out=xd[:], in0=xs[:], in1=bc3(2), op=ALU.mult), reads=[xs_r, sm_r], writes=[xd_r]))
                A(lambda: k.op("pool", lambda e: e.tensor_tensor(out=xdd[:], in0=xs[:], in1=bc3(8), op=ALU.mult), reads=[xs_r, sm_r], writes=[xdd_r]))
                for g in range(2):
                    A(lambda g=g: k.op("pe", lambda e: e.matmul(ps[3][:, g * 128:(g + 1) * 128], lhsT=bcT[:, g, :], rhs=bcT[:, 2 + g, :], start=True, stop=True), reads=[bcT_r], writes=[ps_r[3]]))
                msk = self.cmat[:, tri:tri + 128].unsqueeze(1).to_broadcast([128, 2, 128])
                A(lambda: k.op("dve", lambda e: e.tensor_tensor(out=cbm[:], in0=ps[3][:, 0:256].rearrange("p (g l) -> p g l", g=2), in1=msk, op=ALU.mult), reads=[ps_r[3], self.cmat_r], writes=[cbm_r]))
                return T

            def main_stage(i, filler):
                c = order[i]
                B_ = P[i % 2]
                bcT, bcT_r = B_["bcT"]
                xs, xs_r = B_["xs"]
                btok, btok_r = B_["btok"]
                sm, sm_r = B_["sm"]
                xd, xd_r = B_["xd"]
                xdd, xdd_r = B_["xdd"]
                cbm, cbm_r = B_["cbm"]
                ya, yar = yac[i % 2]
                nfill = (len(filler) + 15) // 16
                q_r = self.__dict__.setdefault("q_r", [k.res("q%d" % j) for j in range(4)])
                sg4 = sg

                def fill(n):
                    for _ in range(n):
                        if filler:
                            filler.pop(0)()
                for g in range(2):
                    k.op("pe", lambda e: e.matmul(ps[4 + g][:], lhsT=bcT[:, 2 + g, :], rhs=stb[:, g, :], start=True, stop=True), reads=[bcT_r, stb_r], writes=[q_r[2 * g], q_r[2 * g + 1]])
                for g in range(2):
                    k.op("dve", lambda e: e.tensor_tensor(out=ya[:, g * 8:(g + 1) * 8, :], in0=ps[4 + g][:].rearrange("p (a b) -> p a b", b=64),
                                                          in1=sm[:, g * 8:(g + 1) * 8, 7:8].to_broadcast([128, 8, 64]), op=ALU.mult), reads=[q_r[2 * g], q_r[2 * g + 1], sm_r], writes=[yar])
                for h in range(16):
                    g = h // 8
                    j4 = h % 4
                    b = 4 + j4 // 2
                    c4 = (j4 % 2) * 128
                    sgt, sgr = sg[h % 4]
                    mt, mr = mm[h % 4]
                    ntri = CM_NU if d == 0 else CM_NL
                    k.op("pe", lambda e: e.matmul(ps[b][:, c4:c4 + 128], lhsT=sm[:, h, 3:4].to_broadcast([128, 128]), rhs=self.cmat[:, tri:tri + 128], start=True, stop=False),
                         reads=[sm_r, self.cmat_r], writes=[q_r[j4]], inc=False)
                    k.op("pe", lambda e: e.matmul(ps[b][:, c4:c4 + 128], lhsT=ident, rhs=self.cmat[:, ntri:ntri + 128], start=False, stop=True),
                         reads=[self.cmat_r], writes=[q_r[j4]])
                    k.op("act", lambda e: e.activation(out=sgt[:], in_=ps[b][:, c4:c4 + 128], func=AF.Exp, bias=sm[:, h, 11:12], scale=1.0), reads=[q_r[j4], sm_r], writes=[sgr])
                    k.op("dve", lambda e: e.tensor_tensor(out=mt[:], in0=sgt[:], in1=cbm[:, g, :], op=ALU.mult), reads=[sgr, cbm_r], writes=[mr])
                    by = 6 + g
                    k.op("pe", lambda e: e.matmul(ps[by][:, (h % 8) * 64:(h % 8 + 1) * 64], lhsT=mt[:], rhs=xd[:, h, :], start=True, stop=True), reads=[mr, xd_r], writes=[ps_r[by]])
                    fill(nfill)
                for g in range(2):
                    k.op("dve", lambda e: e.tensor_tensor(out=ya[:, g * 8:(g + 1) * 8, :], in0=ya[:, g * 8:(g + 1) * 8, :], in1=ps[6 + g][:].rearrange("p (a b) -> p a b", b=64), op=ALU.add),
                         reads=[yar, ps_r[6 + g]], writes=[yar])
                if d == 0:
                    dsk = pt[:, PT_DSKIP:PT_DSKIP + 16].unsqueeze(2).to_broadcast([128, 16, 64])
                    k.op("pool", lambda e: e.tensor_tensor(out=ty[:], in0=xs[:], in1=dsk, op=ALU.mult), reads=[xs_r, self.ptab_r], writes=[ty_r])
                    k.op("dve", lambda e: e.tensor_tensor(out=ya[:], in0=ya[:], in1=ty[:], op=ALU.add), reads=[yar, ty_r], writes=[yar])
                else:
                    ypt, ypr = B_["yp"]
                    k.op("dve", lambda e: e.tensor_tensor(out=ya[:], in0=ya[:], in1=ypt[:].rearrange("p (a b) -> p a b", b=64), op=ALU.add), reads=[yar, ypr], writes=[yar])
                k.store(self.Y[s, c * 128:(c + 1) * 128, :], ya[:].rearrange("p a b -> p (a b)"), yar)
                for g in range(2):
                    k.op("pe", lambda e: e.matmul(ps[4 + g][:], lhsT=btok[:, g * 128:(g + 1) * 128], rhs=xdd[:, g * 8:(g + 1) * 8, :].rearrange("p a b -> p (a b)"), start=True, stop=True),
                         reads=[btok_r, xdd_r], writes=[q_r[2 * g], q_r[2 * g + 1]])
                for g in range(2):
                    k.op("dve", lambda e: e.tensor_tensor(out=state[:, g], in0=state[:, g], in1=sm[:, g * 8:(g + 1) * 8, 10:11].to_broadcast([128, 8, 64]), op=ALU.mult),
                         reads=[state_r, sm_r], writes=[state_r])
                    k.op("dve", lambda e: e.tensor_tensor(out=state[:, g], in0=state[:, g], in1=ps[4 + g][:].rearrange("p (a b) -> p a b", b=64), op=ALU.add),
                         reads=[state_r, q_r[2 * g], q_r[2 * g + 1]], writes=[state_r])
                k.op("act", lambda e: e.copy(out=stb[:], in_=state[:].rearrange("p g a b -> p g (a b)")), reads=[state_r], writes=[stb_r])
                fill(len(filler))

            load(0)
            if NCH > 1:
                load(1)
            for t_ in prep_ops(0):
                t_()
            for i in range(NCH):
                nxt = prep_ops(i + 1) if i + 1 < NCH else []
                main_stage(i, nxt)
                if i + 2 < NCH:
                    load(i + 2)
            if s == 1:
                for g in range(2):
                    k.store(self.SSin[d * 256 + g * 128:d * 256 + (g + 1) * 128, :], state[:, g].rearrange("p a b -> p (a b)"), state_r)
                k.store(self.SAin[0:1, d * 16:(d + 1) * 16], run[0:1, :], run_r)
                if self.DBG_ST is not None:
                    k.store(self.DBG_ST[d], state[:].rearrange("p g a b -> p (g a b)"), state_r)
            k.barrier_stores(engines=("sp", "pool"))


    def phase_exchange2(self):
        k = self.k
        evs = [self.collective(self.SSin, self.SSg), self.collective(self.SAin, self.SAg)]
        k.wait_events("sp", evs)

    def phase_l1c(self):
        k = self.k
        self.need_w("out1")
        TP, NT = self.TP, self.NT
        ps, ps_r = self.ps, self.ps_r
        pt = self.ptab
        with ExitStack() as es:
            xTs = [k.sb(es, f"e_xT{i}", [128, KC, 512], F32) for i in range(2)]
            cats = [k.sb(es, f"e_cat{i}", [128, 16, 512], BF16) for i in range(2)]
            yt = [k.sb(es, f"e_y{i}", [128, 16, 64], F32) for i in range(2)]
            zt = [k.sb(es, f"e_z{i}", [128, 1024], F32) for i in range(2)]
            ctc = [k.sb(es, f"e_ct{i}", [128, 2, 128], BF16) for i in range(2)]
            cum = [k.sb(es, f"e_cum{i}", [128, 2, 16], F32) for i in range(2)]
            junk, junk_r = k.sb(es, "e_junk", [128, 512], F32)
            ssq, ssq_r = k.sb(es, "e_ssq", [128, 2], F32)
            dn, dn_r = k.sb(es, "e_dn", [128, 1024], F32)
            dn2, dn2_r = k.sb(es, "e_dn2", [128, 1024], F32)
            abc, abc_r = k.sb(es, "e_abc", [128, NR, 32], F32)
            ee, ee_r = k.sb(es, "e_ee", [128, 16], F32)
            sl = [k.sb(es, f"e_sl{i}", [128, 8, 64], F32) for i in range(2)]
            ini, ini_r = k.sb(es, "e_ini", [128, 4, 8, 64], F32)
            inib, inib_r = k.sb(es, "e_inib", [128, 4, 512], BF16)
            ws = Prog.WStream(self, es, "e_ws", KC, 3)
            for r in range(NR):
                k.load(abc[:, r, :], self.SAg[r * 8:r * 8 + 1, :].partition_broadcast(128), abc_r)
            k.op("dve", lambda e: e.memset(ini[:], 0.0), writes=[ini_r])
            cs_ = 0
            for d in range(2):
                MC = PT_MF if d == 0 else PT_MB
                FC_ = PT_FF if d == 0 else PT_FB
                for r1 in range(NR):
                    k.op("dve", lambda e: e.tensor_scalar(out=ee[:], in0=abc[:, 0, d * 16:(d + 1) * 16], scalar1=pt[:, MC + 4 * r1:MC + 4 * r1 + 1], scalar2=None, op0=ALU.mult),
                         reads=[abc_r, self.ptab_r], writes=[ee_r])
                    for r2 in range(1, NR):
                        k.op("dve", lambda e: e.scalar_tensor_tensor(out=ee[:], in0=abc[:, r2, d * 16:(d + 1) * 16], scalar=pt[:, MC + 4 * r1 + r2:MC + 4 * r1 + r2 + 1], in1=ee[:],
                                                                     op0=ALU.mult, op1=ALU.add), reads=[abc_r, self.ptab_r, ee_r], writes=[ee_r])
                    k.op("act", lambda e: e.activation(out=ee[:], in_=ee[:], func=AF.Exp), reads=[ee_r], writes=[ee_r])
                    k.op("dve", lambda e: e.tensor_scalar(out=ee[:], in0=ee[:], scalar1=pt[:, FC_ + r1:FC_ + r1 + 1], scalar2=None, op0=ALU.mult), reads=[ee_r, self.ptab_r], writes=[ee_r])
                    for g in range(2):
                        st, sr = sl[cs_ % 2]
                        cs_ += 1
                        k.load(st[:].rearrange("p a b -> p (a b)"), self.SSg[r1 * 512 + d * 256 + g * 128:r1 * 512 + d * 256 + (g + 1) * 128, :], sr)
                        k.op("dve", lambda e: e.tensor_tensor(out=st[:], in0=st[:], in1=ee[:, g * 8:(g + 1) * 8].unsqueeze(2).to_broadcast([128, 8, 64]), op=ALU.mult),
                             reads=[sr, ee_r], writes=[sr])
                        k.op("dve", lambda e: e.tensor_tensor(out=ini[:, d * 2 + g], in0=ini[:, d * 2 + g], in1=st[:], op=ALU.add), reads=[sr, ini_r], writes=[ini_r])
            k.op("act", lambda e: e.copy(out=inib[:], in_=ini[:].rearrange("p q a b -> p q (a b)")), reads=[ini_r], writes=[inib_r])
            if self.DBG_INI is not None:
                k.store(self.DBG_INI[:, :], ini[:].rearrange("p q a b -> p (q a b)"), ini_r)
            tiles = [(s, t) for s in range(2) for t in range(NT)]
            nchunk = 16 * len(tiles)
            pos = [0]

            def prefetch_next():
                if pos[0] < nchunk:
                    ws.prefetch(self.wt_out1[pos[0] % 16])
                    pos[0] += 1
            prefetch_next()
            prefetch_next()
            ident = self.cmat[:, CM_I:CM_I + 128]
            cnt = [0]
            dns = [(dn, dn_r), (dn2, dn2_r)]
            chunkbuf = {}

            def gate_pre(i, j):
                s, t = tiles[i]
                tok0 = t * 512 + j * 128
                cc = cnt[0]
                cnt[0] += 1
                y_, yr = yt[cc % 2]
                z_, zr = zt[cc % 2]
                ct_, ctr = ctc[cc % 2]
                cu_, cur = cum[cc % 2]
                dnt, dnr = dns[cc % 2]
                chunkbuf[(i, j)] = (dnt, dnr)
                k.load(y_[:].rearrange("p a b -> p (a b)"), self.Y[s, tok0:tok0 + 128, :], yr)
                k.load(z_[:], self.Z[s, tok0:tok0 + 128, :], zr)
                if s == 1:
                    k.load(ct_[:], self.CT[:, tok0:tok0 + 128].rearrange("(g p) t -> p g t", p=128), ctr)
                    for d in range(2):
                        k.load(cu_[:, d, :], self.CUM[1, d, tok0:tok0 + 128, :], cur)
                    k.op("act", lambda e: e.activation(out=cu_[:], in_=cu_[:], func=AF.Exp), reads=[cur], writes=[cur])
                    for d in range(2):
                        for g in range(2):
                            b = 3 + ((2 * d + g) % 2)
                            k.op("pe", lambda e: e.matmul(ps[b][:], lhsT=ct_[:, g, :], rhs=inib[:, 2 * d + g, :], start=True, stop=True), reads=[ctr, inib_r], writes=[ps_r[b]])
                            k.op("dve", lambda e: e.tensor_tensor(out=junk[:].rearrange("p (a b) -> p a b", b=64), in0=ps[b][:].rearrange("p (a b) -> p a b", b=64),
                                                                  in1=cu_[:, d, g * 8:(g + 1) * 8].unsqueeze(2).to_broadcast([128, 8, 64]), op=ALU.mult),
                                 reads=[ps_r[b], cur], writes=[junk_r])
                            k.op("dve", lambda e: e.tensor_tensor(out=y_[:, g * 8:(g + 1) * 8, :], in0=y_[:, g * 8:(g + 1) * 8, :], in1=junk[:].rearrange("p (a b) -> p a b", b=64), op=ALU.add),
                                 reads=[yr, junk_r], writes=[yr])
                yf = y_[:].rearrange("p a b -> p (a b)")
                k.op("dve", lambda e: e.tensor_tensor(out=yf, in0=yf, in1=z_[:], op=ALU.mult), reads=[yr, zr], writes=[yr])
                for g in range(2):
                    k.op("act", lambda e: e.activation(out=junk[:], in_=yf[:, g * 512:(g + 1) * 512], func=AF.Square, accum_out=ssq[:, g:g + 1]), reads=[yr], writes=[junk_r, ssq_r])
                k.op("act", lambda e: e.activation(out=ssq[:], in_=ssq[:], func=AF.Ln, bias=EPS, scale=1.0 / 512), reads=[ssq_r], writes=[ssq_r])
                k.op("act", lambda e: e.activation(out=ssq[:], in_=ssq[:], func=AF.Exp, scale=-0.5), reads=[ssq_r], writes=[ssq_r])
                for g in range(2):
                    k.op("dve", lambda e: e.scalar_tensor_tensor(out=dnt[:, g * 512:(g + 1) * 512], in0=yf[:, g * 512:(g + 1) * 512], scalar=ssq[:, g:g + 1],
                                                                 in1=self.gnrow[:, g * 512:(g + 1) * 512], op0=ALU.mult, op1=ALU.mult), reads=[yr, ssq_r, self.gn_r], writes=[dnr])

            def gate_pe(i, j):
                dnt, dnr = chunkbuf.pop((i, j))
                ct_, ctr_ = cats[i % 2]
                for m in range(8):
                    b = 5 + (m // 4)
                    k.op("pe", lambda e: e.transpose(ps[b][:, (m % 4) * 128:(m % 4 + 1) * 128], dnt[:, m * 128:(m + 1) * 128], ident), reads=[dnr, self.cmat_r], writes=[ps_r[b]])
                for hb in range(2):
                    k.op("act", lambda e: e.copy(out=ct_[:, 8 + hb * 4:8 + hb * 4 + 4, j * 128:(j + 1) * 128], in_=ps[5 + hb][:].rearrange("p (m t) -> p m t", t=128)),
                         reads=[ps_r[5 + hb]], writes=[ctr_])

            def load_co(i):
                s, t = tiles[i]
                ct_, ctr_ = cats[i % 2]
                k.load(ct_[:, 0:8, :], self.CO[s, :, t * 512:(t + 1) * 512].rearrange("(h p) t -> p h t", p=128), ctr_)

            load_co(0)
            for j in range(4):
                gate_pre(0, j)
                gate_pe(0, j)
            for i, (s, t) in enumerate(tiles):
                xt_, xtr = xTs[i % 2]
                ct_, ctr_ = cats[i % 2]
                k.load(xt_[:], self.X1b[s, :, t * 512:(t + 1) * 512].rearrange("(kc p) t -> p kc t", p=128), xtr)
                more = i + 1 < len(tiles)
                if more:
                    load_co(i + 1)
                for j in range(4):
                    if more:
                        gate_pre(i + 1, j)
                    for mc in range(4 * j, 4 * j + 4):
                        prefetch_next()
                        wt, wr = ws.pop()
                        b = 1 + (mc % 2)
                        self.mm_group(ps[b][:], ps_r[b], wt, wr, KC, lambda kc: ct_[:, kc, :], [ctr_], 512)
                        k.op("dve", lambda e: e.tensor_tensor(out=xt_[:, mc, :], in0=xt_[:, mc, :], in1=ps[b][:], op=ALU.add), reads=[xtr, ps_r[b]], writes=[xtr])
                    if more:
                        gate_pe(i + 1, j)
                k.store(self.X2a[s, :, t * 512:(t + 1) * 512].rearrange("(kc p) t -> p kc t", p=128), xt_[:], xtr)
            k.barrier_stores()


PT_GMIX = 0
PT_GFFN = 32
PT_QN0 = 64
PT_KN0 = 65
PT_QN1 = 66
PT_KN1 = 67
PT_PSCALE = 68
PT_SINK = 76
PT_VALID = 84
PT_SELP = 88
PT_SELN = 92
PT_CONVW = 96
PT_CONVB = 156
PT_DTB = 168
PT_ALOG = 200
PT_DSKIP = 232
PT_MF = 248
PT_MB = 264
PT_FF = 280
PT_FB = 284
PT_N = 288

CM_ONES = 0
CM_BAND = 128
CMB_N = 512
CM_R0 = 512
CM_R1 = 640
CM_U = 768
CM_L = 896
CM_I = 1024
CM_NU = 1152
CM_NL = 1280
CM_N = 1408


def _rot_lhsT(blocks):
    R = np.zeros((128, 128), np.float32)
    for a, h in blocks:
        for i in range(a, a + h):
            R[i, i + h] = -1.0
        for i in range(a + h, a + 2 * h):
            R[i, i - h] = 1.0
    return np.ascontiguousarray(R.T)


def make_cmat():
    cm = np.zeros((128, CM_N), np.float32)
    cm[:, CM_ONES:CM_ONES + 128] = 1.0
    b = np.arange(128)[:, None]
    a = np.arange(128)[None, :]
    cm[:, CM_BAND:CM_BAND + 128] = (b <= a)
    cm[:, CM_BAND + 128:CM_BAND + 256] = 1.0
    cm[:, CM_BAND + 256:CM_BAND + 384] = (a <= b)
    cm[:, CM_R0:CM_R0 + 128] = _rot_lhsT([(0, 16)])
    cm[:, CM_R1:CM_R1 + 128] = _rot_lhsT([(0, 32), (64, 32)])
    cm[:, CM_U:CM_U + 128] = (b <= a)
    cm[:, CM_L:CM_L + 128] = (b >= a)
    cm[:, CM_I:CM_I + 128] = np.eye(128, dtype=np.float32)
    cm[:, CM_NU:CM_NU + 128] = -30000.0 * (1.0 - (b <= a))
    cm[:, CM_NL:CM_NL + 128] = -30000.0 * (1.0 - (b >= a))
    return cm


def rope_tables0(pos):
    half = 16
    freqs = (500000.0 ** (-np.arange(half, dtype=np.float32) / half)).astype(np.float32)
    ang = pos.astype(np.float32)[None, :] * freqs[:, None]
    c = np.ones((128, len(pos)), np.float32)
    s = np.zeros((128, len(pos)), np.float32)
    c[0:16] = np.cos(ang)
    c[16:32] = np.cos(ang)
    s[0:16] = np.sin(ang)
    s[16:32] = np.sin(ang)
    return np.stack([c, s])


def rope_tables1(pos):
    half = 32
    freqs = (10000.0 ** (-np.arange(half, dtype=np.float32) / half)).astype(np.float32)
    row = (pos // 64).astype(np.float32)
    col = (pos % 64).astype(np.float32)
    ar = row[None, :] * freqs[:, None]
    ac = col[None, :] * freqs[:, None]
    c = np.concatenate([np.cos(ar), np.cos(ar), np.cos(ac), np.cos(ac)], 0).astype(np.float32)
    s = np.concatenate([np.sin(ar), np.sin(ar), np.sin(ac), np.sin(ac)], 0).astype(np.float32)
    return np.stack([c, s])


def inv_counts(pos, S):
    out = np.zeros((4, len(pos)), np.float32)
    for gi, win in enumerate((2, 4, 8, 16)):
        lo = np.clip(pos - win // 2, 0, S)
        hi = np.clip(pos + win // 2, 0, S)
        out[gi] = 1.0 / (hi - lo).astype(np.float32)
    return out


def host_prepare(inputs, TP, n_cores=8):
    TE = TP + 2 * HALO
    xp = inputs["x_prompt"]
    xs = inputs["x_sample"]
    SP = xp.shape[1]
    SS = xs.shape[1]
    assert SP == TP and SS == NR * TP and xp.shape[0] == n_cores and xs.shape[0] * NR == n_cores
    cmat = make_cmat()
    in_maps = []
    f = np.float32
    for c in range(n_cores):
        r = c % NR
        sq = c // NR
        xin = np.zeros((2, D, TE), f)
        xin[0, :, HALO:HALO + TP] = xp[c].T
        lo = r * TP - HALO
        hi = (r + 1) * TP + HALO
        clo, chi = max(lo, 0), min(hi, SS)
        xin[1, :, clo - lo:chi - lo] = xs[sq, clo:chi].T
        pt = np.zeros((128, PT_N), f)
        pt[:, PT_GMIX:PT_GMIX + 16] = inputs["norm_mix"][0].reshape(16, 128).T
        pt[:, PT_GMIX + 16:PT_GMIX + 32] = inputs["norm_mix"][1].reshape(16, 128).T
        pt[:, PT_GFFN:PT_GFFN + 16] = inputs["norm_ffn"][0].reshape(16, 128).T
        pt[:, PT_GFFN + 16:PT_GFFN + 32] = inputs["norm_ffn"][1].reshape(16, 128).T
        pt[:, PT_QN0] = inputs["ev_q_norm"][0]
        pt[:, PT_KN0] = inputs["ev_k_norm"][0]
        pt[:, PT_QN1] = inputs["od_q_norm"][0]
        pt[:, PT_KN1] = inputs["od_k_norm"][0]
        pt[:, PT_PSCALE:PT_PSCALE + 8] = inputs["ev_pool_scale"][0].reshape(8, 128).T
        pt[:, PT_SINK:PT_SINK + 8] = inputs["ev_sink"][0][None, :]
        pt[:, PT_VALID + 0] = 0.0
        pt[:, PT_VALID + 1] = 0.0
        pt[:, PT_VALID + 2] = 1.0 if r > 0 else 0.0
        pt[:, PT_VALID + 3] = 1.0 if r < NR - 1 else 0.0
        if r > 0:
            pt[:, PT_SELP + r - 1] = 1.0
        if r < NR - 1:
            pt[:, PT_SELN + r + 1] = 1.0
        pt[:, PT_CONVW:PT_CONVW + 60] = inputs["od_conv_w"][0].reshape(5, 12, 128).transpose(2, 1, 0).reshape(128, 60)
        pt[:, PT_CONVB:PT_CONVB + 12] = inputs["od_conv_b"][0].reshape(12, 128).T
        pt[:, PT_DTB:PT_DTB + 32] = inputs["od_dt_bias"][0].reshape(1, 32)
        pt[:, PT_ALOG:PT_ALOG + 32] = inputs["od_a_log"][0].reshape(1, 32)
        pt[:, PT_DSKIP:PT_DSKIP + 16] = inputs["od_d_skip"][0].reshape(1, 16)
        for r1 in range(NR):
            for r2 in range(NR):
                pt[:, PT_MF + 4 * r1 + r2] = 1.0 if (r1 < r2 < r) else 0.0
                pt[:, PT_MB + 4 * r1 + r2] = 1.0 if (r < r2 < r1) else 0.0
            pt[:, PT_FF + r1] = 1.0 if r1 < r else 0.0
            pt[:, PT_FB + r1] = 1.0 if r1 > r else 0.0
        pos0 = np.arange(-HALO, TP + HALO)
        rope0 = np.stack([rope_tables0(pos0), rope_tables0(pos0 + r * TP)]).astype(f)
        pos1 = np.arange(TP)
        rope1 = np.stack([rope_tables1(pos1), rope_tables1(pos1 + r * TP)]).astype(f)
        invc = np.stack([inv_counts(pos1, SP), inv_counts(pos1 + r * TP, SS)]).astype(f)
        invc = np.ascontiguousarray(invc.reshape(2, 4, TP // 512, 512).transpose(0, 2, 1, 3).reshape(2, TP // 512, 2048))
        gn = np.ascontiguousarray(np.broadcast_to(inputs["od_gate_norm"][0][None, :], (128, 1024)), dtype=f)
        m = {"xin": xin, "ptab": pt, "rope0": rope0, "rope1": rope1, "invc": invc, "cmat": cmat, "gnrow": gn}
        for nm in ("ev_w_in", "ev_w_out", "ev_pool_w", "ffn_w_gate", "ffn_w_up", "ffn_w_down", "od_w_in", "od_w_out"):
            m[nm] = np.ascontiguousarray(inputs[nm], dtype=f)
        in_maps.append(m)
    return in_maps


def host_gather(results, TP, n_cores=8):
    yp = np.stack([results[c]["yout"][0].T for c in range(n_cores)])
    ys = np.stack([np.concatenate([results[sq * NR + r]["yout"][1].T for r in range(NR)], 0) for sq in range(n_cores // NR)])
    return np.ascontiguousarray(yp), np.ascontiguousarray(ys)


def kernel(**inputs):
    TP = inputs["x_prompt"].shape[1]
    prog = Prog(TP)
    nc = prog.build()
    in_maps = host_prepare(inputs, TP)
    res = run_bass_kernel_spmd(nc, in_maps, core_ids=list(range(8)))
    return host_gather(res.results, TP)
```
